# Optimizing a Trainium2 kernel written in Bass

```python
import math
import jax, jax.numpy as jnp
from jax import lax
import numpy as np

D_MODEL = 1024
BATCH = 8
SEQ = 2048
DEPTH = 1
DEC_BATCH = 128
DEC_SEQ = 8
PAST_LEN = 8192
PAGE_SIZE = 128

N_HEADS = 16
HEAD_DIM = D_MODEL // N_HEADS
N_KV_HEADS = 4
GQA_GROUP = N_HEADS // N_KV_HEADS
ROT_DIM = HEAD_DIM // 4
ROPE_THETA = 500000.0
WINDOW = 128
BLOCK = WINDOW
CONV_CH = D_MODEL
CONV_WIDTH = 31
D_FF = ((8 * D_MODEL // 3 + 127) // 128) * 128
EPS = 1e-6
NEG_INF = -1e30
ATTN_W = N_HEADS * HEAD_DIM
KV_W = N_KV_HEADS * HEAD_DIM
IN_COLS = 2 * CONV_CH + ATTN_W + 2 * KV_W + 2 * D_MODEL

kernel_name = 'hybrid_conformer_conv_swa_sink_decode_step'


def rmsnorm(x, g):
    xf = x.astype(jnp.float32)
    r = lax.rsqrt(jnp.mean(xf * xf, axis=-1, keepdims=True) + EPS)
    return (xf * r * g.astype(jnp.float32)).astype(x.dtype)


def layernorm(x, g, b):
    xf = x.astype(jnp.float32)
    mu = jnp.mean(xf, axis=-1, keepdims=True)
    var = jnp.mean(jnp.square(xf - mu), axis=-1, keepdims=True)
    return ((xf - mu) * lax.rsqrt(var + EPS) * g.astype(jnp.float32) + b.astype(jnp.float32)).astype(x.dtype)


def swiglu(x, w_up, w_down):
    g, u = jnp.split(x @ w_up, 2, axis=-1)
    return (jax.nn.silu(g) * u) @ w_down


def rope(x, pos):
    inv = jnp.exp(-math.log(ROPE_THETA) * jnp.arange(0, ROT_DIM, 2, dtype=jnp.float32) / ROT_DIM)
    ang = pos.astype(jnp.float32)[:, None] * inv[None, :]
    cos = jnp.cos(ang)[:, None, :]
    sin = jnp.sin(ang)[:, None, :]
    xr = x[..., :ROT_DIM].astype(jnp.float32)
    x1, x2 = xr[..., :ROT_DIM // 2], xr[..., ROT_DIM // 2:]
    rot = jnp.concatenate([x1 * cos - x2 * sin, x2 * cos + x1 * sin], axis=-1).astype(x.dtype)
    return jnp.concatenate([rot, x[..., ROT_DIM:]], axis=-1)


def sink_attention(q, k, v, mask, sinks):
    s = jnp.einsum('bnqkgd,bnskd->bnkgqs', q, k).astype(jnp.float32) * (HEAD_DIM ** -0.5)
    s = jnp.where(mask[None, :, None, None], s, NEG_INF)
    sink = jnp.broadcast_to(sinks.astype(jnp.float32).reshape(N_KV_HEADS, GQA_GROUP, 1, 1), s.shape[:-1] + (1,))
    prob = jax.nn.softmax(jnp.concatenate([s, sink], axis=-1), axis=-1)[..., :-1]
    return jnp.einsum('bnkgqs,bnskd->bnqkgd', prob.astype(v.dtype), v)


def window_mask(q_pos, k_pos):
    return (k_pos <= q_pos) & (q_pos - k_pos < WINDOW) & (k_pos >= 0)


def prompt_attend(q, k, v, sinks):
    B, T = q.shape[:2]
    nb = T // BLOCK
    qb = q.reshape(B, nb, BLOCK, N_KV_HEADS, GQA_GROUP, HEAD_DIM)
    kb = k.reshape(B, nb, BLOCK, N_KV_HEADS, HEAD_DIM)
    vb = v.reshape(B, nb, BLOCK, N_KV_HEADS, HEAD_DIM)
    prev = lambda t: jnp.concatenate([jnp.zeros_like(t[:, :1]), t[:, :-1]], axis=1)
    kk = jnp.concatenate([prev(kb), kb], axis=2)
    vv = jnp.concatenate([prev(vb), vb], axis=2)
    blk = jnp.arange(nb)[:, None, None]
    q_pos = blk * BLOCK + jnp.arange(BLOCK)[None, :, None]
    k_pos = (blk - 1) * BLOCK + jnp.arange(2 * BLOCK)[None, None, :]
    o = sink_attention(qb, kk, vv, window_mask(q_pos, k_pos), sinks).reshape(B, T, ATTN_W)
    return o, k[:, T - WINDOW:], v[:, T - WINDOW:]


def sample_attend(q, k, v, sinks, k_buf, v_buf):
    Bd, S = q.shape[:2]
    L = k_buf.shape[1]
    kk = jnp.concatenate([k_buf.astype(k.dtype), k], axis=1)
    vv = jnp.concatenate([v_buf.astype(v.dtype), v], axis=1)
    q_pos = (PAST_LEN + jnp.arange(S))[None, :, None]
    k_pos = (PAST_LEN - L + jnp.arange(L + S))[None, None, :]
    qb = q.reshape(Bd, 1, S, N_KV_HEADS, GQA_GROUP, HEAD_DIM)
    o = sink_attention(qb, kk[:, None], vv[:, None], window_mask(q_pos, k_pos), sinks).reshape(Bd, S, ATTN_W)
    return o, kk[:, L + S - L:], vv[:, L + S - L:]


def causal_depthwise_conv(full, w, b):
    y = lax.conv_general_dilated(full, w[:, None, :].astype(full.dtype), window_strides=(1,), padding='VALID',
                                 dimension_numbers=('NWC', 'WIO', 'NWC'), feature_group_count=CONV_CH)
    return y + b


def decoder_layer(x, pos, conv_prefix, attend, p):
    x = x + 0.5 * rmsnorm(swiglu(rmsnorm(x, p['ffn1_pre_g']), p['ffn1_w_up'], p['ffn1_w_down']), p['ffn1_post_g'])
    h = rmsnorm(x, p['mix_pre_g'])
    z = h @ p['w_in']
    o1 = 2 * CONV_CH
    o2 = o1 + ATTN_W
    o3 = o2 + KV_W
    o4 = o3 + KV_W
    B, T = x.shape[:2]
    u = z[..., :CONV_CH] * jax.nn.sigmoid(z[..., CONV_CH:o1])
    full = jnp.concatenate([conv_prefix.astype(u.dtype), u], axis=1)
    c = causal_depthwise_conv(full, p['conv_dw_w'], p['conv_dw_b'])
    c = jax.nn.silu(layernorm(c, p['conv_ln_g'], p['conv_ln_b']))
    conv_out = c @ p['w_conv_out']
    conv_state = full[:, full.shape[1] - (CONV_WIDTH - 1):]
    q = rope(z[..., o1:o2].reshape(B, T, N_HEADS, HEAD_DIM), pos)
    k = rope(z[..., o2:o3].reshape(B, T, N_KV_HEADS, HEAD_DIM), pos)
    v = z[..., o3:o4].reshape(B, T, N_KV_HEADS, HEAD_DIM)
    o, k_state, v_state = attend(q, k, v, p['attn_sinks'])
    attn_out = o @ p['w_attn_out']
    g_conv, g_attn = jnp.split(jax.nn.sigmoid(z[..., o4:]), 2, axis=-1)
    y = (g_conv * conv_out + g_attn * attn_out) @ p['w_out']
    x = x + rmsnorm(y, p['mix_post_g'])
    x = x + 0.5 * rmsnorm(swiglu(rmsnorm(x, p['ffn2_pre_g']), p['ffn2_w_up'], p['ffn2_w_down']), p['ffn2_post_g'])
    return x, conv_state, k_state, v_state


def setup_inputs(seed: int = 0) -> dict:
    key = jax.random.key(seed)
    ks = iter(jax.random.split(key, 32))
    f32 = jnp.float32
    nrm = lambda shape, scale: jax.random.normal(next(ks), shape, f32) * scale
    gain = lambda n: 1.0 + 0.05 * jax.random.normal(next(ks), (DEPTH, n), f32)
    win_buf = min(WINDOW, PAST_LEN)
    d = {}
    d['x_prompt'] = nrm((BATCH, SEQ, D_MODEL), 1.0)
    d['x_sample'] = nrm((DEC_BATCH, DEC_SEQ, D_MODEL), 1.0)
    d['state_conv'] = nrm((DEPTH, DEC_BATCH, CONV_WIDTH - 1, CONV_CH), 0.5)
    d['cache_k_win'] = nrm((DEPTH, DEC_BATCH, win_buf, N_KV_HEADS, HEAD_DIM), 1.0)
    d['cache_v_win'] = nrm((DEPTH, DEC_BATCH, win_buf, N_KV_HEADS, HEAD_DIM), 1.0)
    d['ffn1_pre_g'] = gain(D_MODEL)
    d['ffn1_w_up'] = nrm((DEPTH, D_MODEL, 2 * D_FF), D_MODEL ** -0.5)
    d['ffn1_w_down'] = nrm((DEPTH, D_FF, D_MODEL), D_FF ** -0.5)
    d['ffn1_post_g'] = gain(D_MODEL)
    d['mix_pre_g'] = gain(D_MODEL)
    d['w_in'] = nrm((DEPTH, D_MODEL, IN_COLS), D_MODEL ** -0.5)
    d['conv_dw_w'] = nrm((DEPTH, CONV_WIDTH, CONV_CH), CONV_WIDTH ** -0.5)
    d['conv_dw_b'] = nrm((DEPTH, CONV_CH), 0.02)
    d['conv_ln_g'] = gain(CONV_CH)
    d['conv_ln_b'] = nrm((DEPTH, CONV_CH), 0.02)
    d['w_conv_out'] = nrm((DEPTH, CONV_CH, D_MODEL), CONV_CH ** -0.5)
    d['attn_sinks'] = nrm((DEPTH, N_HEADS), 1.0)
    d['w_attn_out'] = nrm((DEPTH, ATTN_W, D_MODEL), ATTN_W ** -0.5)
    d['w_out'] = nrm((DEPTH, D_MODEL, D_MODEL), D_MODEL ** -0.5)
    d['mix_post_g'] = gain(D_MODEL)
    d['ffn2_pre_g'] = gain(D_MODEL)
    d['ffn2_w_up'] = nrm((DEPTH, D_MODEL, 2 * D_FF), D_MODEL ** -0.5)
    d['ffn2_w_down'] = nrm((DEPTH, D_FF, D_MODEL), D_FF ** -0.5)
    d['ffn2_post_g'] = gain(D_MODEL)
    return d


def reference(x_prompt, x_sample, state_conv, cache_k_win, cache_v_win,
              ffn1_pre_g, ffn1_w_up, ffn1_w_down, ffn1_post_g,
              mix_pre_g, w_in, conv_dw_w, conv_dw_b, conv_ln_g, conv_ln_b, w_conv_out,
              attn_sinks, w_attn_out, w_out, mix_post_g,
              ffn2_pre_g, ffn2_w_up, ffn2_w_down, ffn2_post_g):
    weights = dict(ffn1_pre_g=ffn1_pre_g, ffn1_w_up=ffn1_w_up, ffn1_w_down=ffn1_w_down, ffn1_post_g=ffn1_post_g,
                   mix_pre_g=mix_pre_g, w_in=w_in, conv_dw_w=conv_dw_w, conv_dw_b=conv_dw_b,
                   conv_ln_g=conv_ln_g, conv_ln_b=conv_ln_b, w_conv_out=w_conv_out,
                   attn_sinks=attn_sinks, w_attn_out=w_attn_out, w_out=w_out, mix_post_g=mix_post_g,
                   ffn2_pre_g=ffn2_pre_g, ffn2_w_up=ffn2_w_up, ffn2_w_down=ffn2_w_down, ffn2_post_g=ffn2_post_g)
    pos_p = jnp.arange(x_prompt.shape[1])
    pos_s = PAST_LEN + jnp.arange(x_sample.shape[1])
    hp, hs = x_prompt, x_sample
    conv_p, kp, vp, conv_s, ks, vs = [], [], [], [], [], []
    for l in range(DEPTH):
        p = {n: w[l] for n, w in weights.items()}
        prefix_p = jnp.zeros((hp.shape[0], CONV_WIDTH - 1, CONV_CH), hp.dtype)
        hp, c1, k1, v1 = decoder_layer(hp, pos_p, prefix_p, prompt_attend, p)
        kb, vb = cache_k_win[l], cache_v_win[l]
        s_attend = lambda q, k, v, sk, kb=kb, vb=vb: sample_attend(q, k, v, sk, kb, vb)
        hs, c2, k2, v2 = decoder_layer(hs, pos_s, state_conv[l], s_attend, p)
        conv_p.append(c1); kp.append(k1); vp.append(v1)
        conv_s.append(c2); ks.append(k2); vs.append(v2)
    return (hp, hs, jnp.stack(conv_p), jnp.stack(kp), jnp.stack(vp),
            jnp.stack(conv_s), jnp.stack(ks), jnp.stack(vs))
```

```python
import math
import numpy as np
from contextlib import ExitStack
import concourse.bass as bass
import concourse.mybir as mybir
from concourse.bass_utils import run_bass_kernel_spmd

F32 = mybir.dt.float32
BF16 = mybir.dt.bfloat16
ALU = mybir.AluOpType
AF = mybir.ActivationFunctionType
AX = mybir.AxisListType

COMPUTE = ('pe', 'act', 'dve', 'pool')


class Op:
    __slots__ = ('eng', 'fn', 'deps', 'dma_key', 'dma_cnt', 'needed', 'ms', 'is_dma')

    def __init__(self, eng, fn, dma_key=None):
        self.eng = eng
        self.fn = fn
        self.deps = []
        self.dma_key = dma_key
        self.is_dma = dma_key is not None
        self.dma_cnt = 0
        self.needed = False
        self.ms = 0


class Prog:
    def __init__(self):
        self.ops = {e: [] for e in ('pe', 'act', 'dve', 'pool', 'sp')}
        self.last_writer = {}
        self.readers = {}
        self.dma_count = {}

    def add(self, eng, fn, reads=(), writes=(), dma_key=None):
        op = Op(eng, fn, dma_key)
        deps = {}
        for b in reads:
            w = self.last_writer.get(b)
            if w is not None:
                deps[id(w)] = (w, True)
        for b in writes:
            w = self.last_writer.get(b)
            if w is not None and id(w) not in deps:
                deps[id(w)] = (w, False)
            for r in self.readers.get(b, ()):
                if id(r) not in deps:
                    deps[id(r)] = (r, False)
        for d, raw in deps.values():
            if (not d.is_dma) and (not op.is_dma) and d.eng == op.eng:
                if op.eng == 'pe':
                    continue
            op.deps.append(d)
            d.needed = True
        if op.is_dma:
            c = self.dma_count.get(dma_key, 0) + 1
            self.dma_count[dma_key] = c
            op.dma_cnt = c
        for b in writes:
            self.last_writer[b] = op
            self.readers[b] = []
        for b in reads:
            if b in writes:
                continue
            self.readers.setdefault(b, []).append(op)
        self.ops[eng].append(op)
        return op

    def alias(self, old_keys, new_keys):
        pend = []
        seen = set()
        for k in old_keys:
            w = self.last_writer.get(k)
            if w is not None and id(w) not in seen:
                seen.add(id(w))
                pend.append(w)
            for r in self.readers.get(k, ()):
                if id(r) not in seen:
                    seen.add(id(r))
                    pend.append(r)
        for k in new_keys:
            extra = []
            w = self.last_writer.get(k)
            if w is not None and id(w) not in seen:
                extra.append(w)
            for r in self.readers.get(k, ()):
                if id(r) not in seen:
                    extra.append(r)
            self.last_writer[k] = None
            self.readers[k] = list(pend) + extra

    def emit(self, nc):
        with ExitStack() as es:
            esem = {e: es.enter_context(nc.semaphore('ms_' + e)) for e in COMPUTE}
            dsem = {}
            for i, k in enumerate(self.dma_count):
                dsem[k] = es.enter_context(nc.semaphore('dq%d' % i))
            for e in COMPUTE:
                m = 0
                for op in self.ops[e]:
                    if (not op.is_dma) and op.needed:
                        m += 1
                        op.ms = m
            block = es.enter_context(nc.Block())

            def run(eng_name, eng):
                seen = {}
                for op in self.ops[eng_name]:
                    need = {}
                    for d in op.deps:
                        if d.is_dma:
                            s, v = dsem[d.dma_key], 16 * d.dma_cnt
                        else:
                            s, v = esem[d.eng], d.ms
                        key = id(s)
                        if key not in need or need[key][1] < v:
                            need[key] = (s, v)
                    for key, (s, v) in need.items():
                        if seen.get(key, 0) >= v:
                            continue
                        seen[key] = v
                        eng.wait_ge(s, v)
                    ins = op.fn(eng)
                    if op.is_dma:
                        ins.then_inc(dsem[op.dma_key], 16)
                    elif op.needed:
                        ins.then_inc(esem[eng_name], 1)
                if eng_name == 'sp':
                    for k, c in self.dma_count.items():
                        eng.wait_ge(dsem[k], 16 * c)

            @block.tensor
            def _(e):
                run('pe', e)

            @block.scalar
            def _(e):
                run('act', e)

            @block.vector
            def _(e):
                run('dve', e)

            @block.gpsimd
            def _(e):
                run('pool', e)

            @block.sync
            def _(e):
                run('sp', e)


D = 1024
DFF = 2816
KD = 8
KF = 22
SEQ = 2048
NS = 16
DS = 8
PAST = 8192
CWID = 31
HALO = 30
EPS = 1e-6
SCALE = 0.125
NEG = -1e30
W = 640
HP = [0, 4, 1, 5, 2, 6, 3, 7, 8, 12, 9, 13, 10, 14, 11, 15]

PC_F1PRE, PC_F1POST, PC_MPRE, PC_MPOST, PC_F2PRE, PC_F2POST, PC_LNG, PC_LNB, PC_CB = [8 * i for i in range(9)]
PC_CW = 72
PC_SINKROW = PC_CW + 8 * CWID
PC_SINKBC = PC_SINKROW + 1
PC_EPS = PC_SINKBC + 16
PC_EPS4 = PC_EPS + 1
PC_SINKW = 340
NPP = 356
CC_ID, CC_PERM, CC_MP, CC_MP0, CC_MS = 0, 128, 256, 768, 1280
NCC = 1280 + 136

FFN_STAGE = 3
USE_WSCR = True
DIAG_ENG = 'pool'
MIX_STAGE = 7
ATT_STAGE = 6
TILES = [
    [('p', 0, 512), ('s', 0, 128)],
    [('p', 512, 512)],
    [('p', 1024, 512)],
    [('p', 1536, 512)],
]


def weight_plan():
    units = []

    def ffn_units(up, dn):
        for g in range(11):
            units.append([(0, 8, 256, up, 0, g * 256), (2048, 8, 256, up, 0, DFF + g * 256)])
        for g in range(8):
            units.append([(0, 11, 128, dn, 0, g * 128), (1408, 11, 128, dn, 1408, g * 128)])
    ffn_units('f1u', 'f1d')
    units.append([(0, 8, 512, 'win', 0, 3072)])
    units.append([(0, 8, 512, 'win', 0, 2048)])
    units.append([(0, 8, 512, 'win', 0, 2560)])
    for g in range(4):
        units.append([(0, 8, 256, 'win', 0, g * 256), (2048, 8, 256, 'win', 0, 1024 + g * 256)])
    for g in range(4):
        units.append([(0, 8, 256, 'wco', 0, g * 256), (2048, 8, 256, 'wao', 0, g * 256)])
        units.append([(0, 8, 256, 'win', 0, 3584 + g * 256), (2048, 8, 256, 'win', 0, 4608 + g * 256)])
    for g in range(2):
        units.append([(0, 8, 512, 'wo', 0, g * 512)])
    ffn_units('f2u', 'f2d')
    return units


def build_program(tiles=TILES, skip=()):
    nc = bass.Bass("TRN2", target_bir_lowering=False)
    P = Prog()

    def din(name, shape):
        return nc.dram_tensor(name, shape, F32, kind="ExternalInput").ap()

    def dout(name, shape):
        return nc.dram_tensor(name, shape, F32, kind="ExternalOutput").ap()

    xp = din("xp", [SEQ, D])
    xs = din("xs", [NS * DS, D])
    stc = din("stc", [NS * HALO, D])
    ck = din("ck", [NS, 128, 256])
    cv = din("cv", [NS, 128, 256])
    UNITS = weight_plan()
    wall = din("wall", [len(UNITS), 128, 4096])
    ppd = din("pp", [128, NPP])
    cstd = din("cst", [128, NCC])
    roped = din("rope", [128, 2, SEQ + NS * DS])

    yp = dout("yp", [SEQ, D])
    ys = dout("ys", [NS * DS, D])
    csp = dout("csp", [HALO, D])
    kwp = dout("kwp", [128, 256])
    vwp = dout("vwp", [128, 256])
    css = dout("css", [NS, HALO, D])
    kws = dout("kws", [NS, 128, 256])
    vws = dout("vws", [NS, 128, 256])

    with ExitStack() as es:
        def sb(name, shape, dt):
            return es.enter_context(nc.sbuf_tensor(name, shape, dt))

        xT = sb("xT", [128, KD, W], F32)
        hT = sb("hT", [128, KD, W], BF16)
        yT = sb("yT", [128, KD, W], F32)
        sq = sb("sq", [128, KD * W], BF16)
        c2 = sb("c2", [128, KD, W], BF16)
        R1 = sb("R1", [128, 7680], F32)
        kT = sb("kT", [128, 2, 128 + W], BF16)
        vpad = sb("vpad", [128, 6, 4, 128], BF16)
        ropeC = sb("ropeC", [128, W], F32)
        ropeS = sb("ropeS", [128, W], F32)
        wsl = [sb("wsl%d" % i, [128, 4096], BF16) for i in range(3)]
        pp = sb("pp_sb", [128, NPP], F32)
        cst = sb("cst_sb", [128, NCC], F32)
        identb = sb("identb", [128, 128], BF16)
        ones = sb("ones", [128, 128], BF16)
        stage = [sb("stage%d" % i, [128, D], F32) for i in range(2)]
        rsb = [sb("rs%d" % i, [128, 512], F32) for i in range(2)]
        tmp = [sb("tmp%d" % i, [128, 512], F32) for i in range(4)]
        uhalo = sb("uhalo", [128, KD, HALO], F32)
        usb = sb("usb", [128, (HALO + DS) * NS], BF16)
        diagt = [sb("diag%d" % i, [128, CWID * 128], BF16) for i in range(2)]
        diag = [d[:].rearrange("p (t m) -> p t m", t=CWID) for d in diagt]
        stst = [sb("stst0", [128, 4, 128], F32)] * 2
        smb = [sb("sm%d" % i, [128, 512], F32) for i in range(4)]
        pbb = [sb("pb%d" % i, [128, 512], BF16) for i in range(4)]
        pTb = [sb("pT%d" % i, [128, 512], BF16) for i in range(4)]
        stw = [sb("stw%d" % i, [128, 64], F32) for i in range(2)]
        pT8 = [sb("pT8%d" % i, [128, 128], BF16) for i in range(2)]
        stt_ = [sb("st%d" % i, [128, 16], F32) for i in range(4)]
        ckin = [diagt[0][:, 2048 + i * 512:2048 + (i + 1) * 512].bitcast(F32) for i in range(2)]
        kcat = [sb("kcat%d" % i, [128, 2, 136], BF16) for i in range(2)]
        vcat = [diagt[0][:, 3072 + i * 256:3072 + (i + 1) * 256] for i in range(2)]
        vnew = [sb("vnew%d" % i, [128, 256], BF16) for i in range(2)]
        ckin += [diagt[1][:, i * 512:(i + 1) * 512].bitcast(F32) for i in range(2)]
        vcat += [diagt[1][:, 1024 + i * 256:1024 + (i + 1) * 256] for i in range(2)]
        kcat = [kcat[0][:], kcat[1][:]] + [diagt[1][:, 1536 + i * 272:1536 + (i + 1) * 272].rearrange("p (k c) -> p k c", k=2) for i in range(2)]
        vnew = [vnew[0][:], vnew[1][:]] + [diagt[1][:, 2080 + i * 256:2080 + (i + 1) * 256] for i in range(2)]
        pT8 = [pT8[0][:], pT8[1][:]] + [diagt[1][:, 2592 + i * 128:2592 + (i + 1) * 128] for i in range(2)]
        SAMP_D1_KEYS = [(nm, i) for nm in ('ckin', 'vcat', 'kcat', 'vnew', 'pT8') for i in (2, 3)]
        ostage = diagt[0][:, 0:2048].rearrange("p (s c) -> p s c", s=NS)
        kf = sb("kf", [128, 2, 256], F32)
        vf = [stage[1][:, i * 256:(i + 1) * 256] for i in range(2)]
        vsb = sb("vsb", [128, 256], BF16)
        psum = [es.enter_context(nc.psum_tensor("ps%d" % i, [128, 512], F32)) for i in range(8)]

        sq3 = sq[:].rearrange("p (k w) -> p k w", k=KD)
        qpad = sq[:, 0:4096].rearrange("p (a s c) -> p a s c", a=2, s=NS)
        R1b = R1[:].bitcast(BF16)
        hid = R1b[:, 0:KF * W].rearrange("p (k w) -> p k w", k=KF)
        qT = R1b[:, 0:KD * W].rearrange("p (k w) -> p k w", k=KD)
        oT = R1b[:, KD * W:2 * KD * W].rearrange("p (k w) -> p k w", k=KD)
        mT = R1b[:, 2 * KD * W:3 * KD * W].rearrange("p (k w) -> p k w", k=KD)
        up = [R1[:, 5120:5662]] * 2
        us = [R1[:, 5664:6272]] * 2
        upb = R1b[:, 12544:13088]
        ident = cst[:, CC_ID:CC_ID + 128]
        permT = cst[:, CC_PERM:CC_PERM + 128]
        maskP = cst[:, CC_MP:CC_MP + 512]
        maskP0 = cst[:, CC_MP0:CC_MP0 + 512]
        maskS = cst[:, CC_MS:CC_MS + 136]
        eps_ap = pp[:, PC_EPS:PC_EPS + 1]
        eps4_ap = pp[:, PC_EPS4:PC_EPS4 + 1]

        R1_KEYS_HID = [('hid', nt) for nt in range(2)]
        R1_KEYS_U = [('up', 0), ('us', 0), 'upb']
        R1_KEYS_QO = [('qT', nt) for nt in range(2)] + [('oT', nt) for nt in range(2)]
        R1_KEYS_M = [('mT', nt) for nt in range(2)]
        R1_KEYS_QOM = R1_KEYS_QO + R1_KEYS_M

        state = {'bank': 0, 'tmp': 0, 'rs': 0}

        def bank():
            b = state['bank']
            state['bank'] = (b + 1) % 8
            return b

        def newtmp():
            t = state['tmp']
            state['tmp'] = (t + 1) % 4
            return t

        def newrs():
            t = state['rs']
            state['rs'] = (t + 1) % 2
            return t

        def PS(b):
            return ('ps', b)

        upt = len(UNITS)
        wunits = list(range(upt)) * len(tiles)
        UEXT = [max(a + kk * cc for (a, kk, cc, nm, r0, c0) in u_) for u_ in UNITS]
        wstate = {'issued': 0, 'used': 0, 'done': -1}
        PF = 2

        wscr = nc.dram_tensor("wscr", [upt, 128, 4096], BF16).ap() if (USE_WSCR and len(tiles) > 1) else None

        def w_issue(i):
            slot = i % 3
            u = i % upt
            ext = UEXT[u]
            if wscr is not None and i >= upt:
                P.add('pool', lambda e, slot=slot, u=u, ext=ext: e.dma_start(out=wsl[slot][:, 0:ext], in_=wscr[u, :, 0:ext]),
                      reads=[('wscr', u)], writes=[('w', slot)], dma_key=('w', slot))
                return
            for off in range(0, ext, 2048):
                nn = min(2048, ext - off)
                P.add('pool', lambda e, slot=slot, u=u, off=off, nn=nn: e.dma_start(out=wsl[slot][:, off:off + nn], in_=wall[u, :, off:off + nn]),
                      writes=[('w', slot)], dma_key=('w', slot))
            if wscr is not None:
                P.add('sp', lambda e, slot=slot, u=u, ext=ext: e.dma_start(out=wscr[u, :, 0:ext], in_=wsl[slot][:, 0:ext]),
                      reads=[('w', slot)], writes=[('wscr', u)], dma_key=('wst', slot))

        def w_next(keep_prev=False):
            i = wstate['used']
            wstate['used'] = i + 1
            if not keep_prev:
                wstate['done'] = i - 1
            while (wstate['issued'] < len(wunits) and wstate['issued'] <= i + PF
                   and (wstate['issued'] - 3 <= wstate['done'] or wstate['issued'] <= i)):
                assert wstate['issued'] - 3 <= wstate['done'], "weight ring too small"
                w_issue(wstate['issued'])
                wstate['issued'] += 1
            slot = i % 3
            return wsl[slot], ('w', slot)

        P.add('sp', lambda e: e.dma_start(out=pp[:], in_=ppd), writes=['pp'], dma_key='pp')
        P.add('sp', lambda e: e.dma_start(out=cst[:], in_=cstd), writes=['cst'], dma_key='cst')
        P.add('dve', lambda e: e.memset(ones[:], 1.0), writes=['ones'])
        P.add('dve', lambda e: e.tensor_copy(identb[:], ident), reads=['cst'], writes=['identb'])
        P.add('dve', lambda e: e.memset(vpad[:], 0.0), writes=[('vpad', s) for s in range(6)])
        P.add('dve', lambda e: e.memset(kT[:, :, 0:128], 0.0), writes=[('kT', 'halo')])
        P.add('dve', lambda e: e.memset(uhalo[:], 0.0), writes=['uhalo'])
        P.add('sp', lambda e: e.dma_start(out=kws[:, 0:120, :], in_=ck[:, 8:128, :]), dma_key='d2d')
        P.add('sp', lambda e: e.dma_start(out=vws[:, 0:120, :], in_=cv[:, 8:128, :]), dma_key='d2d')
        stc3 = stc.rearrange("(s r) c -> s r c", r=HALO)
        P.add('sp', lambda e: e.dma_start(out=css[:, 0:22, :], in_=stc3[:, 8:30, :]), dma_key='d2d')

        def stats_rs(src_keys, nt, c0, n, scale, eps_t):
            b = bank()
            for k in range(KD):
                P.add('pe', lambda e, b=b, k=k: e.matmul(psum[b][:, 0:n], lhsT=ones[:], rhs=sq3[:, k, c0:c0 + n],
                                                         start=(k == 0), stop=(k == KD - 1)),
                      reads=['ones', ('sq', nt, k)], writes=[PS(b)])
            r = newrs()
            P.add('act', lambda e, b=b, r=r: e.activation(rsb[r][:, 0:n], psum[b][:, 0:n], AF.Ln, bias=eps_t, scale=scale),
                  reads=[PS(b), 'pp'], writes=[('rs', r)])
            P.add('act', lambda e, r=r: e.activation(rsb[r][:, 0:n], rsb[r][:, 0:n], AF.Exp, scale=-0.5),
                  reads=[('rs', r)], writes=[('rs', r)])
            return r

        def norm_to_hT(pcol, nt, c0, n):
            for k in range(KD):
                P.add('act', lambda e, k=k: e.activation(sq3[:, k, c0:c0 + n], xT[:, k, c0:c0 + n], AF.Square),
                      reads=[('xT', nt, k)], writes=[('sq', nt, k)])
            r = stats_rs(None, nt, c0, n, 1.0 / D, eps_ap)
            for k in range(KD):
                P.add('dve', lambda e, k=k, r=r: e.scalar_tensor_tensor(hT[:, k, c0:c0 + n], xT[:, k, c0:c0 + n],
                                                                       pp[:, pcol + k:pcol + k + 1], rsb[r][:, 0:n],
                                                                       ALU.mult, ALU.mult),
                      reads=[('xT', nt, k), 'pp', ('rs', r)], writes=[('hT', nt, k)])

        def post_norm_residual(pcol, nt, c0, n, half):
            r = stats_rs(None, nt, c0, n, (4.0 if half else 1.0) / D, eps4_ap if half else eps_ap)
            for c in range(KD):
                t = newtmp()
                P.add('dve', lambda e, c=c, r=r, t=t: e.scalar_tensor_tensor(tmp[t][:, 0:n], yT[:, c, c0:c0 + n],
                                                                            pp[:, pcol + c:pcol + c + 1], rsb[r][:, 0:n],
                                                                            ALU.mult, ALU.mult),
                      reads=[('yT', nt), 'pp', ('rs', r)], writes=[('tmp', t)])
                P.add('dve', lambda e, c=c, t=t: e.tensor_tensor(xT[:, c, c0:c0 + n], xT[:, c, c0:c0 + n], tmp[t][:, 0:n], ALU.add),
                      reads=[('xT', nt, c), ('tmp', t)], writes=[('xT', nt, c)])

        def ffn(ntl, pre, post):
            for nt, (kind, t0, n, c0) in enumerate(ntl):
                norm_to_hT(pre, nt, c0, n)
            if FFN_STAGE < 1:
                return
            P.alias(R1_KEYS_QOM + R1_KEYS_U + R1S_KEYS, R1_KEYS_HID)
            for g in range(11):
                wv, wk = w_next()
                wg = wv[:, 0:2048].rearrange("p (k c) -> p k c", k=8)
                wu = wv[:, 2048:4096].rearrange("p (k c) -> p k c", k=8)
                for ci in range(2):
                    i = g * 2 + ci
                    for nt, (kind, t0, n, c0) in enumerate(ntl):
                        bg = bank()
                        for k in range(KD):
                            P.add('pe', lambda e, b=bg, k=k, wg=wg, ci=ci, c0=c0, n=n: e.matmul(
                                psum[b][:, 0:n], lhsT=wg[:, k, ci * 128:(ci + 1) * 128], rhs=hT[:, k, c0:c0 + n],
                                start=(k == 0), stop=(k == KD - 1)), reads=[wk, ('hT', nt, k)], writes=[PS(bg)])
                        bu = bank()
                        for k in range(KD):
                            P.add('pe', lambda e, b=bu, k=k, wu=wu, ci=ci, c0=c0, n=n: e.matmul(
                                psum[b][:, 0:n], lhsT=wu[:, k, ci * 128:(ci + 1) * 128], rhs=hT[:, k, c0:c0 + n],
                                start=(k == 0), stop=(k == KD - 1)), reads=[wk, ('hT', nt, k)], writes=[PS(bu)])
                        t = newtmp()
                        P.add('act', lambda e, b=bg, t=t, n=n: e.activation(tmp[t][:, 0:n], psum[b][:, 0:n], AF.Silu),
                              reads=[PS(bg)], writes=[('tmp', t)])
                        P.add('dve', lambda e, b=bu, t=t, i=i, c0=c0, n=n: e.tensor_tensor(
                            hid[:, i, c0:c0 + n], tmp[t][:, 0:n], psum[b][:, 0:n], ALU.mult),
                            reads=[PS(bu), ('tmp', t)], writes=[('hid', nt)])
            for c in range(KD):
                wv, wk = w_next()
                wd = wv[:, 0:2816].rearrange("p (k c) -> p k c", k=KF)
                for nt, (kind, t0, n, c0) in enumerate(ntl):
                    b = bank()
                    for i in range(KF):
                        P.add('pe', lambda e, b=b, i=i, wd=wd, c0=c0, n=n: e.matmul(
                            psum[b][:, 0:n], lhsT=wd[:, i, :], rhs=hid[:, i, c0:c0 + n],
                            start=(i == 0), stop=(i == KF - 1)), reads=[wk, ('hid', nt)], writes=[PS(b)])
                    P.add('dve', lambda e, b=b, c=c, c0=c0, n=n: e.tensor_copy(yT[:, c, c0:c0 + n], psum[b][:, 0:n]),
                          reads=[PS(b)], writes=[('yT', nt)])
                    P.add('act', lambda e, c=c, c0=c0, n=n: e.activation(sq3[:, c, c0:c0 + n], yT[:, c, c0:c0 + n], AF.Square),
                          reads=[('yT', nt)], writes=[('sq', nt, c)])
            if FFN_STAGE < 3:
                return
            for nt, (kind, t0, n, c0) in enumerate(ntl):
                post_norm_residual(post, nt, c0, n, True)

        def rope(b, n, c0):
            tq = newtmp()
            P.add('act', lambda e: e.copy(tmp[tq][:, 0:n], psum[b][:, 0:n]), reads=[PS(b)], writes=[('tmp', tq)])
            b2 = bank()
            P.add('pe', lambda e: e.matmul(psum[b2][:, 0:n], lhsT=permT, rhs=tmp[tq][:, 0:n], start=True, stop=True),
                  reads=['cst', ('tmp', tq)], writes=[PS(b2)])
            tB = newtmp()
            P.add('dve', lambda e: e.tensor_tensor(tmp[tB][:, 0:n], psum[b2][:, 0:n], ropeS[:, c0:c0 + n], ALU.mult),
                  reads=[PS(b2), 'rope'], writes=[('tmp', tB)])
            P.add('dve', lambda e: e.tensor_tensor(tmp[tq][:, 0:n], tmp[tq][:, 0:n], ropeC[:, c0:c0 + n], ALU.mult),
                  reads=[('tmp', tq), 'rope'], writes=[('tmp', tq)])
            return tq, tB

        n_tiles = len(tiles)
        r1s = [R1[:, i * 1024:(i + 1) * 1024] for i in range(7)]
        R1S_KEYS = [('r1s', i) for i in range(7)]
        c2f = c2[:].rearrange("p k w -> p (k w)").bitcast(F32)
        C2K = [('c2', nt_, k_) for nt_ in range(2) for k_ in range(KD)]
        LBUF = [(stage[0][:], ('stage', 0), []), (stage[1][:], ('stage', 1), []),
                (c2f[:, 0:1024], ('xstg', 2), C2K), (c2f[:, 1024:2048], ('xstg', 3), C2K),
                (stage[0][:], ('stage', 0), [])]

        def tile_ntl(tl):
            out, c0 = [], 0
            for (kind, t0, n) in tl:
                out.append((kind, t0, n, c0))
                c0 += n
            return out

        def tile_blocks(ntl_):
            blocks = []
            for nt, (kind, t0, n, c0) in enumerate(ntl_):
                for bl in range(n // 128):
                    blocks.append((nt, kind, t0 + bl * 128, c0 + bl * 128))
            return blocks

        def issue_rope(ntl_):
            for nt, (kind, t0, n, c0) in enumerate(ntl_):
                src0 = t0 if kind == 'p' else SEQ
                P.add('sp', lambda e, c0=c0, n=n, src0=src0: e.dma_start(out=ropeC[:, c0:c0 + n], in_=roped[:, 0, src0:src0 + n]),
                      writes=['rope'], dma_key='rope')
                P.add('sp', lambda e, c0=c0, n=n, src0=src0: e.dma_start(out=ropeS[:, c0:c0 + n], in_=roped[:, 1, src0:src0 + n]),
                      writes=['rope'], dma_key='rope')

        def issue_x_load(ntl_, bi):
            nt, kind, r0, col = tile_blocks(ntl_)[bi]
            src = xp if kind == 'p' else xs
            buf, key, extra = LBUF[bi]
            P.add('sp', lambda e: e.dma_start(out=buf, in_=src[r0:r0 + 128, :]), writes=[key] + extra, dma_key=key)

        for ti, tl in enumerate(tiles):
            ntl = []
            c0 = 0
            for (kind, t0, n) in tl:
                ntl.append((kind, t0, n, c0))
                c0 += n
            last_tile = (ti == n_tiles - 1)
            has_s = any(k == 's' for (k, _, _, _) in ntl)
            tok0 = ntl[0][1]
            blocks = tile_blocks(ntl)
            if ti == 0:
                issue_rope(ntl)
                for bi in range(min(2, len(blocks))):
                    issue_x_load(ntl, bi)
            P.alias(R1_KEYS_HID, R1S_KEYS[5:7])
            if ti == 0:
                for bi in range(2, min(4, len(blocks))):
                    issue_x_load(ntl, bi)
            for bi, (nt, kind, r0, col) in enumerate(blocks):
                buf, key, extra = LBUF[bi]
                for hb in range(2):
                    b = bank()
                    for kk in range(4):
                        k = hb * 4 + kk
                        P.add('pe', lambda e, b=b, kk=kk, k=k, buf=buf: e.transpose(
                            psum[b][:, kk * 128:(kk + 1) * 128], buf[:, k * 128:(k + 1) * 128], ident),
                            reads=[key, 'cst'] + extra, writes=[PS(b)])
                    dst = xT[:, hb * 4:hb * 4 + 4, col:col + 128]
                    srcv = psum[b][:, 0:512].rearrange("p (k c) -> p k c", k=4)
                    if hb == 0:
                        P.add('act', lambda e, dst=dst, srcv=srcv: e.copy(dst, srcv), reads=[PS(b)], writes=[('xT', nt, hb * 4 + i_) for i_ in range(4)])
                    else:
                        P.add('dve', lambda e, dst=dst, srcv=srcv: e.tensor_copy(dst, srcv), reads=[PS(b)], writes=[('xT', nt, hb * 4 + i_) for i_ in range(4)])
                if bi + 4 < len(blocks):
                    issue_x_load(ntl, bi + 4)

            if 'ffn1' not in skip:
                ffn(ntl, PC_F1PRE, PC_F1POST)

            if 'mixer' not in skip:
                for nt, (kind, t0, n, c0) in enumerate(ntl):
                    norm_to_hT(PC_MPRE, nt, c0, n)
                P.alias(R1_KEYS_HID, R1_KEYS_U + R1_KEYS_QO)
                if MIX_STAGE >= 3:
                    if has_s:
                        P.alias([('sq', nt, k) for nt in range(2) for k in range(KD)], ['qpad'])
                        P.add('pool', lambda e: e.memset(sq[:, 0:4096], 0.0), writes=['qpad'])
                    wv, wk = w_next()
                    wkv = wv[:, 0:4096].rearrange("p (k c) -> p k c", k=8)
                    for kc in range(2):
                        for nt, (kind, t0, n, c0) in enumerate(ntl):
                            b = bank()
                            for k in range(KD):
                                P.add('pe', lambda e, b=b, k=k, kc=kc, c0=c0, n=n, wkv=wkv: e.matmul(
                                    psum[b][:, 0:n], lhsT=wkv[:, k, kc * 128:(kc + 1) * 128], rhs=hT[:, k, c0:c0 + n],
                                    start=(k == 0), stop=(k == KD - 1)), reads=[wk, ('hT', nt, k)], writes=[PS(b)])
                            tA, tB = rope(b, n, c0)
                            kcol = 128 + c0
                            P.add('dve', lambda e, tA=tA, tB=tB, kc=kc, kcol=kcol, n=n: e.tensor_tensor(
                                kT[:, kc, kcol:kcol + n], tmp[tA][:, 0:n], tmp[tB][:, 0:n], ALU.add),
                                reads=[('tmp', tA), ('tmp', tB)], writes=[('kT', nt)])
                            if (last_tile and kind == 'p') or kind == 's':
                                if kind == 'p':
                                    P.add('pool', lambda e, tA=tA, tB=tB, kc=kc, n=n: e.tensor_tensor(
                                        kf[:, kc, 0:128], tmp[tA][:, n - 128:n], tmp[tB][:, n - 128:n], ALU.add),
                                        reads=[('tmp', tA), ('tmp', tB)], writes=['kf'])
                                else:
                                    P.add('pool', lambda e, tA=tA, tB=tB, kc=kc, n=n: e.tensor_tensor(
                                        kf[:, kc, 128:256], tmp[tA][:, 0:128], tmp[tB][:, 0:128], ALU.add),
                                        reads=[('tmp', tA), ('tmp', tB)], writes=['kf'])
                    if last_tile or has_s:
                        for part in ([0] if last_tile else []) + ([1] if has_s else []):
                            b = bank()
                            for kc in range(2):
                                P.add('pe', lambda e, b=b, kc=kc, part=part: e.transpose(
                                    psum[b][:, kc * 128:(kc + 1) * 128], kf[:, kc, part * 128:(part + 1) * 128], ident),
                                    reads=['kf', 'cst'], writes=[PS(b)])
                            P.add('act', lambda e, b=b, part=part: e.copy(stage[0][:, part * 256:(part + 1) * 256], psum[b][:, 0:256]),
                                  reads=[PS(b)], writes=[('stage', 0)])
                            if part == 0:
                                P.add('sp', lambda e: e.dma_start(out=kwp, in_=stage[0][:, 0:256]), reads=[('stage', 0)], dma_key=('stage', 0))
                            else:
                                for s in range(NS):
                                    P.add('sp', lambda e, s=s: e.dma_start(out=kws[s, 120:128, :], in_=stage[0][s * DS:(s + 1) * DS, 256:512]),
                                          reads=[('stage', 0)], dma_key=('stage', 0))
                    for nt, (kind, t0, n, c0) in enumerate(ntl):
                        for bl in range(n // 128):
                            b = bank()
                            for k in range(KD):
                                P.add('pe', lambda e, b=b, k=k, c0=c0, bl=bl, wkv=wkv: e.matmul(
                                    psum[b][:, 0:256], lhsT=hT[:, k, c0 + bl * 128:c0 + (bl + 1) * 128], rhs=wkv[:, k, 256:512],
                                    start=(k == 0), stop=(k == KD - 1)), reads=[wk, ('hT', nt, k)], writes=[PS(b)])
                            if kind == 'p':
                                slot = 1 + bl
                                for hf in range(2):
                                    dst = vpad[:, slot, :, :].rearrange("p (kc hf) c -> p kc hf c", hf=2)[:, :, hf, hf * 64:(hf + 1) * 64]
                                    srcv = psum[b][:, 0:256].rearrange("p (kc hf c) -> p kc hf c", kc=2, hf=2)[:, :, hf, :]
                                    eng = 'act' if bl % 2 == 0 else 'dve'
                                    if eng == 'act':
                                        P.add('act', lambda e, dst=dst, srcv=srcv: e.copy(dst, srcv), reads=[PS(b)], writes=[('vpad', slot)])
                                    else:
                                        P.add('dve', lambda e, dst=dst, srcv=srcv: e.tensor_copy(dst, srcv), reads=[PS(b)], writes=[('vpad', slot)])
                                if last_tile and bl == n // 128 - 1:
                                    if bl % 2 == 0:
                                        P.add('act', lambda e, b=b: e.copy(vf[0], psum[b][:, 0:256]), reads=[PS(b)], writes=[('stage', 1)])
                                    else:
                                        P.add('dve', lambda e, b=b: e.tensor_copy(vf[0], psum[b][:, 0:256]), reads=[PS(b)], writes=[('stage', 1)])
                                    P.add('sp', lambda e: e.dma_start(out=vwp, in_=vf[0]), reads=[('stage', 1)], dma_key=('stage', 1))
                            else:
                                P.add('act', lambda e, b=b: e.copy(vf[1], psum[b][:, 0:256]), reads=[PS(b)], writes=[('stage', 1)])
                                P.add('dve', lambda e: e.tensor_copy(vsb[:], vf[1]), reads=[('stage', 1)], writes=['vsb'])
                                for s in range(NS):
                                    P.add('sp', lambda e, s=s: e.dma_start(out=vws[s, 120:128, :], in_=stage[1][s * DS:(s + 1) * DS, 256:512]),
                                          reads=[('stage', 1)], dma_key=('stage', 1))
                    for c in range(KD):
                        if c % 4 == 0:
                            wv, wk = w_next()
                            wq = wv[:, 0:4096].rearrange("p (k c) -> p k c", k=8)
                        cq = c % 4
                        kc = c // 4
                        for nt, (kind, t0, n, c0) in enumerate(ntl):
                            b = bank()
                            for k in range(KD):
                                P.add('pe', lambda e, b=b, k=k, wq=wq, cq=cq, c0=c0, n=n: e.matmul(
                                    psum[b][:, 0:n], lhsT=wq[:, k, cq * 128:(cq + 1) * 128], rhs=hT[:, k, c0:c0 + n],
                                    start=(k == 0), stop=(k == KD - 1)), reads=[wk, ('hT', nt, k)], writes=[PS(b)])
                            tA, tB = rope(b, n, c0)
                            if kind == 'p':
                                P.add('dve', lambda e, tA=tA, tB=tB, c=c, c0=c0, n=n: e.tensor_tensor(
                                    qT[:, c, c0:c0 + n], tmp[tA][:, 0:n], tmp[tB][:, 0:n], ALU.add),
                                    reads=[('tmp', tA), ('tmp', tB)], writes=[('qT', nt)])
                            else:
                                for hf in range(2):
                                    pos = 2 * c + hf
                                    dst = qpad[hf * 64:(hf + 1) * 64, kc, :, pos * DS:(pos + 1) * DS]
                                    P.add('dve', lambda e, tA=tA, tB=tB, hf=hf, dst=dst, n=n: e.tensor_tensor(
                                        dst, tmp[tA][hf * 64:(hf + 1) * 64, 0:n].rearrange("p (s c) -> p s c", s=NS),
                                        tmp[tB][hf * 64:(hf + 1) * 64, 0:n].rearrange("p (s c) -> p s c", s=NS), ALU.add),
                                        reads=[('tmp', tA), ('tmp', tB)], writes=['qpad'])


                pkind, pt0, pn, pc0 = ntl[0]
                np_ = pn
                gl = {}

                def glu_part(j):
                    if j % 2 == 0:
                        wv, wk = w_next()
                        gl['wk'] = wk
                        gl['wval'] = wv[:, 0:2048].rearrange("p (k c) -> p k c", k=8)
                        gl['wgate'] = wv[:, 2048:4096].rearrange("p (k c) -> p k c", k=8)
                    wk, wval, wgate = gl['wk'], gl['wval'], gl['wgate']
                    cj = j % 2
                    jb = 0
                    us3 = us[jb].rearrange("p (s c) -> p s c", s=NS)
                    P.add('dve', lambda e: e.tensor_copy(up[jb][:, 0:HALO], uhalo[:, j, :]),
                          reads=['uhalo'], writes=[('up', jb)])
                    for nt, (kind, t0, n, c0) in enumerate(ntl):
                        bv = bank()
                        for k in range(KD):
                            P.add('pe', lambda e, b=bv, k=k, c0=c0, n=n: e.matmul(
                                psum[b][:, 0:n], lhsT=wval[:, k, cj * 128:(cj + 1) * 128], rhs=hT[:, k, c0:c0 + n],
                                start=(k == 0), stop=(k == KD - 1)), reads=[wk, ('hT', nt, k)], writes=[PS(bv)])
                        bg = bank()
                        for k in range(KD):
                            P.add('pe', lambda e, b=bg, k=k, c0=c0, n=n: e.matmul(
                                psum[b][:, 0:n], lhsT=wgate[:, k, cj * 128:(cj + 1) * 128], rhs=hT[:, k, c0:c0 + n],
                                start=(k == 0), stop=(k == KD - 1)), reads=[wk, ('hT', nt, k)], writes=[PS(bg)])
                        t = newtmp()
                        P.add('act', lambda e, b=bg, t=t, n=n: e.activation(tmp[t][:, 0:n], psum[b][:, 0:n], AF.Sigmoid),
                              reads=[PS(bg)], writes=[('tmp', t)])
                        if kind == 'p':
                            P.add('dve', lambda e, b=bv, t=t, n=n: e.tensor_tensor(
                                up[jb][:, HALO:HALO + n], psum[b][:, 0:n], tmp[t][:, 0:n], ALU.mult),
                                reads=[PS(bv), ('tmp', t)], writes=[('up', jb)])
                            P.add('act', lambda e, n=n: e.copy(upb[:, 0:HALO + n], up[jb][:, 0:HALO + n]),
                                  reads=[('up', jb)], writes=['upb'])
                        else:
                            P.add('dve', lambda e, b=bv, t=t, n=n: e.tensor_tensor(
                                us3[:, :, HALO:HALO + DS], psum[b][:, 0:n].rearrange("p (s c) -> p s c", s=NS),
                                tmp[t][:, 0:n].rearrange("p (s c) -> p s c", s=NS), ALU.mult),
                                reads=[PS(bv), ('tmp', t)], writes=[('us', jb)])
                    P.add('pool', lambda e: e.tensor_copy(uhalo[:, j, :], up[jb][:, np_:np_ + HALO]),
                          reads=[('up', jb)], writes=['uhalo'])
                    if has_s:
                        stv = stc[:, j * 128:(j + 1) * 128].rearrange("(q r) c -> r q c", r=120)
                        P.add('sp', lambda e: e.dma_start(out=stst[jb][0:120, :, :], in_=stv),
                              writes=[('stst', 0)], dma_key=('stst', 0))
                        b = bank()
                        for q in range(4):
                            P.add('pe', lambda e, b=b, q=q: e.transpose(
                                psum[b][:, q * 120:(q + 1) * 120], stst[jb][0:120, q, :], ident[0:120, 0:120]),
                                reads=[('stst', 0), 'cst'], writes=[PS(b)])
                        P.add('act', lambda e, b=b: e.copy(us3[:, :, 0:HALO], psum[b][:, 0:480].rearrange("p (s c) -> p s c", s=NS)),
                              reads=[PS(b)], writes=[('us', jb)])

                def diag_build(j):
                    cwc = PC_CW + j * CWID
                    dg = diag[j % 2]
                    P.add(DIAG_ENG, lambda e: e.tensor_tensor(
                        dg, identb[:].unsqueeze(1).to_broadcast([128, CWID, 128]),
                        pp[:, cwc:cwc + CWID].unsqueeze(2).to_broadcast([128, CWID, 128]), ALU.mult),
                        reads=['identb', 'pp'], writes=[('diag', j % 2)])

                def conv_part(j):
                    jb = 0
                    us3 = us[jb].rearrange("p (s c) -> p s c", s=NS)
                    cwc = PC_CW + j * CWID
                    for nt, (kind, t0, n, c0) in enumerate(ntl):
                        if kind == 'p':
                            bcv = bank()
                            for t_ in range(CWID):
                                P.add('pe', lambda e, b=bcv, t_=t_, n=n: e.matmul(
                                    psum[b][:, 0:n], lhsT=diag[j % 2][:, t_, :], rhs=upb[:, t_:t_ + n],
                                    start=(t_ == 0), stop=(t_ == CWID - 1)), reads=[('diag', j % 2), 'upb'], writes=[PS(bcv)])
                            P.add('act', lambda e, b=bcv, c0=c0, n=n: e.activation(
                                yT[:, j, c0:c0 + n], psum[b][:, 0:n], AF.Identity, bias=pp[:, PC_CB + j:PC_CB + j + 1], scale=1.0),
                                reads=[PS(bcv), 'pp'], writes=[('yT', nt)])
                            if not has_s:
                                P.add('act', lambda e, c0=c0, n=n: e.copy(c2[:, j, c0:c0 + n], yT[:, j, c0:c0 + n]),
                                      reads=[('yT', nt)], writes=[('c2', nt, j)])
                                P.add('act', lambda e, c0=c0, n=n: e.activation(sq3[:, j, c0:c0 + n], yT[:, j, c0:c0 + n], AF.Square),
                                      reads=[('yT', nt)], writes=[('sq', nt, j)])
                            continue
                        P.add('dve', lambda e: e.tensor_copy(usb[:].rearrange("p (t s) -> p t s", s=NS), us3.rearrange("p s t -> p t s")),
                              reads=[('us', jb)], writes=['usb'])
                        bcs = bank()
                        for t_ in range(CWID):
                            P.add('pe', lambda e, b=bcs, t_=t_: e.matmul(
                                psum[b][:, 0:DS * NS], lhsT=diag[j % 2][:, t_, :], rhs=usb[:, t_ * NS:(t_ + DS) * NS],
                                start=(t_ == 0), stop=(t_ == CWID - 1)), reads=[('diag', j % 2), 'usb'], writes=[PS(bcs)])
                        P.add('act', lambda e, b=bcs, c0=c0, n=n: e.activation(
                            yT[:, j, c0:c0 + n].rearrange("p (s t) -> p s t", s=NS),
                            psum[b][:, 0:DS * NS].rearrange("p (t s) -> p s t", s=NS), AF.Identity,
                            bias=pp[:, PC_CB + j:PC_CB + j + 1], scale=1.0),
                            reads=[PS(bcs), 'pp'], writes=[('yT', nt)])
                    if last_tile:
                        b = bank()
                        P.add('pe', lambda e, b=b: e.transpose(psum[b][0:HALO, 0:128], up[jb][:, np_:np_ + HALO], ident),
                              reads=[('up', jb), 'cst'], writes=[PS(b)])
                        P.add('act', lambda e, b=b: e.copy(stage[0][0:HALO, j * 128:(j + 1) * 128], psum[b][0:HALO, 0:128]),
                              reads=[PS(b)], writes=[('stage', 0)])
                    if True:
                        if has_s:
                            b2 = bank()
                            tu = newtmp()
                            P.add('pool', lambda e, tu=tu: e.tensor_copy(
                                tmp[tu][:, 0:128].rearrange("p (s c) -> p s c", s=NS), us3[:, :, HALO:HALO + DS]),
                                reads=[('us', jb)], writes=[('tmp', tu)])
                            P.add('pe', lambda e, b=b2, tu=tu: e.transpose(psum[b][:, 0:128], tmp[tu][:, 0:128], ident),
                                  reads=[('tmp', tu), 'cst'], writes=[PS(b2)])
                            P.add('act', lambda e, b=b2: e.copy(stage[1][:, j * 128:(j + 1) * 128], psum[b][:, 0:128]),
                                  reads=[PS(b2)], writes=[('stage', 1)])

                def att_stage1(wi, bl, cg):
                    wa = wi % 2
                    st = stw[wa]
                    gb = (pt0 // 128) + bl
                    mk = maskP0 if gb == 0 else maskP
                    for pair in range(2):
                        bSs = [bank(), bank()]
                        for uu in range(2):
                            c = cg * 4 + 2 * pair + uu
                            for hf in range(2):
                                P.add('pe', lambda e, b=bSs[hf], hf=hf, c=c, uu=uu: e.matmul(
                                    psum[b][:, uu * 256:(uu + 1) * 256],
                                    lhsT=qT[hf * 64:(hf + 1) * 64, c, pc0 + bl * 128:pc0 + (bl + 1) * 128],
                                    rhs=kT[hf * 64:(hf + 1) * 64, cg, bl * 128:bl * 128 + 256], start=True, stop=True),
                                    reads=[('qT', 0), ('kT', 0), ('kT', 'halo')], writes=[PS(bSs[hf])])
                        for hf in range(2):
                            si = 2 * pair + hf
                            sm = smb[si]
                            P.add('dve', lambda e, b=bSs[hf], sm=sm: e.tensor_tensor(sm[:], psum[b][:, 0:512], mk, ALU.add),
                                  reads=[PS(bSs[hf]), 'cst'], writes=[('sm', si)])
                            k8 = 4 * pair + 2 * hf
                            P.add('dve', lambda e, sm=sm, k8=k8: e.tensor_reduce(
                                st[:, k8:k8 + 2], sm[:].rearrange("p (h k) -> p h k", h=2), AX.X, ALU.max),
                                reads=[('sm', si)], writes=[('stw', wa)])
                    sk = pp[:, PC_SINKW + 8 * cg:PC_SINKW + 8 * cg + 8]
                    P.add('dve', lambda e: e.scalar_tensor_tensor(st[:, 8:16], st[:, 0:8], SCALE, sk, ALU.mult, ALU.max),
                          reads=[('stw', wa), 'pp'], writes=[('stw', wa)])
                    P.add('dve', lambda e: e.tensor_scalar(st[:, 16:24], st[:, 8:16], -1.0, None, ALU.mult),
                          reads=[('stw', wa)], writes=[('stw', wa)])
                    P.add('dve', lambda e: e.tensor_tensor(st[:, 24:32], sk, st[:, 16:24], ALU.add),
                          reads=[('stw', wa), 'pp'], writes=[('stw', wa)])

                def att_stage2(wi, bl, cg):
                    wa = wi % 2
                    st = stw[wa]
                    for pair in range(2):
                        for hf in range(2):
                            si = 2 * pair + hf
                            sm = smb[si]
                            for uu in range(2):
                                k8 = 4 * pair + 2 * hf + uu
                                P.add('act', lambda e, sm=sm, uu=uu, k8=k8: e.activation(
                                    sm[:, uu * 256:(uu + 1) * 256], sm[:, uu * 256:(uu + 1) * 256], AF.Exp,
                                    bias=st[:, 16 + k8:17 + k8], scale=SCALE, accum_out=st[:, 32 + k8:33 + k8]),
                                    reads=[('sm', si), ('stw', wa)], writes=[('sm', si), ('stw', wa)])
                    P.add('act', lambda e: e.activation(st[:, 40:48], st[:, 24:32], AF.Exp),
                          reads=[('stw', wa)], writes=[('stw', wa)])

                def att_stage3(wi, bl, cg):
                    wa = wi % 2
                    st = stw[wa]
                    P.add('dve', lambda e: e.tensor_tensor(st[:, 48:56], st[:, 32:40], st[:, 40:48], ALU.add),
                          reads=[('stw', wa)], writes=[('stw', wa)])
                    P.add('dve', lambda e: e.reciprocal(st[:, 56:64], st[:, 48:56]), reads=[('stw', wa)], writes=[('stw', wa)])
                    for pair in range(2):
                        for uu in range(2):
                            u = 2 * pair + uu
                            pb = pbb[u]
                            for hf in range(2):
                                si = 2 * pair + hf
                                sm = smb[si]
                                k8 = 4 * pair + 2 * hf + uu
                                P.add('dve', lambda e, sm=sm, pb=pb, hf=hf, uu=uu, k8=k8: e.tensor_scalar(
                                    pb[:, hf * 256:(hf + 1) * 256], sm[:, uu * 256:(uu + 1) * 256],
                                    st[:, 56 + k8:57 + k8], None, ALU.mult),
                                    reads=[('sm', si), ('stw', wa)], writes=[('pb', u)])

                def att_stage45(wi, bl, cg):
                    for u in range(4):
                        c = cg * 4 + u
                        pb, pT = pbb[u], pTb[u]
                        bT = bank()
                        bTb = psum[bT][:].bitcast(BF16)
                        for m in range(4):
                            P.add('pe', lambda e, bTb=bTb, pb=pb, m=m: e.transpose(
                                bTb[:, m * 128:(m + 1) * 128], pb[:, m * 128:(m + 1) * 128], identb[:]),
                                reads=[('pb', u), 'identb'], writes=[PS(bT)])
                        P.add('dve', lambda e, bTb=bTb, pT=pT: e.tensor_copy(pT[:], bTb[:, 0:512]), reads=[PS(bT)], writes=[('pT', u)])
                    for u in range(4):
                        c = cg * 4 + u
                        pT = pTb[u]
                        bO = bank()
                        idx = 0
                        for hf in range(2):
                            for kb in range(2):
                                P.add('pe', lambda e, b=bO, hf=hf, kb=kb, pT=pT, idx=idx: e.matmul(
                                    psum[b][:, 0:128], lhsT=vpad[:, bl + kb, cg * 2 + hf, :],
                                    rhs=pT[:, (hf * 2 + kb) * 128:(hf * 2 + kb + 1) * 128], start=(idx == 0), stop=(idx == 3)),
                                    reads=[('pT', u), ('vpad', bl + kb)], writes=[PS(bO)])
                                idx += 1
                        P.add('act', lambda e, b=bO, c=c: e.copy(oT[:, c, pc0 + bl * 128:pc0 + (bl + 1) * 128], psum[b][:, 0:128]),
                              reads=[PS(bO)], writes=[('oT', 0)])

                waves = [(bl, cg) for bl in range(pn // 128) for cg in range(2)]
                assert len(waves) == KD
                P.alias(['ostage', ('ckin', 0), ('ckin', 1), ('vcat', 0), ('vcat', 1)], [('diag', 0)])
                P.alias(SAMP_D1_KEYS, [('diag', 1)])
                diag_build(0)
                for j in range(KD):
                    bl, cg = waves[j]
                    att_stage1(j, bl, cg)
                    glu_part(j)
                    if j + 1 < KD:
                        diag_build(j + 1)
                    if j > 0:
                        att_stage45(j - 1, *waves[j - 1])
                    att_stage2(j, bl, cg)
                    conv_part(j)
                    att_stage3(j, bl, cg)
                att_stage45(KD - 1, *waves[KD - 1])
                if last_tile:
                    P.add('sp', lambda e: e.dma_start(out=csp, in_=stage[0][0:HALO, :]), reads=[('stage', 0)], dma_key=('stage', 0))
                if True:
                    if has_s:
                        for s in range(NS):
                            P.add('sp', lambda e, s=s: e.dma_start(out=css[s, 22:30, :], in_=stage[1][s * DS:(s + 1) * DS, :]),
                                  reads=[('stage', 1)], dma_key=('stage', 1))
                if not last_tile:
                    nb_ = pn // 128
                    P.add('pool', lambda e: e.tensor_copy(kT[:, :, 0:128], kT[:, :, pn:pn + 128]),
                          reads=[('kT', 0)], writes=[('kT', 'halo')])
                    P.add('pool', lambda e: e.tensor_copy(vpad[:, 0, :, :], vpad[:, nb_, :, :]),
                          reads=[('vpad', nb_)], writes=[('vpad', 0)])

                if MIX_STAGE >= 5:
                    if has_s:
                        skind, st0, sn, sc0 = ntl[1]
                        P.alias([('diag', 0)], ['ostage', ('ckin', 0), ('ckin', 1), ('vcat', 0), ('vcat', 1)])
                        P.alias([('diag', 1)], SAMP_D1_KEYS)
                        def samp_A(s):
                            a = s % 4
                            sm, pe_, pb, pT, st = smb[a], smb[a], pbb[a], pTb[a], stt_[a]
                            P.add('sp', lambda e, a=a, s=s: e.dma_start(out=ckin[a], in_=ck[s, :, :]), writes=[('ckin', a)], dma_key=('ckin', a))
                            P.add('pool', lambda e, a=a, s=s: e.dma_start(out=vcat[a], in_=cv[s, :, :]), writes=[('vcat', a)], dma_key=('vcat', a))
                            P.add('sp', lambda e, a=a, s=s: e.dma_start(out=vnew[a][0:DS, :], in_=vsb[s * DS:(s + 1) * DS, :]),
                                  reads=['vsb'], writes=[('vnew', a)], dma_key=('vnew', a))
                            b = bank()
                            for kc in range(2):
                                P.add('pe', lambda e, b=b, kc=kc, a=a: e.transpose(psum[b][:, kc * 128:(kc + 1) * 128], ckin[a][:, kc * 128:(kc + 1) * 128], ident),
                                      reads=[('ckin', a), 'cst'], writes=[PS(b)])
                            P.add('act', lambda e, b=b, a=a: e.copy(kcat[a][:, :, 0:128], psum[b][:, 0:256].rearrange("p (k c) -> p k c", k=2)),
                                  reads=[PS(b)], writes=[('kcat', a)])
                            P.add('dve', lambda e, a=a, s=s, sc0=sc0: e.tensor_copy(kcat[a][:, :, 128:136], kT[:, :, 128 + sc0 + s * DS:128 + sc0 + (s + 1) * DS]),
                                  reads=[('kT', 1)], writes=[('kcat', a)])
                            bS = bank()
                            for kc in range(2):
                                P.add('pe', lambda e, b=bS, kc=kc, a=a, s=s: e.matmul(psum[b][:, 0:136], lhsT=qpad[:, kc, s, :], rhs=kcat[a][:, kc, :],
                                                                                   start=(kc == 0), stop=(kc == 1)),
                                      reads=['qpad', ('kcat', a)], writes=[PS(bS)])
                            P.add('dve', lambda e, b=bS, sm=sm: e.tensor_tensor(sm[:, 0:136], psum[b][:, 0:136], maskS, ALU.add),
                                  reads=[PS(bS), 'cst'], writes=[('sm', a)])
                            P.add('dve', lambda e, sm=sm, st=st: e.tensor_reduce(st[:, 0:1], sm[:, 0:136], AX.X, ALU.max),
                                  reads=[('sm', a)], writes=[('st', a)])
                            sk = pp[:, PC_SINKROW:PC_SINKROW + 1]
                            P.add('dve', lambda e, st=st, sk=sk: e.scalar_tensor_tensor(st[:, 2:3], st[:, 0:1], SCALE, sk, ALU.mult, ALU.max),
                                  reads=[('st', a), 'pp'], writes=[('st', a)])
                            P.add('dve', lambda e, st=st: e.tensor_scalar(st[:, 4:5], st[:, 2:3], -1.0, None, ALU.mult),
                                  reads=[('st', a)], writes=[('st', a)])
                            P.add('dve', lambda e, st=st, sk=sk: e.tensor_tensor(st[:, 8:9], sk, st[:, 4:5], ALU.add),
                                  reads=[('st', a), 'pp'], writes=[('st', a)])
                        def samp_B(s):
                            a = s % 4
                            sm, pe_, pb, pT, st = smb[a], smb[a], pbb[a], pTb[a], stt_[a]
                            P.add('act', lambda e, sm=sm, pe_=pe_, st=st: e.activation(pe_[:, 0:136], sm[:, 0:136], AF.Exp, bias=st[:, 4:5], scale=SCALE,
                                                                                      accum_out=st[:, 6:7]),
                                  reads=[('sm', a), ('st', a)], writes=[('sm', a), ('st', a)])
                            P.add('act', lambda e, st=st: e.activation(st[:, 10:11], st[:, 8:9], AF.Exp), reads=[('st', a)], writes=[('st', a)])
                            P.add('dve', lambda e, st=st: e.tensor_tensor(st[:, 12:13], st[:, 6:7], st[:, 10:11], ALU.add),
                                  reads=[('st', a)], writes=[('st', a)])
                            P.add('dve', lambda e, st=st: e.reciprocal(st[:, 14:15], st[:, 12:13]), reads=[('st', a)], writes=[('st', a)])
                            P.add('dve', lambda e, pe_=pe_, pb=pb, st=st: e.tensor_scalar(pb[:, 0:136], pe_[:, 0:136], st[:, 14:15], None, ALU.mult),
                                  reads=[('sm', a), ('st', a)], writes=[('pb', a)])
                        def samp_C(s):
                            a = s % 4
                            sm, pe_, pb, pT, st = smb[a], smb[a], pbb[a], pTb[a], stt_[a]
                            bT = bank()
                            bTb = psum[bT][:].bitcast(BF16)
                            P.add('pe', lambda e, bTb=bTb, pb=pb: e.transpose(bTb[:, 0:128], pb[:, 0:128], identb[:]),
                                  reads=[('pb', a), 'identb'], writes=[PS(bT)])
                            P.add('pe', lambda e, bTb=bTb, pb=pb: e.transpose(bTb[0:DS, 128:256], pb[:, 128:136], identb[:]),
                                  reads=[('pb', a), 'identb'], writes=[PS(bT)])
                            P.add('act', lambda e, bTb=bTb, pT=pT: e.copy(pT[:, 0:128], bTb[:, 0:128]), reads=[PS(bT)], writes=[('pT', a)])
                            P.add('act', lambda e, bTb=bTb, a=a: e.copy(pT8[a][0:DS, :], bTb[0:DS, 128:256]), reads=[PS(bT)], writes=[('pT8', a)])
                            bO = bank()
                            for kc in range(2):
                                P.add('pe', lambda e, b=bO, kc=kc, a=a, pT=pT: e.matmul(psum[b][:, kc * 64:(kc + 1) * 64], lhsT=vcat[a][:, kc * 128:(kc + 1) * 128],
                                                                                      rhs=pT[:, kc * 64:(kc + 1) * 64], start=True, stop=False),
                                      reads=[('vcat', a), ('pT', a)], writes=[PS(bO)])
                                P.add('pe', lambda e, b=bO, kc=kc, a=a: e.matmul(psum[b][:, kc * 64:(kc + 1) * 64], lhsT=vnew[a][0:DS, kc * 128:(kc + 1) * 128],
                                                                               rhs=pT8[a][0:DS, kc * 64:(kc + 1) * 64], start=False, stop=True),
                                      reads=[('vnew', a), ('pT8', a)], writes=[PS(bO)])
                            P.add('act', lambda e, b=bO, s=s: e.copy(ostage[:, s, :], psum[b][:, 0:128]), reads=[PS(bO)], writes=['ostage'])

                        for t_ in range(NS + 2):
                            if t_ < NS:
                                samp_A(t_)
                            if 0 <= t_ - 1 < NS:
                                samp_B(t_ - 1)
                            if 0 <= t_ - 2 < NS:
                                samp_C(t_ - 2)
                        for pos in range(16):
                            c, hf = pos // 2, pos % 2
                            P.add('dve', lambda e, c=c, hf=hf, pos=pos, sc0=sc0, sn=sn: e.tensor_copy(
                                oT[hf * 64:(hf + 1) * 64, c, sc0:sc0 + sn].rearrange("p (s c) -> p s c", s=NS),
                                ostage[hf * 64:(hf + 1) * 64, :, pos * DS:(pos + 1) * DS]),
                                reads=['ostage'], writes=[('oT', 1)])
                        P.alias(['qpad'], [('sq', nt, k) for nt in range(2) for k in range(KD)])

                if MIX_STAGE >= 2:
                    for nt, (kind, t0, n, c0) in enumerate(ntl):
                        for k in range(KD if has_s else 0):
                            P.add('act', lambda e, k=k, c0=c0, n=n: e.copy(c2[:, k, c0:c0 + n], yT[:, k, c0:c0 + n]),
                                  reads=[('yT', nt)], writes=[('c2', nt, k)])
                            P.add('act', lambda e, k=k, c0=c0, n=n: e.activation(sq3[:, k, c0:c0 + n], yT[:, k, c0:c0 + n], AF.Square),
                                  reads=[('yT', nt)], writes=[('sq', nt, k)])
                        b1 = bank()
                        for k in range(KD):
                            P.add('pe', lambda e, b=b1, k=k, c0=c0, n=n: e.matmul(psum[b][:, 0:n], lhsT=ones[:], rhs=c2[:, k, c0:c0 + n],
                                                                               start=(k == 0), stop=(k == KD - 1)),
                                  reads=['ones', ('c2', nt, k)], writes=[PS(b1)])
                        b2 = bank()
                        for k in range(KD):
                            P.add('pe', lambda e, b=b2, k=k, c0=c0, n=n: e.matmul(psum[b][:, 0:n], lhsT=ones[:], rhs=sq3[:, k, c0:c0 + n],
                                                                               start=(k == 0), stop=(k == KD - 1)),
                                  reads=['ones', ('sq', nt, k)], writes=[PS(b2)])
                        tm = newtmp()
                        P.add('dve', lambda e, b=b1, tm=tm, n=n: e.tensor_scalar(tmp[tm][:, 0:n], psum[b][:, 0:n], 1.0 / D, None, ALU.mult),
                              reads=[PS(b1)], writes=[('tmp', tm)])
                        tv = newtmp()
                        P.add('dve', lambda e, tm=tm, tv=tv, n=n: e.tensor_tensor(tmp[tv][:, 0:n], tmp[tm][:, 0:n], tmp[tm][:, 0:n], ALU.mult),
                              reads=[('tmp', tm)], writes=[('tmp', tv)])
                        P.add('dve', lambda e, b=b2, tv=tv, n=n: e.scalar_tensor_tensor(tmp[tv][:, 0:n], psum[b][:, 0:n], 1.0 / D, tmp[tv][:, 0:n],
                                                                                       ALU.mult, ALU.subtract),
                              reads=[PS(b2), ('tmp', tv)], writes=[('tmp', tv)])
                        r = newrs()
                        P.add('act', lambda e, tv=tv, r=r, n=n: e.activation(rsb[r][:, 0:n], tmp[tv][:, 0:n], AF.Ln, bias=eps_ap, scale=1.0),
                              reads=[('tmp', tv), 'pp'], writes=[('rs', r)])
                        P.add('act', lambda e, r=r, n=n: e.activation(rsb[r][:, 0:n], rsb[r][:, 0:n], AF.Exp, scale=-0.5),
                              reads=[('rs', r)], writes=[('rs', r)])
                        P.add('dve', lambda e, tm=tm, r=r, n=n: e.scalar_tensor_tensor(tmp[tm][:, 0:n], tmp[tm][:, 0:n], -1.0, rsb[r][:, 0:n],
                                                                                      ALU.mult, ALU.mult),
                              reads=[('tmp', tm), ('rs', r)], writes=[('tmp', tm)])
                        for k in range(KD):
                            t1 = newtmp()
                            while t1 in (tm, tv):
                                t1 = newtmp()
                            P.add('dve', lambda e, k=k, t1=t1, r=r, c0=c0, n=n: e.scalar_tensor_tensor(
                                tmp[t1][:, 0:n], yT[:, k, c0:c0 + n], pp[:, PC_LNG + k:PC_LNG + k + 1], rsb[r][:, 0:n], ALU.mult, ALU.mult),
                                reads=[('yT', nt), 'pp', ('rs', r)], writes=[('tmp', t1)])
                            P.add('dve', lambda e, k=k, t1=t1, tm=tm, n=n: e.scalar_tensor_tensor(
                                tmp[t1][:, 0:n], tmp[tm][:, 0:n], pp[:, PC_LNG + k:PC_LNG + k + 1], tmp[t1][:, 0:n], ALU.mult, ALU.add),
                                reads=[('tmp', tm), 'pp', ('tmp', t1)], writes=[('tmp', t1)])
                            P.add('act', lambda e, k=k, t1=t1, c0=c0, n=n: e.activation(
                                c2[:, k, c0:c0 + n], tmp[t1][:, 0:n], AF.Silu, bias=pp[:, PC_LNB + k:PC_LNB + k + 1], scale=1.0),
                                reads=[('tmp', t1), 'pp'], writes=[('c2', nt, k)])

                P.alias(R1_KEYS_U, R1_KEYS_M)
                if MIX_STAGE >= 6:
                    for g in range(4):
                        wvX, wkX = w_next()
                        wvY, wkY = w_next(keep_prev=True)
                        wc_ = wvX[:, 0:2048].rearrange("p (k c) -> p k c", k=8)
                        wa_ = wvX[:, 2048:4096].rearrange("p (k c) -> p k c", k=8)
                        wgc_ = wvY[:, 0:2048].rearrange("p (k c) -> p k c", k=8)
                        wga_ = wvY[:, 2048:4096].rearrange("p (k c) -> p k c", k=8)
                        for cc in range(2):
                            c = g * 2 + cc
                            for nt, (kind, t0, n, c0) in enumerate(ntl):
                                bs_ = []
                                for (wmat, wkey, src, skey) in ((wc_, wkX, c2, ('c2', nt)), (wa_, wkX, oT, ('oT', nt)),
                                                                (wgc_, wkY, hT, ('hT', nt)), (wga_, wkY, hT, ('hT', nt))):
                                    b = bank()
                                    bs_.append(b)
                                    for k in range(KD):
                                        P.add('pe', lambda e, b=b, k=k, wmat=wmat, src=src, cc=cc, c0=c0, n=n: e.matmul(
                                            psum[b][:, 0:n], lhsT=wmat[:, k, cc * 128:(cc + 1) * 128], rhs=src[:, k, c0:c0 + n],
                                            start=(k == 0), stop=(k == KD - 1)), reads=[wkey, (skey + (k,)) if skey[0] in ('hT', 'c2') else skey], writes=[PS(b)])
                                bA, bB, bC, bD = bs_
                                t1 = newtmp()
                                t2 = newtmp()
                                P.add('act', lambda e, b=bC, t1=t1, n=n: e.activation(tmp[t1][:, 0:n], psum[b][:, 0:n], AF.Sigmoid),
                                      reads=[PS(bC)], writes=[('tmp', t1)])
                                P.add('act', lambda e, b=bD, t2=t2, n=n: e.activation(tmp[t2][:, 0:n], psum[b][:, 0:n], AF.Sigmoid),
                                      reads=[PS(bD)], writes=[('tmp', t2)])
                                P.add('dve', lambda e, b=bA, t1=t1, n=n: e.tensor_tensor(tmp[t1][:, 0:n], psum[b][:, 0:n], tmp[t1][:, 0:n], ALU.mult),
                                      reads=[PS(bA), ('tmp', t1)], writes=[('tmp', t1)])
                                P.add('dve', lambda e, b=bB, t2=t2, n=n: e.tensor_tensor(tmp[t2][:, 0:n], psum[b][:, 0:n], tmp[t2][:, 0:n], ALU.mult),
                                      reads=[PS(bB), ('tmp', t2)], writes=[('tmp', t2)])
                                P.add('dve', lambda e, t1=t1, t2=t2, c=c, c0=c0, n=n: e.tensor_tensor(mT[:, c, c0:c0 + n], tmp[t1][:, 0:n], tmp[t2][:, 0:n], ALU.add),
                                      reads=[('tmp', t1), ('tmp', t2)], writes=[('mT', nt)])
                if MIX_STAGE >= 7:
                    for c in range(KD):
                        if c % 4 == 0:
                            wv, wk = w_next()
                            wo_ = wv[:, 0:4096].rearrange("p (k c) -> p k c", k=8)
                        cq = c % 4
                        for nt, (kind, t0, n, c0) in enumerate(ntl):
                            b = bank()
                            for k in range(KD):
                                P.add('pe', lambda e, b=b, k=k, wo_=wo_, cq=cq, c0=c0, n=n: e.matmul(
                                    psum[b][:, 0:n], lhsT=wo_[:, k, cq * 128:(cq + 1) * 128], rhs=mT[:, k, c0:c0 + n],
                                    start=(k == 0), stop=(k == KD - 1)), reads=[wk, ('mT', nt)], writes=[PS(b)])
                            P.add('dve', lambda e, b=b, c=c, c0=c0, n=n: e.tensor_copy(yT[:, c, c0:c0 + n], psum[b][:, 0:n]),
                                  reads=[PS(b)], writes=[('yT', nt)])
                            P.add('act', lambda e, c=c, c0=c0, n=n: e.activation(sq3[:, c, c0:c0 + n], yT[:, c, c0:c0 + n], AF.Square),
                                  reads=[('yT', nt)], writes=[('sq', nt, c)])
                    for nt, (kind, t0, n, c0) in enumerate(ntl):
                        post_norm_residual(PC_MPOST, nt, c0, n, False)

            if ti + 1 < n_tiles:
                nxt = tile_ntl(tiles[ti + 1])
                issue_rope(nxt)
                for bi in range(min(4, len(tile_blocks(nxt)))):
                    issue_x_load(nxt, bi)
            if 'ffn2' not in skip:
                ffn(ntl, PC_F2PRE, PC_F2POST)

            P.alias(R1_KEYS_HID, R1S_KEYS[0:5])
            for bi, (nt, kind, r0, col) in enumerate(blocks):
                dstd = yp if kind == 'p' else ys
                sbuf_, skey = r1s[bi], ('r1s', bi)
                for hb in range(2):
                    b = bank()
                    for kk in range(4):
                        k = hb * 4 + kk
                        P.add('pe', lambda e, b=b, kk=kk, k=k, col=col: e.transpose(
                            psum[b][:, kk * 128:(kk + 1) * 128], xT[:, k, col:col + 128], ident),
                            reads=[('xT', nt, k), 'cst'], writes=[PS(b)])
                    if hb == 0:
                        P.add('act', lambda e, b=b, sbuf_=sbuf_: e.copy(sbuf_[:, 0:512], psum[b][:, 0:512]),
                              reads=[PS(b)], writes=[skey])
                    else:
                        P.add('dve', lambda e, b=b, sbuf_=sbuf_: e.tensor_copy(sbuf_[:, 512:1024], psum[b][:, 0:512]),
                              reads=[PS(b)], writes=[skey])
                P.add('sp', lambda e, sbuf_=sbuf_, dstd=dstd, r0=r0: e.dma_start(out=dstd[r0:r0 + 128, :], in_=sbuf_),
                      reads=[skey], dma_key=skey)

        P.emit(nc)
    return nc


def _fm(v):
    return np.ascontiguousarray(np.asarray(v, np.float32).reshape(KD, 128).T)


def _constants():
    cst = np.zeros((128, NCC), np.float32)
    cst[:, CC_ID:CC_ID + 128] = np.eye(128, dtype=np.float32)
    for m in range(128):
        d = m % 64
        k = m + 8 if d < 8 else (m - 8 if d < 16 else m)
        cst[k, CC_PERM + m] = 1.0
    i = np.arange(128)[:, None]
    j = np.arange(256)[None, :]
    vis = (j > i) & (j <= i + 128)
    mp = np.where(vis, 0.0, NEG).astype(np.float32)
    cst[:, CC_MP:CC_MP + 256] = mp
    cst[:, CC_MP + 256:CC_MP + 512] = mp
    mp0 = mp.copy()
    mp0[:, 0:128] = NEG
    cst[:, CC_MP0:CC_MP0 + 256] = mp0
    cst[:, CC_MP0 + 256:CC_MP0 + 512] = mp0
    jq = (np.arange(128) % DS)[:, None]
    key = np.arange(136)[None, :]
    vis_s = np.where(key < 128, key > jq, (key - 128) <= jq)
    cst[:, CC_MS:CC_MS + 136] = np.where(vis_s, 0.0, NEG).astype(np.float32)
    inv = np.exp(-math.log(500000.0) * np.arange(0, 16, 2, dtype=np.float32) / np.float32(16)).astype(np.float32)
    pos = np.concatenate([np.arange(SEQ), np.tile(PAST + np.arange(DS), NS)]).astype(np.float32)
    ang = (pos[:, None] * inv[None, :]).astype(np.float32)
    cos = np.cos(ang).astype(np.float32)
    sin = np.sin(ang).astype(np.float32)
    rope = np.zeros((128, 2, SEQ + NS * DS), np.float32)
    rope[:, 0, :] = 1.0
    for p in range(128):
        d = p % 64
        if d < 8:
            rope[p, 0] = cos[:, d]
            rope[p, 1] = -sin[:, d]
        elif d < 16:
            rope[p, 0] = cos[:, d - 8]
            rope[p, 1] = sin[:, d - 8]
    return cst, rope


_CACHE = {}


def kernel(x_prompt, x_sample, state_conv, cache_k_win, cache_v_win,
           ffn1_pre_g, ffn1_w_up, ffn1_w_down, ffn1_post_g,
           mix_pre_g, w_in, conv_dw_w, conv_dw_b, conv_ln_g, conv_ln_b, w_conv_out,
           attn_sinks, w_attn_out, w_out, mix_post_g,
           ffn2_pre_g, ffn2_w_up, ffn2_w_down, ffn2_post_g):
    f = lambda a: np.ascontiguousarray(np.asarray(a, dtype=np.float32))
    n_cores = 8
    if 'nc' not in _CACHE:
        _CACHE['nc'] = build_program()
        _CACHE['cst'] = _constants()
    nc = _CACHE['nc']
    cst, rope = _CACHE['cst']

    win_ = f(w_in[0]).copy()
    qcols = np.concatenate([np.arange(2048 + h * 64, 2048 + (h + 1) * 64) for h in HP])
    win_[:, 2048:3072] = win_[:, qcols]
    wao_ = f(w_attn_out[0])
    arows = np.concatenate([np.arange(h * 64, (h + 1) * 64) for h in HP])
    wao_ = np.ascontiguousarray(wao_[arows, :])
    pp = np.zeros((128, NPP), np.float32)
    for col, v in ((PC_F1PRE, ffn1_pre_g), (PC_F1POST, ffn1_post_g), (PC_MPRE, mix_pre_g), (PC_MPOST, mix_post_g),
                   (PC_F2PRE, ffn2_pre_g), (PC_F2POST, ffn2_post_g), (PC_LNG, conv_ln_g), (PC_LNB, conv_ln_b),
                   (PC_CB, conv_dw_b)):
        pp[:, col:col + 8] = _fm(f(v)[0])
    cw = f(conv_dw_w[0])
    pp[:, PC_CW:PC_CW + 8 * CWID] = cw.reshape(CWID, KD, 128).transpose(2, 1, 0).reshape(128, KD * CWID)
    sinks = f(attn_sinks[0])
    sp = sinks[np.array(HP)]
    pp[:, PC_SINKROW] = np.repeat(sp, DS)
    pp[:, PC_SINKBC:PC_SINKBC + 16] = sp[None, :]
    for cg in range(2):
        for pair in range(2):
            for hf in range(2):
                for uu in range(2):
                    pos = 2 * (cg * 4 + 2 * pair + uu) + hf
                    pp[:, PC_SINKW + 8 * cg + 4 * pair + 2 * hf + uu] = sp[pos]
    pp[:, PC_EPS] = EPS
    pp[:, PC_EPS4] = 4 * EPS

    wsrc = dict(f1u=f(ffn1_w_up[0]), f1d=f(ffn1_w_down[0]), win=win_, wco=f(w_conv_out[0]), wao=wao_, wo=f(w_out[0]),
                f2u=f(ffn2_w_up[0]), f2d=f(ffn2_w_down[0]))
    units = weight_plan()
    wall = np.zeros((len(units), 128, 4096), np.float32)
    for u, parts in enumerate(units):
        for (a, kk, cc, nm, r0, c0) in parts:
            blk = wsrc[nm][r0:r0 + kk * 128, c0:c0 + cc].reshape(kk, 128, cc).transpose(1, 0, 2).reshape(128, kk * cc)
            wall[u, :, a:a + kk * cc] = blk
    shared = dict(wall=wall, pp=pp, cst=cst, rope=rope)
    xpa, xsa = f(x_prompt), f(x_sample)
    sca, cka, cva = f(state_conv[0]), f(cache_k_win[0]), f(cache_v_win[0])
    in_maps = []
    for i in range(n_cores):
        m = dict(shared)
        m['xp'] = xpa[i]
        m['xs'] = xsa[NS * i:NS * (i + 1)].reshape(NS * DS, D)
        m['stc'] = sca[NS * i:NS * (i + 1)].reshape(NS * HALO, D)
        m['ck'] = cka[NS * i:NS * (i + 1)].reshape(NS, 128, 256)
        m['cv'] = cva[NS * i:NS * (i + 1)].reshape(NS, 128, 256)
        in_maps.append(m)
    if _CACHE.get('dry'):
        return nc, in_maps
    res = run_bass_kernel_spmd(nc, in_maps, core_ids=list(range(n_cores)))
    R = res.results
    y_prompt = np.stack([R[i]['yp'] for i in range(n_cores)]).astype(np.float32)
    y_sample = np.concatenate([R[i]['ys'].reshape(NS, DS, D) for i in range(n_cores)]).astype(np.float32)
    conv_p = np.stack([R[i]['csp'] for i in range(n_cores)])[None].astype(np.float32)
    k_p = np.stack([R[i]['kwp'].reshape(128, 4, 64) for i in range(n_cores)])[None].astype(np.float32)
    v_p = np.stack([R[i]['vwp'].reshape(128, 4, 64) for i in range(n_cores)])[None].astype(np.float32)
    conv_s = np.concatenate([R[i]['css'] for i in range(n_cores)])[None].astype(np.float32)
    k_s = np.concatenate([R[i]['kws'].reshape(NS, 128, 4, 64) for i in range(n_cores)])[None].astype(np.float32)
    v_s = np.concatenate([R[i]['vws'].reshape(NS, 128, 4, 64) for i in range(n_cores)])[None].astype(np.float32)
    return (y_prompt, y_sample, conv_p, k_p, v_p, conv_s, k_s, v_s)
```

```python
import math
import numpy as np
from contextlib import ExitStack
import concourse.bass as bass
import concourse.mybir as mybir
from concourse.bass_utils import run_bass_kernel_spmd

F32 = mybir.dt.float32
BF16 = mybir.dt.bfloat16
ALU = mybir.AluOpType
AF = mybir.ActivationFunctionType
AX = mybir.AxisListType

COMPUTE = ('pe', 'act', 'dve', 'pool')


class Op:
    __slots__ = ('eng', 'fn', 'deps', 'dma_key', 'dma_cnt', 'needed', 'ms', 'is_dma')

    def __init__(self, eng, fn, dma_key=None):
        self.eng = eng
        self.fn = fn
        self.deps = []
        self.dma_key = dma_key
        self.is_dma = dma_key is not None
        self.dma_cnt = 0
        self.needed = False
        self.ms = 0


class Prog:
    def __init__(self):
        self.ops = {e: [] for e in ('pe', 'act', 'dve', 'pool', 'sp')}
        self.last_writer = {}
        self.readers = {}
        self.dma_count = {}

    def add(self, eng, fn, reads=(), writes=(), dma_key=None):
        op = Op(eng, fn, dma_key)
        deps = {}
        for b in reads:
            w = self.last_writer.get(b)
            if w is not None:
                deps[id(w)] = (w, True)
        for b in writes:
            w = self.last_writer.get(b)
            if w is not None and id(w) not in deps:
                deps[id(w)] = (w, False)
            for r in self.readers.get(b, ()):
                if id(r) not in deps:
                    deps[id(r)] = (r, False)
        for d, raw in deps.values():
            if (not d.is_dma) and (not op.is_dma) and d.eng == op.eng:
                if op.eng == 'pe':
                    continue
            op.deps.append(d)
            d.needed = True
        if op.is_dma:
            c = self.dma_count.get(dma_key, 0) + 1
            self.dma_count[dma_key] = c
            op.dma_cnt = c
        for b in writes:
            self.last_writer[b] = op
            self.readers[b] = []
        for b in reads:
            if b in writes:
                continue
            self.readers.setdefault(b, []).append(op)
        self.ops[eng].append(op)
        return op

    def alias(self, old_keys, new_keys):
        pend = []
        seen = set()
        for k in old_keys:
            w = self.last_writer.get(k)
            if w is not None and id(w) not in seen:
                seen.add(id(w))
                pend.append(w)
            for r in self.readers.get(k, ()):
                if id(r) not in seen:
                    seen.add(id(r))
                    pend.append(r)
        for k in new_keys:
            extra = []
            w = self.last_writer.get(k)
            if w is not None and id(w) not in seen:
                extra.append(w)
            for r in self.readers.get(k, ()):
                if id(r) not in seen:
                    extra.append(r)
            self.last_writer[k] = None
            self.readers[k] = list(pend) + extra

    def emit(self, nc):
        with ExitStack() as es:
            esem = {e: es.enter_context(nc.semaphore('ms_' + e)) for e in COMPUTE}
            dsem = {}
            for i, k in enumerate(self.dma_count):
                dsem[k] = es.enter_context(nc.semaphore('dq%d' % i))
            for e in COMPUTE:
                m = 0
                for op in self.ops[e]:
                    if (not op.is_dma) and op.needed:
                        m += 1
                        op.ms = m
            block = es.enter_context(nc.Block())

            def run(eng_name, eng):
                seen = {}
                for op in self.ops[eng_name]:
                    need = {}
                    for d in op.deps:
                        if d.is_dma:
                            s, v = dsem[d.dma_key], 16 * d.dma_cnt
                        else:
                            s, v = esem[d.eng], d.ms
                        key = id(s)
                        if key not in need or need[key][1] < v:
                            need[key] = (s, v)
                    for key, (s, v) in need.items():
                        if seen.get(key, 0) >= v:
                            continue
                        seen[key] = v
                        eng.wait_ge(s, v)
                    ins = op.fn(eng)
                    if op.is_dma:
                        ins.then_inc(dsem[op.dma_key], 16)
                    elif op.needed:
                        ins.then_inc(esem[eng_name], 1)
                if eng_name == 'sp':
                    for k, c in self.dma_count.items():
                        eng.wait_ge(dsem[k], 16 * c)

            @block.tensor
            def _(e):
                run('pe', e)

            @block.scalar
            def _(e):
                run('act', e)

            @block.vector
            def _(e):
                run('dve', e)

            @block.gpsimd
            def _(e):
                run('pool', e)

            @block.sync
            def _(e):
                run('sp', e)


D = 1024
DFF = 2816
KD = 8
KF = 22
SEQ = 2048
NS = 16
DS = 8
PAST = 8192
CWID = 31
HALO = 30
EPS = 1e-6
SCALE = 0.125
NEG = -1e30
W = 640
HP = [0, 4, 1, 5, 2, 6, 3, 7, 8, 12, 9, 13, 10, 14, 11, 15]

PC_F1PRE, PC_F1POST, PC_MPRE, PC_MPOST, PC_F2PRE, PC_F2POST, PC_LNG, PC_LNB, PC_CB = [8 * i for i in range(9)]
PC_CW = 72
PC_SINKROW = PC_CW + 8 * CWID
PC_SINKBC = PC_SINKROW + 1
PC_EPS = PC_SINKBC + 16
PC_EPS4 = PC_EPS + 1
PC_SINKW = 340
NPP = 356
CC_ID, CC_PERM, CC_MP, CC_MP0, CC_MS = 0, 128, 256, 768, 1280
NCC = 1280 + 136

FFN_STAGE = 3
USE_WSCR = True
DIAG_ENG = 'pool'
MIX_STAGE = 7
ATT_STAGE = 6
TILES = [
    [('p', 0, 512), ('s', 0, 128)],
    [('p', 512, 512)],
    [('p', 1024, 512)],
    [('p', 1536, 512)],
]


def weight_plan():
    units = []

    def ffn_units(up, dn):
        for g in range(11):
            units.append([(0, 8, 256, up, 0, g * 256), (2048, 8, 256, up, 0, DFF + g * 256)])
        for g in range(8):
            units.append([(0, 11, 128, dn, 0, g * 128), (1408, 11, 128, dn, 1408, g * 128)])
    ffn_units('f1u', 'f1d')
    units.append([(0, 8, 512, 'win', 0, 3072)])
    units.append([(0, 8, 512, 'win', 0, 2048)])
    units.append([(0, 8, 512, 'win', 0, 2560)])
    for g in range(4):
        units.append([(0, 8, 256, 'win', 0, g * 256), (2048, 8, 256, 'win', 0, 1024 + g * 256)])
    for g in range(4):
        units.append([(0, 8, 256, 'wco', 0, g * 256), (2048, 8, 256, 'wao', 0, g * 256)])
        units.append([(0, 8, 256, 'win', 0, 3584 + g * 256), (2048, 8, 256, 'win', 0, 4608 + g * 256)])
    for g in range(2):
        units.append([(0, 8, 512, 'wo', 0, g * 512)])
    ffn_units('f2u', 'f2d')
    return units


def build_program(tiles=TILES, skip=()):
    nc = bass.Bass("TRN2", target_bir_lowering=False)
    P = Prog()

    def din(name, shape):
        return nc.dram_tensor(name, shape, F32, kind="ExternalInput").ap()

    def dout(name, shape):
        return nc.dram_tensor(name, shape, F32, kind="ExternalOutput").ap()

    xp = din("xp", [SEQ, D])
    xs = din("xs", [NS * DS, D])
    stc = din("stc", [NS * HALO, D])
    ck = din("ck", [NS, 128, 256])
    cv = din("cv", [NS, 128, 256])
    UNITS = weight_plan()
    wall = din("wall", [len(UNITS), 128, 4096])
    ppd = din("pp", [128, NPP])
    cstd = din("cst", [128, NCC])
    roped = din("rope", [128, 2, SEQ + NS * DS])

    yp = dout("yp", [SEQ, D])
    ys = dout("ys", [NS * DS, D])
    csp = dout("csp", [HALO, D])
    kwp = dout("kwp", [128, 256])
    vwp = dout("vwp", [128, 256])
    css = dout("css", [NS, HALO, D])
    kws = dout("kws", [NS, 128, 256])
    vws = dout("vws", [NS, 128, 256])

    with ExitStack() as es:
        def sb(name, shape, dt):
            return es.enter_context(nc.sbuf_tensor(name, shape, dt))

        xT = sb("xT", [128, KD, W], F32)
        hT = sb("hT", [128, KD, W], BF16)
        yT = sb("yT", [128, KD, W], F32)
        sq = sb("sq", [128, KD * W], BF16)
        c2 = sb("c2", [128, KD, W], BF16)
        R1 = sb("R1", [128, 7680], F32)
        kT = sb("kT", [128, 2, 128 + W], BF16)
        vpad = sb("vpad", [128, 6, 4, 128], BF16)
        ropeC = sb("ropeC", [128, W], F32)
        ropeS = sb("ropeS", [128, W], F32)
        wsl = [sb("wsl%d" % i, [128, 4096], BF16) for i in range(3)]
        pp = sb("pp_sb", [128, NPP], F32)
        cst = sb("cst_sb", [128, NCC], F32)
        identb = sb("identb", [128, 128], BF16)
        ones = sb("ones", [128, 128], BF16)
        stage = [sb("stage%d" % i, [128, D], F32) for i in range(2)]
        rsb = [sb("rs%d" % i, [128, 512], F32) for i in range(2)]
        tmp = [sb("tmp%d" % i, [128, 512], F32) for i in range(4)]
        uhalo = sb("uhalo", [128, KD, HALO], F32)
        usb = sb("usb", [128, (HALO + DS) * NS], BF16)
        diagt = [sb("diag%d" % i, [128, CWID * 128], BF16) for i in range(2)]
        diag = [d[:].rearrange("p (t m) -> p t m", t=CWID) for d in diagt]
        stst = [sb("stst0", [128, 4, 128], F32)] * 2
        smb = [sb("sm%d" % i, [128, 512], F32) for i in range(4)]
        pbb = [sb("pb%d" % i, [128, 512], BF16) for i in range(4)]
        pTb = [sb("pT%d" % i, [128, 512], BF16) for i in range(4)]
        stw = [sb("stw%d" % i, [128, 64], F32) for i in range(2)]
        pT8 = [sb("pT8%d" % i, [128, 128], BF16) for i in range(2)]
        stt_ = [sb("st%d" % i, [128, 16], F32) for i in range(4)]
        ckin = [diagt[0][:, 2048 + i * 512:2048 + (i + 1) * 512].bitcast(F32) for i in range(2)]
        kcat = [sb("kcat%d" % i, [128, 2, 136], BF16) for i in range(2)]
        vcat = [diagt[0][:, 3072 + i * 256:3072 + (i + 1) * 256] for i in range(2)]
        vnew = [sb("vnew%d" % i, [128, 256], BF16) for i in range(2)]
        ckin += [diagt[1][:, i * 512:(i + 1) * 512].bitcast(F32) for i in range(2)]
        vcat += [diagt[1][:, 1024 + i * 256:1024 + (i + 1) * 256] for i in range(2)]
        kcat = [kcat[0][:], kcat[1][:]] + [diagt[1][:, 1536 + i * 272:1536 + (i + 1) * 272].rearrange("p (k c) -> p k c", k=2) for i in range(2)]
        vnew = [vnew[0][:], vnew[1][:]] + [diagt[1][:, 2080 + i * 256:2080 + (i + 1) * 256] for i in range(2)]
        pT8 = [pT8[0][:], pT8[1][:]] + [diagt[1][:, 2592 + i * 128:2592 + (i + 1) * 128] for i in range(2)]
        SAMP_D1_KEYS = [(nm, i) for nm in ('ckin', 'vcat', 'kcat', 'vnew', 'pT8') for i in (2, 3)]
        ostage = diagt[0][:, 0:2048].rearrange("p (s c) -> p s c", s=NS)
        kf = sb("kf", [128, 2, 256], F32)
        vf = [stage[1][:, i * 256:(i + 1) * 256] for i in range(2)]
        vsb = sb("vsb", [128, 256], BF16)
        psum = [es.enter_context(nc.psum_tensor("ps%d" % i, [128, 512], F32)) for i in range(8)]

        sq3 = sq[:].rearrange("p (k w) -> p k w", k=KD)
        qpad = sq[:, 0:4096].rearrange("p (a s c) -> p a s c", a=2, s=NS)
        R1b = R1[:].bitcast(BF16)
        hid = R1b[:, 0:KF * W].rearrange("p (k w) -> p k w", k=KF)
        qT = R1b[:, 0:KD * W].rearrange("p (k w) -> p k w", k=KD)
        oT = R1b[:, KD * W:2 * KD * W].rearrange("p (k w) -> p k w", k=KD)
        mT = R1b[:, 2 * KD * W:3 * KD * W].rearrange("p (k w) -> p k w", k=KD)
        up = [R1[:, 5120:5662]] * 2
        us = [R1[:, 5664:6272]] * 2
        upb = R1b[:, 12544:13088]
        ident = cst[:, CC_ID:CC_ID + 128]
        permT = cst[:, CC_PERM:CC_PERM + 128]
        maskP = cst[:, CC_MP:CC_MP + 512]
        maskP0 = cst[:, CC_MP0:CC_MP0 + 512]
        maskS = cst[:, CC_MS:CC_MS + 136]
        eps_ap = pp[:, PC_EPS:PC_EPS + 1]
        eps4_ap = pp[:, PC_EPS4:PC_EPS4 + 1]

        R1_KEYS_HID = [('hid', nt) for nt in range(2)]
        R1_KEYS_U = [('up', 0), ('us', 0), 'upb']
        R1_KEYS_QO = [('qT', nt) for nt in range(2)] + [('oT', nt) for nt in range(2)]
        R1_KEYS_M = [('mT', nt) for nt in range(2)]
        R1_KEYS_QOM = R1_KEYS_QO + R1_KEYS_M

        state = {'bank': 0, 'tmp': 0, 'rs': 0}

        def bank():
            b = state['bank']
            state['bank'] = (b + 1) % 8
            return b

        def newtmp():
            t = state['tmp']
            state['tmp'] = (t + 1) % 4
            return t

        def newrs():
            t = state['rs']
            state['rs'] = (t + 1) % 2
            return t

        def PS(b):
            return ('ps', b)

        upt = len(UNITS)
        wunits = list(range(upt)) * len(tiles)
        UEXT = [max(a + kk * cc for (a, kk, cc, nm, r0, c0) in u_) for u_ in UNITS]
        wstate = {'issued': 0, 'used': 0, 'done': -1}
        PF = 2

        wscr = nc.dram_tensor("wscr", [upt, 128, 4096], BF16).ap() if (USE_WSCR and len(tiles) > 1) else None

        def w_issue(i):
            slot = i % 3
            u = i % upt
            ext = UEXT[u]
            if wscr is not None and i >= upt:
                P.add('pool', lambda e, slot=slot, u=u, ext=ext: e.dma_start(out=wsl[slot][:, 0:ext], in_=wscr[u, :, 0:ext]),
                      reads=[('wscr', u)], writes=[('w', slot)], dma_key=('w', slot))
                return
            for off in range(0, ext, 2048):
                nn = min(2048, ext - off)
                P.add('pool', lambda e, slot=slot, u=u, off=off, nn=nn: e.dma_start(out=wsl[slot][:, off:off + nn], in_=wall[u, :, off:off + nn]),
                      writes=[('w', slot)], dma_key=('w', slot))
            if wscr is not None:
                P.add('sp', lambda e, slot=slot, u=u, ext=ext: e.dma_start(out=wscr[u, :, 0:ext], in_=wsl[slot][:, 0:ext]),
                      reads=[('w', slot)], writes=[('wscr', u)], dma_key=('wst', slot))

        def w_next(keep_prev=False):
            i = wstate['used']
            wstate['used'] = i + 1
            if not keep_prev:
                wstate['done'] = i - 1
            while (wstate['issued'] < len(wunits) and wstate['issued'] <= i + PF
                   and (wstate['issued'] - 3 <= wstate['done'] or wstate['issued'] <= i)):
                assert wstate['issued'] - 3 <= wstate['done'], "weight ring too small"
                w_issue(wstate['issued'])
                wstate['issued'] += 1
            slot = i % 3
            return wsl[slot], ('w', slot)

        P.add('sp', lambda e: e.dma_start(out=pp[:], in_=ppd), writes=['pp'], dma_key='pp')
        P.add('sp', lambda e: e.dma_start(out=cst[:], in_=cstd), writes=['cst'], dma_key='cst')
        P.add('dve', lambda e: e.memset(ones[:], 1.0), writes=['ones'])
        P.add('dve', lambda e: e.tensor_copy(identb[:], ident), reads=['cst'], writes=['identb'])
        P.add('dve', lambda e: e.memset(vpad[:], 0.0), writes=[('vpad', s) for s in range(6)])
        P.add('dve', lambda e: e.memset(kT[:, :, 0:128], 0.0), writes=[('kT', 'halo')])
        P.add('dve', lambda e: e.memset(uhalo[:], 0.0), writes=['uhalo'])
        P.add('sp', lambda e: e.dma_start(out=kws[:, 0:120, :], in_=ck[:, 8:128, :]), dma_key='d2d')
        P.add('sp', lambda e: e.dma_start(out=vws[:, 0:120, :], in_=cv[:, 8:128, :]), dma_key='d2d')
        stc3 = stc.rearrange("(s r) c -> s r c", r=HALO)
        P.add('sp', lambda e: e.dma_start(out=css[:, 0:22, :], in_=stc3[:, 8:30, :]), dma_key='d2d')

        def stats_rs(src_keys, nt, c0, n, scale, eps_t):
            b = bank()
            for k in range(KD):
                P.add('pe', lambda e, b=b, k=k: e.matmul(psum[b][:, 0:n], lhsT=ones[:], rhs=sq3[:, k, c0:c0 + n],
                                                         start=(k == 0), stop=(k == KD - 1)),
                      reads=['ones', ('sq', nt, k)], writes=[PS(b)])
            r = newrs()
            P.add('act', lambda e, b=b, r=r: e.activation(rsb[r][:, 0:n], psum[b][:, 0:n], AF.Ln, bias=eps_t, scale=scale),
                  reads=[PS(b), 'pp'], writes=[('rs', r)])
            P.add('act', lambda e, r=r: e.activation(rsb[r][:, 0:n], rsb[r][:, 0:n], AF.Exp, scale=-0.5),
                  reads=[('rs', r)], writes=[('rs', r)])
            return r

        def norm_to_hT(pcol, nt, c0, n):
            for k in range(KD):
                P.add('act', lambda e, k=k: e.activation(sq3[:, k, c0:c0 + n], xT[:, k, c0:c0 + n], AF.Square),
                      reads=[('xT', nt, k)], writes=[('sq', nt, k)])
            r = stats_rs(None, nt, c0, n, 1.0 / D, eps_ap)
            for k in range(KD):
                P.add('dve', lambda e, k=k, r=r: e.scalar_tensor_tensor(hT[:, k, c0:c0 + n], xT[:, k, c0:c0 + n],
                                                                       pp[:, pcol + k:pcol + k + 1], rsb[r][:, 0:n],
                                                                       ALU.mult, ALU.mult),
                      reads=[('xT', nt, k), 'pp', ('rs', r)], writes=[('hT', nt, k)])

        def post_norm_residual(pcol, nt, c0, n, half):
            r = stats_rs(None, nt, c0, n, (4.0 if half else 1.0) / D, eps4_ap if half else eps_ap)
            for c in range(KD):
                t = newtmp()
                P.add('dve', lambda e, c=c, r=r, t=t: e.scalar_tensor_tensor(tmp[t][:, 0:n], yT[:, c, c0:c0 + n],
                                                                            pp[:, pcol + c:pcol + c + 1], rsb[r][:, 0:n],
                                                                            ALU.mult, ALU.mult),
                      reads=[('yT', nt), 'pp', ('rs', r)], writes=[('tmp', t)])
                P.add('dve', lambda e, c=c, t=t: e.tensor_tensor(xT[:, c, c0:c0 + n], xT[:, c, c0:c0 + n], tmp[t][:, 0:n], ALU.add),
                      reads=[('xT', nt, c), ('tmp', t)], writes=[('xT', nt, c)])

        def ffn(ntl, pre, post):
            for nt, (kind, t0, n, c0) in enumerate(ntl):
                norm_to_hT(pre, nt, c0, n)
            if FFN_STAGE < 1:
                return
            P.alias(R1_KEYS_QOM + R1_KEYS_U + R1S_KEYS, R1_KEYS_HID)
            for g in range(11):
                wv, wk = w_next()
                wg = wv[:, 0:2048].rearrange("p (k c) -> p k c", k=8)
                wu = wv[:, 2048:4096].rearrange("p (k c) -> p k c", k=8)
                for ci in range(2):
                    i = g * 2 + ci
                    for nt, (kind, t0, n, c0) in enumerate(ntl):
                        bg = bank()
                        for k in range(KD):
                            P.add('pe', lambda e, b=bg, k=k, wg=wg, ci=ci, c0=c0, n=n: e.matmul(
                                psum[b][:, 0:n], lhsT=wg[:, k, ci * 128:(ci + 1) * 128], rhs=hT[:, k, c0:c0 + n],
                                start=(k == 0), stop=(k == KD - 1)), reads=[wk, ('hT', nt, k)], writes=[PS(bg)])
                        bu = bank()
                        for k in range(KD):
                            P.add('pe', lambda e, b=bu, k=k, wu=wu, ci=ci, c0=c0, n=n: e.matmul(
                                psum[b][:, 0:n], lhsT=wu[:, k, ci * 128:(ci + 1) * 128], rhs=hT[:, k, c0:c0 + n],
                                start=(k == 0), stop=(k == KD - 1)), reads=[wk, ('hT', nt, k)], writes=[PS(bu)])
                        t = newtmp()
                        P.add('act', lambda e, b=bg, t=t, n=n: e.activation(tmp[t][:, 0:n], psum[b][:, 0:n], AF.Silu),
                              reads=[PS(bg)], writes=[('tmp', t)])
                        P.add('dve', lambda e, b=bu, t=t, i=i, c0=c0, n=n: e.tensor_tensor(
                            hid[:, i, c0:c0 + n], tmp[t][:, 0:n], psum[b][:, 0:n], ALU.mult),
                            reads=[PS(bu), ('tmp', t)], writes=[('hid', nt)])
            for c in range(KD):
                wv, wk = w_next()
                wd = wv[:, 0:2816].rearrange("p (k c) -> p k c", k=KF)
                for nt, (kind, t0, n, c0) in enumerate(ntl):
                    b = bank()
                    for i in range(KF):
                        P.add('pe', lambda e, b=b, i=i, wd=wd, c0=c0, n=n: e.matmul(
                            psum[b][:, 0:n], lhsT=wd[:, i, :], rhs=hid[:, i, c0:c0 + n],
                            start=(i == 0), stop=(i == KF - 1)), reads=[wk, ('hid', nt)], writes=[PS(b)])
                    P.add('dve', lambda e, b=b, c=c, c0=c0, n=n: e.tensor_copy(yT[:, c, c0:c0 + n], psum[b][:, 0:n]),
                          reads=[PS(b)], writes=[('yT', nt)])
                    P.add('act', lambda e, c=c, c0=c0, n=n: e.activation(sq3[:, c, c0:c0 + n], yT[:, c, c0:c0 + n], AF.Square),
                          reads=[('yT', nt)], writes=[('sq', nt, c)])
            if FFN_STAGE < 3:
                return
            for nt, (kind, t0, n, c0) in enumerate(ntl):
                post_norm_residual(post, nt, c0, n, True)

        def rope(b, n, c0):
            tq = newtmp()
            P.add('act', lambda e: e.copy(tmp[tq][:, 0:n], psum[b][:, 0:n]), reads=[PS(b)], writes=[('tmp', tq)])
            b2 = bank()
            P.add('pe', lambda e: e.matmul(psum[b2][:, 0:n], lhsT=permT, rhs=tmp[tq][:, 0:n], start=True, stop=True),
                  reads=['cst', ('tmp', tq)], writes=[PS(b2)])
            tB = newtmp()
            P.add('dve', lambda e: e.tensor_tensor(tmp[tB][:, 0:n], psum[b2][:, 0:n], ropeS[:, c0:c0 + n], ALU.mult),
                  reads=[PS(b2), 'rope'], writes=[('tmp', tB)])
            P.add('dve', lambda e: e.tensor_tensor(tmp[tq][:, 0:n], tmp[tq][:, 0:n], ropeC[:, c0:c0 + n], ALU.mult),
                  reads=[('tmp', tq), 'rope'], writes=[('tmp', tq)])
            return tq, tB

        n_tiles = len(tiles)
        r1s = [R1[:, i * 1024:(i + 1) * 1024] for i in range(7)]
        R1S_KEYS = [('r1s', i) for i in range(7)]
        c2f = c2[:].rearrange("p k w -> p (k w)").bitcast(F32)
        C2K = [('c2', nt_, k_) for nt_ in range(2) for k_ in range(KD)]
        LBUF = [(stage[0][:], ('stage', 0), []), (stage[1][:], ('stage', 1), []),
                (c2f[:, 0:1024], ('xstg', 2), C2K), (c2f[:, 1024:2048], ('xstg', 3), C2K),
                (stage[0][:], ('stage', 0), [])]

        def tile_ntl(tl):
            out, c0 = [], 0
            for (kind, t0, n) in tl:
                out.append((kind, t0, n, c0))
                c0 += n
            return out

        def tile_blocks(ntl_):
            blocks = []
            for nt, (kind, t0, n, c0) in enumerate(ntl_):
                for bl in range(n // 128):
                    blocks.append((nt, kind, t0 + bl * 128, c0 + bl * 128))
            return blocks

        def issue_rope(ntl_):
            for nt, (kind, t0, n, c0) in enumerate(ntl_):
                src0 = t0 if kind == 'p' else SEQ
                P.add('sp', lambda e, c0=c0, n=n, src0=src0: e.dma_start(out=ropeC[:, c0:c0 + n], in_=roped[:, 0, src0:src0 + n]),
                      writes=['rope'], dma_key='rope')
                P.add('sp', lambda e, c0=c0, n=n, src0=src0: e.dma_start(out=ropeS[:, c0:c0 + n], in_=roped[:, 1, src0:src0 + n]),
                      writes=['rope'], dma_key='rope')

        def issue_x_load(ntl_, bi):
            nt, kind, r0, col = tile_blocks(ntl_)[bi]
            src = xp if kind == 'p' else xs
            buf, key, extra = LBUF[bi]
            P.add('sp', lambda e: e.dma_start(out=buf, in_=src[r0:r0 + 128, :]), writes=[key] + extra, dma_key=key)

        for ti, tl in enumerate(tiles):
            ntl = []
            c0 = 0
            for (kind, t0, n) in tl:
                ntl.append((kind, t0, n, c0))
                c0 += n
            last_tile = (ti == n_tiles - 1)
            has_s = any(k == 's' for (k, _, _, _) in ntl)
            tok0 = ntl[0][1]
            blocks = tile_blocks(ntl)
            if ti == 0:
                issue_rope(ntl)
                for bi in range(min(2, len(blocks))):
                    issue_x_load(ntl, bi)
            P.alias(R1_KEYS_HID, R1S_KEYS[5:7])
            if ti == 0:
                for bi in range(2, min(4, len(blocks))):
                    issue_x_load(ntl, bi)
            for bi, (nt, kind, r0, col) in enumerate(blocks):
                buf, key, extra = LBUF[bi]
                for hb in range(2):
                    b = bank()
                    for kk in range(4):
                        k = hb * 4 + kk
                        P.add('pe', lambda e, b=b, kk=kk, k=k, buf=buf: e.transpose(
                            psum[b][:, kk * 128:(kk + 1) * 128], buf[:, k * 128:(k + 1) * 128], ident),
                            reads=[key, 'cst'] + extra, writes=[PS(b)])
                    dst = xT[:, hb * 4:hb * 4 + 4, col:col + 128]
                    srcv = psum[b][:, 0:512].rearrange("p (k c) -> p k c", k=4)
                    if hb == 0:
                        P.add('act', lambda e, dst=dst, srcv=srcv: e.copy(dst, srcv), reads=[PS(b)], writes=[('xT', nt, hb * 4 + i_) for i_ in range(4)])
                    else:
                        P.add('dve', lambda e, dst=dst, srcv=srcv: e.tensor_copy(dst, srcv), reads=[PS(b)], writes=[('xT', nt, hb * 4 + i_) for i_ in range(4)])
                if bi + 4 < len(blocks):
                    issue_x_load(ntl, bi + 4)

            if 'ffn1' not in skip:
                ffn(ntl, PC_F1PRE, PC_F1POST)

            if 'mixer' not in skip:
                for nt, (kind, t0, n, c0) in enumerate(ntl):
                    norm_to_hT(PC_MPRE, nt, c0, n)
                P.alias(R1_KEYS_HID, R1_KEYS_U + R1_KEYS_QO)
                if MIX_STAGE >= 3:
                    if has_s:
                        P.alias([('sq', nt, k) for nt in range(2) for k in range(KD)], ['qpad'])
                        P.add('pool', lambda e: e.memset(sq[:, 0:4096], 0.0), writes=['qpad'])
                    wv, wk = w_next()
                    wkv = wv[:, 0:4096].rearrange("p (k c) -> p k c", k=8)
                    for kc in range(2):
                        for nt, (kind, t0, n, c0) in enumerate(ntl):
                            b = bank()
                            for k in range(KD):
                                P.add('pe', lambda e, b=b, k=k, kc=kc, c0=c0, n=n, wkv=wkv: e.matmul(
                                    psum[b][:, 0:n], lhsT=wkv[:, k, kc * 128:(kc + 1) * 128], rhs=hT[:, k, c0:c0 + n],
                                    start=(k == 0), stop=(k == KD - 1)), reads=[wk, ('hT', nt, k)], writes=[PS(b)])
                            tA, tB = rope(b, n, c0)
                            kcol = 128 + c0
                            P.add('dve', lambda e, tA=tA, tB=tB, kc=kc, kcol=kcol, n=n: e.tensor_tensor(
                                kT[:, kc, kcol:kcol + n], tmp[tA][:, 0:n], tmp[tB][:, 0:n], ALU.add),
                                reads=[('tmp', tA), ('tmp', tB)], writes=[('kT', nt)])
                            if (last_tile and kind == 'p') or kind == 's':
                                if kind == 'p':
                                    P.add('pool', lambda e, tA=tA, tB=tB, kc=kc, n=n: e.tensor_tensor(
                                        kf[:, kc, 0:128], tmp[tA][:, n - 128:n], tmp[tB][:, n - 128:n], ALU.add),
                                        reads=[('tmp', tA), ('tmp', tB)], writes=['kf'])
                                else:
                                    P.add('pool', lambda e, tA=tA, tB=tB, kc=kc, n=n: e.tensor_tensor(
                                        kf[:, kc, 128:256], tmp[tA][:, 0:128], tmp[tB][:, 0:128], ALU.add),
                                        reads=[('tmp', tA), ('tmp', tB)], writes=['kf'])
                    if last_tile or has_s:
                        for part in ([0] if last_tile else []) + ([1] if has_s else []):
                            b = bank()
                            for kc in range(2):
                                P.add('pe', lambda e, b=b, kc=kc, part=part: e.transpose(
                                    psum[b][:, kc * 128:(kc + 1) * 128], kf[:, kc, part * 128:(part + 1) * 128], ident),
                                    reads=['kf', 'cst'], writes=[PS(b)])
                            P.add('act', lambda e, b=b, part=part: e.copy(stage[0][:, part * 256:(part + 1) * 256], psum[b][:, 0:256]),
                                  reads=[PS(b)], writes=[('stage', 0)])
                            if part == 0:
                                P.add('sp', lambda e: e.dma_start(out=kwp, in_=stage[0][:, 0:256]), reads=[('stage', 0)], dma_key=('stage', 0))
                            else:
                                for s in range(NS):
                                    P.add('sp', lambda e, s=s: e.dma_start(out=kws[s, 120:128, :], in_=stage[0][s * DS:(s + 1) * DS, 256:512]),
                                          reads=[('stage', 0)], dma_key=('stage', 0))
                    for nt, (kind, t0, n, c0) in enumerate(ntl):
                        for bl in range(n // 128):
                            b = bank()
                            for k in range(KD):
                                P.add('pe', lambda e, b=b, k=k, c0=c0, bl=bl, wkv=wkv: e.matmul(
                                    psum[b][:, 0:256], lhsT=hT[:, k, c0 + bl * 128:c0 + (bl + 1) * 128], rhs=wkv[:, k, 256:512],
                                    start=(k == 0), stop=(k == KD - 1)), reads=[wk, ('hT', nt, k)], writes=[PS(b)])
                            if kind == 'p':
                                slot = 1 + bl
                                for hf in range(2):
                                    dst = vpad[:, slot, :, :].rearrange("p (kc hf) c -> p kc hf c", hf=2)[:, :, hf, hf * 64:(hf + 1) * 64]
                                    srcv = psum[b][:, 0:256].rearrange("p (kc hf c) -> p kc hf c", kc=2, hf=2)[:, :, hf, :]
                                    eng = 'act' if bl % 2 == 0 else 'dve'
                                    if eng == 'act':
                                        P.add('act', lambda e, dst=dst, srcv=srcv: e.copy(dst, srcv), reads=[PS(b)], writes=[('vpad', slot)])
                                    else:
                                        P.add('dve', lambda e, dst=dst, srcv=srcv: e.tensor_copy(dst, srcv), reads=[PS(b)], writes=[('vpad', slot)])
                                if last_tile and bl == n // 128 - 1:
                                    if bl % 2 == 0:
                                        P.add('act', lambda e, b=b: e.copy(vf[0], psum[b][:, 0:256]), reads=[PS(b)], writes=[('stage', 1)])
                                    else:
                                        P.add('dve', lambda e, b=b: e.tensor_copy(vf[0], psum[b][:, 0:256]), reads=[PS(b)], writes=[('stage', 1)])
                                    P.add('sp', lambda e: e.dma_start(out=vwp, in_=vf[0]), reads=[('stage', 1)], dma_key=('stage', 1))
                            else:
                                P.add('act', lambda e, b=b: e.copy(vf[1], psum[b][:, 0:256]), reads=[PS(b)], writes=[('stage', 1)])
                                P.add('dve', lambda e: e.tensor_copy(vsb[:], vf[1]), reads=[('stage', 1)], writes=['vsb'])
                                for s in range(NS):
                                    P.add('sp', lambda e, s=s: e.dma_start(out=vws[s, 120:128, :], in_=stage[1][s * DS:(s + 1) * DS, 256:512]),
                                          reads=[('stage', 1)], dma_key=('stage', 1))
                    for c in range(KD):
                        if c % 4 == 0:
                            wv, wk = w_next()
                            wq = wv[:, 0:4096].rearrange("p (k c) -> p k c", k=8)
                        cq = c % 4
                        kc = c // 4
                        for nt, (kind, t0, n, c0) in enumerate(ntl):
                            b = bank()
                            for k in range(KD):
                                P.add('pe', lambda e, b=b, k=k, wq=wq, cq=cq, c0=c0, n=n: e.matmul(
                                    psum[b][:, 0:n], lhsT=wq[:, k, cq * 128:(cq + 1) * 128], rhs=hT[:, k, c0:c0 + n],
                                    start=(k == 0), stop=(k == KD - 1)), reads=[wk, ('hT', nt, k)], writes=[PS(b)])
                            tA, tB = rope(b, n, c0)
                            if kind == 'p':
                                P.add('dve', lambda e, tA=tA, tB=tB, c=c, c0=c0, n=n: e.tensor_tensor(
                                    qT[:, c, c0:c0 + n], tmp[tA][:, 0:n], tmp[tB][:, 0:n], ALU.add),
                                    reads=[('tmp', tA), ('tmp', tB)], writes=[('qT', nt)])
                            else:
                                for hf in range(2):
                                    pos = 2 * c + hf
                                    dst = qpad[hf * 64:(hf + 1) * 64, kc, :, pos * DS:(pos + 1) * DS]
                                    P.add('dve', lambda e, tA=tA, tB=tB, hf=hf, dst=dst, n=n: e.tensor_tensor(
                                        dst, tmp[tA][hf * 64:(hf + 1) * 64, 0:n].rearrange("p (s c) -> p s c", s=NS),
                                        tmp[tB][hf * 64:(hf + 1) * 64, 0:n].rearrange("p (s c) -> p s c", s=NS), ALU.add),
                                        reads=[('tmp', tA), ('tmp', tB)], writes=['qpad'])


                pkind, pt0, pn, pc0 = ntl[0]
                np_ = pn
                gl = {}

                def glu_part(j):
                    if j % 2 == 0:
                        wv, wk = w_next()
                        gl['wk'] = wk
                        gl['wval'] = wv[:, 0:2048].rearrange("p (k c) -> p k c", k=8)
                        gl['wgate'] = wv[:, 2048:4096].rearrange("p (k c) -> p k c", k=8)
                    wk, wval, wgate = gl['wk'], gl['wval'], gl['wgate']
                    cj = j % 2
                    jb = 0
                    us3 = us[jb].rearrange("p (s c) -> p s c", s=NS)
                    P.add('dve', lambda e: e.tensor_copy(up[jb][:, 0:HALO], uhalo[:, j, :]),
                          reads=['uhalo'], writes=[('up', jb)])
                    for nt, (kind, t0, n, c0) in enumerate(ntl):
                        bv = bank()
                        for k in range(KD):
                            P.add('pe', lambda e, b=bv, k=k, c0=c0, n=n: e.matmul(
                                psum[b][:, 0:n], lhsT=wval[:, k, cj * 128:(cj + 1) * 128], rhs=hT[:, k, c0:c0 + n],
                                start=(k == 0), stop=(k == KD - 1)), reads=[wk, ('hT', nt, k)], writes=[PS(bv)])
                        bg = bank()
                        for k in range(KD):
                            P.add('pe', lambda e, b=bg, k=k, c0=c0, n=n: e.matmul(
                                psum[b][:, 0:n], lhsT=wgate[:, k, cj * 128:(cj + 1) * 128], rhs=hT[:, k, c0:c0 + n],
                                start=(k == 0), stop=(k == KD - 1)), reads=[wk, ('hT', nt, k)], writes=[PS(bg)])
                        t = newtmp()
                        P.add('act', lambda e, b=bg, t=t, n=n: e.activation(tmp[t][:, 0:n], psum[b][:, 0:n], AF.Sigmoid),
                              reads=[PS(bg)], writes=[('tmp', t)])
                        if kind == 'p':
                            P.add('dve', lambda e, b=bv, t=t, n=n: e.tensor_tensor(
                                up[jb][:, HALO:HALO + n], psum[b][:, 0:n], tmp[t][:, 0:n], ALU.mult),
                                reads=[PS(bv), ('tmp', t)], writes=[('up', jb)])
                            P.add('dve', lambda e, n=n: e.tensor_copy(upb[:, 0:HALO + n], up[jb][:, 0:HALO + n]),
                                  reads=[('up', jb)], writes=['upb'])
                        else:
                            P.add('dve', lambda e, b=bv, t=t, n=n: e.tensor_tensor(
                                us3[:, :, HALO:HALO + DS], psum[b][:, 0:n].rearrange("p (s c) -> p s c", s=NS),
                                tmp[t][:, 0:n].rearrange("p (s c) -> p s c", s=NS), ALU.mult),
                                reads=[PS(bv), ('tmp', t)], writes=[('us', jb)])
                    P.add('pool', lambda e: e.tensor_copy(uhalo[:, j, :], up[jb][:, np_:np_ + HALO]),
                          reads=[('up', jb)], writes=['uhalo'])
                    if has_s:
                        stv = stc[:, j * 128:(j + 1) * 128].rearrange("(q r) c -> r q c", r=120)
                        P.add('sp', lambda e: e.dma_start(out=stst[jb][0:120, :, :], in_=stv),
                              writes=[('stst', 0)], dma_key=('stst', 0))
                        b = bank()
                        for q in range(4):
                            P.add('pe', lambda e, b=b, q=q: e.transpose(
                                psum[b][:, q * 120:(q + 1) * 120], stst[jb][0:120, q, :], ident[0:120, 0:120]),
                                reads=[('stst', 0), 'cst'], writes=[PS(b)])
                        P.add('act', lambda e, b=b: e.copy(us3[:, :, 0:HALO], psum[b][:, 0:480].rearrange("p (s c) -> p s c", s=NS)),
                              reads=[PS(b)], writes=[('us', jb)])

                def diag_build(j):
                    cwc = PC_CW + j * CWID
                    dg = diag[j % 2]
                    P.add(DIAG_ENG, lambda e: e.tensor_tensor(
                        dg, identb[:].unsqueeze(1).to_broadcast([128, CWID, 128]),
                        pp[:, cwc:cwc + CWID].unsqueeze(2).to_broadcast([128, CWID, 128]), ALU.mult),
                        reads=['identb', 'pp'], writes=[('diag', j % 2)])

                def conv_part(j):
                    jb = 0
                    us3 = us[jb].rearrange("p (s c) -> p s c", s=NS)
                    cwc = PC_CW + j * CWID
                    for nt, (kind, t0, n, c0) in enumerate(ntl):
                        if kind == 'p':
                            bcv = bank()
                            for t_ in range(CWID):
                                P.add('pe', lambda e, b=bcv, t_=t_, n=n: e.matmul(
                                    psum[b][:, 0:n], lhsT=diag[j % 2][:, t_, :], rhs=upb[:, t_:t_ + n],
                                    start=(t_ == 0), stop=(t_ == CWID - 1)), reads=[('diag', j % 2), 'upb'], writes=[PS(bcv)])
                            P.add('act', lambda e, b=bcv, c0=c0, n=n: e.activation(
                                yT[:, j, c0:c0 + n], psum[b][:, 0:n], AF.Identity, bias=pp[:, PC_CB + j:PC_CB + j + 1], scale=1.0),
                                reads=[PS(bcv), 'pp'], writes=[('yT', nt)])
                            if not has_s:
                                P.add('act', lambda e, c0=c0, n=n: e.copy(c2[:, j, c0:c0 + n], yT[:, j, c0:c0 + n]),
                                      reads=[('yT', nt)], writes=[('c2', nt, j)])
                                P.add('act', lambda e, c0=c0, n=n: e.activation(sq3[:, j, c0:c0 + n], yT[:, j, c0:c0 + n], AF.Square),
                                      reads=[('yT', nt)], writes=[('sq', nt, j)])
                            continue
                        P.add('dve', lambda e: e.tensor_copy(usb[:].rearrange("p (t s) -> p t s", s=NS), us3.rearrange("p s t -> p t s")),
                              reads=[('us', jb)], writes=['usb'])
                        bcs = bank()
                        for t_ in range(CWID):
                            P.add('pe', lambda e, b=bcs, t_=t_: e.matmul(
                                psum[b][:, 0:DS * NS], lhsT=diag[j % 2][:, t_, :], rhs=usb[:, t_ * NS:(t_ + DS) * NS],
                                start=(t_ == 0), stop=(t_ == CWID - 1)), reads=[('diag', j % 2), 'usb'], writes=[PS(bcs)])
                        P.add('act', lambda e, b=bcs, c0=c0, n=n: e.activation(
                            yT[:, j, c0:c0 + n].rearrange("p (s t) -> p s t", s=NS),
                            psum[b][:, 0:DS * NS].rearrange("p (t s) -> p s t", s=NS), AF.Identity,
                            bias=pp[:, PC_CB + j:PC_CB + j + 1], scale=1.0),
                            reads=[PS(bcs), 'pp'], writes=[('yT', nt)])
                    if last_tile:
                        b = bank()
                        P.add('pe', lambda e, b=b: e.transpose(psum[b][0:HALO, 0:128], up[jb][:, np_:np_ + HALO], ident),
                              reads=[('up', jb), 'cst'], writes=[PS(b)])
                        P.add('act', lambda e, b=b: e.copy(stage[0][0:HALO, j * 128:(j + 1) * 128], psum[b][0:HALO, 0:128]),
                              reads=[PS(b)], writes=[('stage', 0)])
                    if True:
                        if has_s:
                            b2 = bank()
                            tu = newtmp()
                            P.add('pool', lambda e, tu=tu: e.tensor_copy(
                                tmp[tu][:, 0:128].rearrange("p (s c) -> p s c", s=NS), us3[:, :, HALO:HALO + DS]),
                                reads=[('us', jb)], writes=[('tmp', tu)])
                            P.add('pe', lambda e, b=b2, tu=tu: e.transpose(psum[b][:, 0:128], tmp[tu][:, 0:128], ident),
                                  reads=[('tmp', tu), 'cst'], writes=[PS(b2)])
                            P.add('act', lambda e, b=b2: e.copy(stage[1][:, j * 128:(j + 1) * 128], psum[b][:, 0:128]),
                                  reads=[PS(b2)], writes=[('stage', 1)])

                def att_stage1(wi, bl, cg):
                    wa = wi % 2
                    st = stw[wa]
                    gb = (pt0 // 128) + bl
                    mk = maskP0 if gb == 0 else maskP
                    for pair in range(2):
                        bSs = [bank(), bank()]
                        for uu in range(2):
                            c = cg * 4 + 2 * pair + uu
                            for hf in range(2):
                                P.add('pe', lambda e, b=bSs[hf], hf=hf, c=c, uu=uu: e.matmul(
                                    psum[b][:, uu * 256:(uu + 1) * 256],
                                    lhsT=qT[hf * 64:(hf + 1) * 64, c, pc0 + bl * 128:pc0 + (bl + 1) * 128],
                                    rhs=kT[hf * 64:(hf + 1) * 64, cg, bl * 128:bl * 128 + 256], start=True, stop=True),
                                    reads=[('qT', 0), ('kT', 0), ('kT', 'halo')], writes=[PS(bSs[hf])])
                        for hf in range(2):
                            si = 2 * pair + hf
                            sm = smb[si]
                            P.add('dve', lambda e, b=bSs[hf], sm=sm: e.tensor_tensor(sm[:], psum[b][:, 0:512], mk, ALU.add),
                                  reads=[PS(bSs[hf]), 'cst'], writes=[('sm', si)])
                            k8 = 4 * pair + 2 * hf
                            P.add('dve', lambda e, sm=sm, k8=k8: e.tensor_reduce(
                                st[:, k8:k8 + 2], sm[:].rearrange("p (h k) -> p h k", h=2), AX.X, ALU.max),
                                reads=[('sm', si)], writes=[('stw', wa)])
                    sk = pp[:, PC_SINKW + 8 * cg:PC_SINKW + 8 * cg + 8]
                    P.add('dve', lambda e: e.scalar_tensor_tensor(st[:, 8:16], st[:, 0:8], SCALE, sk, ALU.mult, ALU.max),
                          reads=[('stw', wa), 'pp'], writes=[('stw', wa)])
                    P.add('dve', lambda e: e.tensor_scalar(st[:, 16:24], st[:, 8:16], -1.0, None, ALU.mult),
                          reads=[('stw', wa)], writes=[('stw', wa)])
                    P.add('dve', lambda e: e.tensor_tensor(st[:, 24:32], sk, st[:, 16:24], ALU.add),
                          reads=[('stw', wa), 'pp'], writes=[('stw', wa)])

                def att_stage2(wi, bl, cg):
                    wa = wi % 2
                    st = stw[wa]
                    for pair in range(2):
                        for hf in range(2):
                            si = 2 * pair + hf
                            sm = smb[si]
                            for uu in range(2):
                                k8 = 4 * pair + 2 * hf + uu
                                P.add('act', lambda e, sm=sm, uu=uu, k8=k8: e.activation(
                                    sm[:, uu * 256:(uu + 1) * 256], sm[:, uu * 256:(uu + 1) * 256], AF.Exp,
                                    bias=st[:, 16 + k8:17 + k8], scale=SCALE, accum_out=st[:, 32 + k8:33 + k8]),
                                    reads=[('sm', si), ('stw', wa)], writes=[('sm', si), ('stw', wa)])
                    P.add('act', lambda e: e.activation(st[:, 40:48], st[:, 24:32], AF.Exp),
                          reads=[('stw', wa)], writes=[('stw', wa)])

                def att_stage3(wi, bl, cg):
                    wa = wi % 2
                    st = stw[wa]
                    P.add('dve', lambda e: e.tensor_tensor(st[:, 48:56], st[:, 32:40], st[:, 40:48], ALU.add),
                          reads=[('stw', wa)], writes=[('stw', wa)])
                    P.add('dve', lambda e: e.reciprocal(st[:, 56:64], st[:, 48:56]), reads=[('stw', wa)], writes=[('stw', wa)])
                    for pair in range(2):
                        for uu in range(2):
                            u = 2 * pair + uu
                            pb = pbb[u]
                            for hf in range(2):
                                si = 2 * pair + hf
                                sm = smb[si]
                                k8 = 4 * pair + 2 * hf + uu
                                P.add('dve', lambda e, sm=sm, pb=pb, hf=hf, uu=uu, k8=k8: e.tensor_scalar(
                                    pb[:, hf * 256:(hf + 1) * 256], sm[:, uu * 256:(uu + 1) * 256],
                                    st[:, 56 + k8:57 + k8], None, ALU.mult),
                                    reads=[('sm', si), ('stw', wa)], writes=[('pb', u)])

                def att_stage45(wi, bl, cg):
                    for u in range(4):
                        c = cg * 4 + u
                        pb, pT = pbb[u], pTb[u]
                        bT = bank()
                        bTb = psum[bT][:].bitcast(BF16)
                        for m in range(4):
                            P.add('pe', lambda e, bTb=bTb, pb=pb, m=m: e.transpose(
                                bTb[:, m * 128:(m + 1) * 128], pb[:, m * 128:(m + 1) * 128], identb[:]),
                                reads=[('pb', u), 'identb'], writes=[PS(bT)])
                        P.add('dve', lambda e, bTb=bTb, pT=pT: e.tensor_copy(pT[:], bTb[:, 0:512]), reads=[PS(bT)], writes=[('pT', u)])
                    for u in range(4):
                        c = cg * 4 + u
                        pT = pTb[u]
                        bO = bank()
                        idx = 0
                        for hf in range(2):
                            for kb in range(2):
                                P.add('pe', lambda e, b=bO, hf=hf, kb=kb, pT=pT, idx=idx: e.matmul(
                                    psum[b][:, 0:128], lhsT=vpad[:, bl + kb, cg * 2 + hf, :],
                                    rhs=pT[:, (hf * 2 + kb) * 128:(hf * 2 + kb + 1) * 128], start=(idx == 0), stop=(idx == 3)),
                                    reads=[('pT', u), ('vpad', bl + kb)], writes=[PS(bO)])
                                idx += 1
                        P.add('dve', lambda e, b=bO, c=c: e.tensor_copy(oT[:, c, pc0 + bl * 128:pc0 + (bl + 1) * 128], psum[b][:, 0:128]),
                              reads=[PS(bO)], writes=[('oT', 0)])

                waves = [(bl, cg) for bl in range(pn // 128) for cg in range(2)]
                assert len(waves) == KD
                P.alias(['ostage', ('ckin', 0), ('ckin', 1), ('vcat', 0), ('vcat', 1)], [('diag', 0)])
                P.alias(SAMP_D1_KEYS, [('diag', 1)])
                diag_build(0)
                for j in range(KD):
                    bl, cg = waves[j]
                    att_stage1(j, bl, cg)
                    glu_part(j)
                    if j + 1 < KD:
                        diag_build(j + 1)
                    if j > 0:
                        att_stage45(j - 1, *waves[j - 1])
                    att_stage2(j, bl, cg)
                    conv_part(j)
                    att_stage3(j, bl, cg)
                att_stage45(KD - 1, *waves[KD - 1])
                if last_tile:
                    P.add('sp', lambda e: e.dma_start(out=csp, in_=stage[0][0:HALO, :]), reads=[('stage', 0)], dma_key=('stage', 0))
                if True:
                    if has_s:
                        for s in range(NS):
                            P.add('sp', lambda e, s=s: e.dma_start(out=css[s, 22:30, :], in_=stage[1][s * DS:(s + 1) * DS, :]),
                                  reads=[('stage', 1)], dma_key=('stage', 1))
                if not last_tile:
                    nb_ = pn // 128
                    P.add('pool', lambda e: e.tensor_copy(kT[:, :, 0:128], kT[:, :, pn:pn + 128]),
                          reads=[('kT', 0)], writes=[('kT', 'halo')])
                    P.add('pool', lambda e: e.tensor_copy(vpad[:, 0, :, :], vpad[:, nb_, :, :]),
                          reads=[('vpad', nb_)], writes=[('vpad', 0)])

                if MIX_STAGE >= 5:
                    if has_s:
                        skind, st0, sn, sc0 = ntl[1]
                        P.alias([('diag', 0)], ['ostage', ('ckin', 0), ('ckin', 1), ('vcat', 0), ('vcat', 1)])
                        P.alias([('diag', 1)], SAMP_D1_KEYS)
                        def samp_A(s):
                            a = s % 4
                            sm, pe_, pb, pT, st = smb[a], smb[a], pbb[a], pTb[a], stt_[a]
                            P.add('sp', lambda e, a=a, s=s: e.dma_start(out=ckin[a], in_=ck[s, :, :]), writes=[('ckin', a)], dma_key=('ckin', a))
                            P.add('pool', lambda e, a=a, s=s: e.dma_start(out=vcat[a], in_=cv[s, :, :]), writes=[('vcat', a)], dma_key=('vcat', a))
                            P.add('sp', lambda e, a=a, s=s: e.dma_start(out=vnew[a][0:DS, :], in_=vsb[s * DS:(s + 1) * DS, :]),
                                  reads=['vsb'], writes=[('vnew', a)], dma_key=('vnew', a))
                            b = bank()
                            for kc in range(2):
                                P.add('pe', lambda e, b=b, kc=kc, a=a: e.transpose(psum[b][:, kc * 128:(kc + 1) * 128], ckin[a][:, kc * 128:(kc + 1) * 128], ident),
                                      reads=[('ckin', a), 'cst'], writes=[PS(b)])
                            P.add('act', lambda e, b=b, a=a: e.copy(kcat[a][:, :, 0:128], psum[b][:, 0:256].rearrange("p (k c) -> p k c", k=2)),
                                  reads=[PS(b)], writes=[('kcat', a)])
                            P.add('dve', lambda e, a=a, s=s, sc0=sc0: e.tensor_copy(kcat[a][:, :, 128:136], kT[:, :, 128 + sc0 + s * DS:128 + sc0 + (s + 1) * DS]),
                                  reads=[('kT', 1)], writes=[('kcat', a)])
                            bS = bank()
                            for kc in range(2):
                                P.add('pe', lambda e, b=bS, kc=kc, a=a, s=s: e.matmul(psum[b][:, 0:136], lhsT=qpad[:, kc, s, :], rhs=kcat[a][:, kc, :],
                                                                                   start=(kc == 0), stop=(kc == 1)),
                                      reads=['qpad', ('kcat', a)], writes=[PS(bS)])
                            P.add('dve', lambda e, b=bS, sm=sm: e.tensor_tensor(sm[:, 0:136], psum[b][:, 0:136], maskS, ALU.add),
                                  reads=[PS(bS), 'cst'], writes=[('sm', a)])
                            P.add('dve', lambda e, sm=sm, st=st: e.tensor_reduce(st[:, 0:1], sm[:, 0:136], AX.X, ALU.max),
                                  reads=[('sm', a)], writes=[('st', a)])
                            sk = pp[:, PC_SINKROW:PC_SINKROW + 1]
                            P.add('dve', lambda e, st=st, sk=sk: e.scalar_tensor_tensor(st[:, 2:3], st[:, 0:1], SCALE, sk, ALU.mult, ALU.max),
                                  reads=[('st', a), 'pp'], writes=[('st', a)])
                            P.add('dve', lambda e, st=st: e.tensor_scalar(st[:, 4:5], st[:, 2:3], -1.0, None, ALU.mult),
                                  reads=[('st', a)], writes=[('st', a)])
                            P.add('dve', lambda e, st=st, sk=sk: e.tensor_tensor(st[:, 8:9], sk, st[:, 4:5], ALU.add),
                                  reads=[('st', a), 'pp'], writes=[('st', a)])
                        def samp_B(s):
                            a = s % 4
                            sm, pe_, pb, pT, st = smb[a], smb[a], pbb[a], pTb[a], stt_[a]
                            P.add('act', lambda e, sm=sm, pe_=pe_, st=st: e.activation(pe_[:, 0:136], sm[:, 0:136], AF.Exp, bias=st[:, 4:5], scale=SCALE,
                                                                                      accum_out=st[:, 6:7]),
                                  reads=[('sm', a), ('st', a)], writes=[('sm', a), ('st', a)])
                            P.add('act', lambda e, st=st: e.activation(st[:, 10:11], st[:, 8:9], AF.Exp), reads=[('st', a)], writes=[('st', a)])
                            P.add('dve', lambda e, st=st: e.tensor_tensor(st[:, 12:13], st[:, 6:7], st[:, 10:11], ALU.add),
                                  reads=[('st', a)], writes=[('st', a)])
                            P.add('dve', lambda e, st=st: e.reciprocal(st[:, 14:15], st[:, 12:13]), reads=[('st', a)], writes=[('st', a)])
                            P.add('dve', lambda e, pe_=pe_, pb=pb, st=st: e.tensor_scalar(pb[:, 0:136], pe_[:, 0:136], st[:, 14:15], None, ALU.mult),
                                  reads=[('sm', a), ('st', a)], writes=[('pb', a)])
                        def samp_C(s):
                            a = s % 4
                            sm, pe_, pb, pT, st = smb[a], smb[a], pbb[a], pTb[a], stt_[a]
                            bT = bank()
                            bTb = psum[bT][:].bitcast(BF16)
                            P.add('pe', lambda e, bTb=bTb, pb=pb: e.transpose(bTb[:, 0:128], pb[:, 0:128], identb[:]),
                                  reads=[('pb', a), 'identb'], writes=[PS(bT)])
                            P.add('pe', lambda e, bTb=bTb, pb=pb: e.transpose(bTb[0:DS, 128:256], pb[:, 128:136], identb[:]),
                                  reads=[('pb', a), 'identb'], writes=[PS(bT)])
                            P.add('act', lambda e, bTb=bTb, pT=pT: e.copy(pT[:, 0:128], bTb[:, 0:128]), reads=[PS(bT)], writes=[('pT', a)])
                            P.add('act', lambda e, bTb=bTb, a=a: e.copy(pT8[a][0:DS, :], bTb[0:DS, 128:256]), reads=[PS(bT)], writes=[('pT8', a)])
                            bO = bank()
                            for kc in range(2):
                                P.add('pe', lambda e, b=bO, kc=kc, a=a, pT=pT: e.matmul(psum[b][:, kc * 64:(kc + 1) * 64], lhsT=vcat[a][:, kc * 128:(kc + 1) * 128],
                                                                                      rhs=pT[:, kc * 64:(kc + 1) * 64], start=True, stop=False),
                                      reads=[('vcat', a), ('pT', a)], writes=[PS(bO)])
                                P.add('pe', lambda e, b=bO, kc=kc, a=a: e.matmul(psum[b][:, kc * 64:(kc + 1) * 64], lhsT=vnew[a][0:DS, kc * 128:(kc + 1) * 128],
                                                                               rhs=pT8[a][0:DS, kc * 64:(kc + 1) * 64], start=False, stop=True),
                                      reads=[('vnew', a), ('pT8', a)], writes=[PS(bO)])
                            P.add('act', lambda e, b=bO, s=s: e.copy(ostage[:, s, :], psum[b][:, 0:128]), reads=[PS(bO)], writes=['ostage'])

                        for t_ in range(NS + 2):
                            if t_ < NS:
                                samp_A(t_)
                            if 0 <= t_ - 1 < NS:
                                samp_B(t_ - 1)
                            if 0 <= t_ - 2 < NS:
                                samp_C(t_ - 2)
                        for pos in range(16):
                            c, hf = pos // 2, pos % 2
                            P.add('dve', lambda e, c=c, hf=hf, pos=pos, sc0=sc0, sn=sn: e.tensor_copy(
                                oT[hf * 64:(hf + 1) * 64, c, sc0:sc0 + sn].rearrange("p (s c) -> p s c", s=NS),
                                ostage[hf * 64:(hf + 1) * 64, :, pos * DS:(pos + 1) * DS]),
                                reads=['ostage'], writes=[('oT', 1)])
                        P.alias(['qpad'], [('sq', nt, k) for nt in range(2) for k in range(KD)])

                if MIX_STAGE >= 2:
                    for nt, (kind, t0, n, c0) in enumerate(ntl):
                        for k in range(KD if has_s else 0):
                            P.add('act', lambda e, k=k, c0=c0, n=n: e.copy(c2[:, k, c0:c0 + n], yT[:, k, c0:c0 + n]),
                                  reads=[('yT', nt)], writes=[('c2', nt, k)])
                            P.add('act', lambda e, k=k, c0=c0, n=n: e.activation(sq3[:, k, c0:c0 + n], yT[:, k, c0:c0 + n], AF.Square),
                                  reads=[('yT', nt)], writes=[('sq', nt, k)])
                        b1 = bank()
                        for k in range(KD):
                            P.add('pe', lambda e, b=b1, k=k, c0=c0, n=n: e.matmul(psum[b][:, 0:n], lhsT=ones[:], rhs=c2[:, k, c0:c0 + n],
                                                                               start=(k == 0), stop=(k == KD - 1)),
                                  reads=['ones', ('c2', nt, k)], writes=[PS(b1)])
                        b2 = bank()
                        for k in range(KD):
                            P.add('pe', lambda e, b=b2, k=k, c0=c0, n=n: e.matmul(psum[b][:, 0:n], lhsT=ones[:], rhs=sq3[:, k, c0:c0 + n],
                                                                               start=(k == 0), stop=(k == KD - 1)),
                                  reads=['ones', ('sq', nt, k)], writes=[PS(b2)])
                        tm = newtmp()
                        P.add('dve', lambda e, b=b1, tm=tm, n=n: e.tensor_scalar(tmp[tm][:, 0:n], psum[b][:, 0:n], 1.0 / D, None, ALU.mult),
                              reads=[PS(b1)], writes=[('tmp', tm)])
                        tv = newtmp()
                        P.add('dve', lambda e, tm=tm, tv=tv, n=n: e.tensor_tensor(tmp[tv][:, 0:n], tmp[tm][:, 0:n], tmp[tm][:, 0:n], ALU.mult),
                              reads=[('tmp', tm)], writes=[('tmp', tv)])
                        P.add('dve', lambda e, b=b2, tv=tv, n=n: e.scalar_tensor_tensor(tmp[tv][:, 0:n], psum[b][:, 0:n], 1.0 / D, tmp[tv][:, 0:n],
                                                                                       ALU.mult, ALU.subtract),
                              reads=[PS(b2), ('tmp', tv)], writes=[('tmp', tv)])
                        r = newrs()
                        P.add('act', lambda e, tv=tv, r=r, n=n: e.activation(rsb[r][:, 0:n], tmp[tv][:, 0:n], AF.Ln, bias=eps_ap, scale=1.0),
                              reads=[('tmp', tv), 'pp'], writes=[('rs', r)])
                        P.add('act', lambda e, r=r, n=n: e.activation(rsb[r][:, 0:n], rsb[r][:, 0:n], AF.Exp, scale=-0.5),
                              reads=[('rs', r)], writes=[('rs', r)])
                        P.add('dve', lambda e, tm=tm, r=r, n=n: e.scalar_tensor_tensor(tmp[tm][:, 0:n], tmp[tm][:, 0:n], -1.0, rsb[r][:, 0:n],
                                                                                      ALU.mult, ALU.mult),
                              reads=[('tmp', tm), ('rs', r)], writes=[('tmp', tm)])
                        for k in range(KD):
                            t1 = newtmp()
                            while t1 in (tm, tv):
                                t1 = newtmp()
                            P.add('dve', lambda e, k=k, t1=t1, r=r, c0=c0, n=n: e.scalar_tensor_tensor(
                                tmp[t1][:, 0:n], yT[:, k, c0:c0 + n], pp[:, PC_LNG + k:PC_LNG + k + 1], rsb[r][:, 0:n], ALU.mult, ALU.mult),
                                reads=[('yT', nt), 'pp', ('rs', r)], writes=[('tmp', t1)])
                            P.add('dve', lambda e, k=k, t1=t1, tm=tm, n=n: e.scalar_tensor_tensor(
                                tmp[t1][:, 0:n], tmp[tm][:, 0:n], pp[:, PC_LNG + k:PC_LNG + k + 1], tmp[t1][:, 0:n], ALU.mult, ALU.add),
                                reads=[('tmp', tm), 'pp', ('tmp', t1)], writes=[('tmp', t1)])
                            P.add('act', lambda e, k=k, t1=t1, c0=c0, n=n: e.activation(
                                c2[:, k, c0:c0 + n], tmp[t1][:, 0:n], AF.Silu, bias=pp[:, PC_LNB + k:PC_LNB + k + 1], scale=1.0),
                                reads=[('tmp', t1), 'pp'], writes=[('c2', nt, k)])

                P.alias(R1_KEYS_U, R1_KEYS_M)
                if MIX_STAGE >= 6:
                    for g in range(4):
                        wvX, wkX = w_next()
                        wvY, wkY = w_next(keep_prev=True)
                        wc_ = wvX[:, 0:2048].rearrange("p (k c) -> p k c", k=8)
                        wa_ = wvX[:, 2048:4096].rearrange("p (k c) -> p k c", k=8)
                        wgc_ = wvY[:, 0:2048].rearrange("p (k c) -> p k c", k=8)
                        wga_ = wvY[:, 2048:4096].rearrange("p (k c) -> p k c", k=8)
                        for cc in range(2):
                            c = g * 2 + cc
                            for nt, (kind, t0, n, c0) in enumerate(ntl):
                                bs_ = []
                                for (wmat, wkey, src, skey) in ((wc_, wkX, c2, ('c2', nt)), (wa_, wkX, oT, ('oT', nt)),
                                                                (wgc_, wkY, hT, ('hT', nt)), (wga_, wkY, hT, ('hT', nt))):
                                    b = bank()
                                    bs_.append(b)
                                    for k in range(KD):
                                        P.add('pe', lambda e, b=b, k=k, wmat=wmat, src=src, cc=cc, c0=c0, n=n: e.matmul(
                                            psum[b][:, 0:n], lhsT=wmat[:, k, cc * 128:(cc + 1) * 128], rhs=src[:, k, c0:c0 + n],
                                            start=(k == 0), stop=(k == KD - 1)), reads=[wkey, (skey + (k,)) if skey[0] in ('hT', 'c2') else skey], writes=[PS(b)])
                                bA, bB, bC, bD = bs_
                                t1 = newtmp()
                                t2 = newtmp()
                                P.add('act', lambda e, b=bC, t1=t1, n=n: e.activation(tmp[t1][:, 0:n], psum[b][:, 0:n], AF.Sigmoid),
                                      reads=[PS(bC)], writes=[('tmp', t1)])
                                P.add('act', lambda e, b=bD, t2=t2, n=n: e.activation(tmp[t2][:, 0:n], psum[b][:, 0:n], AF.Sigmoid),
                                      reads=[PS(bD)], writes=[('tmp', t2)])
                                P.add('dve', lambda e, b=bA, t1=t1, n=n: e.tensor_tensor(tmp[t1][:, 0:n], psum[b][:, 0:n], tmp[t1][:, 0:n], ALU.mult),
                                      reads=[PS(bA), ('tmp', t1)], writes=[('tmp', t1)])
                                P.add('dve', lambda e, b=bB, t2=t2, n=n: e.tensor_tensor(tmp[t2][:, 0:n], psum[b][:, 0:n], tmp[t2][:, 0:n], ALU.mult),
                                      reads=[PS(bB), ('tmp', t2)], writes=[('tmp', t2)])
                                P.add('dve', lambda e, t1=t1, t2=t2, c=c, c0=c0, n=n: e.tensor_tensor(mT[:, c, c0:c0 + n], tmp[t1][:, 0:n], tmp[t2][:, 0:n], ALU.add),
                                      reads=[('tmp', t1), ('tmp', t2)], writes=[('mT', nt)])
                if MIX_STAGE >= 7:
                    for c in range(KD):
                        if c % 4 == 0:
                            wv, wk = w_next()
                            wo_ = wv[:, 0:4096].rearrange("p (k c) -> p k c", k=8)
                        cq = c % 4
                        for nt, (kind, t0, n, c0) in enumerate(ntl):
                            b = bank()
                            for k in range(KD):
                                P.add('pe', lambda e, b=b, k=k, wo_=wo_, cq=cq, c0=c0, n=n: e.matmul(
                                    psum[b][:, 0:n], lhsT=wo_[:, k, cq * 128:(cq + 1) * 128], rhs=mT[:, k, c0:c0 + n],
                                    start=(k == 0), stop=(k == KD - 1)), reads=[wk, ('mT', nt)], writes=[PS(b)])
                            P.add('dve', lambda e, b=b, c=c, c0=c0, n=n: e.tensor_copy(yT[:, c, c0:c0 + n], psum[b][:, 0:n]),
                                  reads=[PS(b)], writes=[('yT', nt)])
                            P.add('act', lambda e, c=c, c0=c0, n=n: e.activation(sq3[:, c, c0:c0 + n], yT[:, c, c0:c0 + n], AF.Square),
                                  reads=[('yT', nt)], writes=[('sq', nt, c)])
                    for nt, (kind, t0, n, c0) in enumerate(ntl):
                        post_norm_residual(PC_MPOST, nt, c0, n, False)

            if ti + 1 < n_tiles:
                nxt = tile_ntl(tiles[ti + 1])
                issue_rope(nxt)
                for bi in range(min(4, len(tile_blocks(nxt)))):
                    issue_x_load(nxt, bi)
            if 'ffn2' not in skip:
                ffn(ntl, PC_F2PRE, PC_F2POST)

            P.alias(R1_KEYS_HID, R1S_KEYS[0:5])
            for bi, (nt, kind, r0, col) in enumerate(blocks):
                dstd = yp if kind == 'p' else ys
                sbuf_, skey = r1s[bi], ('r1s', bi)
                for hb in range(2):
                    b = bank()
                    for kk in range(4):
                        k = hb * 4 + kk
                        P.add('pe', lambda e, b=b, kk=kk, k=k, col=col: e.transpose(
                            psum[b][:, kk * 128:(kk + 1) * 128], xT[:, k, col:col + 128], ident),
                            reads=[('xT', nt, k), 'cst'], writes=[PS(b)])
                    if hb == 0:
                        P.add('act', lambda e, b=b, sbuf_=sbuf_: e.copy(sbuf_[:, 0:512], psum[b][:, 0:512]),
                              reads=[PS(b)], writes=[skey])
                    else:
                        P.add('dve', lambda e, b=b, sbuf_=sbuf_: e.tensor_copy(sbuf_[:, 512:1024], psum[b][:, 0:512]),
                              reads=[PS(b)], writes=[skey])
                P.add('sp', lambda e, sbuf_=sbuf_, dstd=dstd, r0=r0: e.dma_start(out=dstd[r0:r0 + 128, :], in_=sbuf_),
                      reads=[skey], dma_key=skey)

        P.emit(nc)
    return nc


def _fm(v):
    return np.ascontiguousarray(np.asarray(v, np.float32).reshape(KD, 128).T)


def _constants():
    cst = np.zeros((128, NCC), np.float32)
    cst[:, CC_ID:CC_ID + 128] = np.eye(128, dtype=np.float32)
    for m in range(128):
        d = m % 64
        k = m + 8 if d < 8 else (m - 8 if d < 16 else m)
        cst[k, CC_PERM + m] = 1.0
    i = np.arange(128)[:, None]
    j = np.arange(256)[None, :]
    vis = (j > i) & (j <= i + 128)
    mp = np.where(vis, 0.0, NEG).astype(np.float32)
    cst[:, CC_MP:CC_MP + 256] = mp
    cst[:, CC_MP + 256:CC_MP + 512] = mp
    mp0 = mp.copy()
    mp0[:, 0:128] = NEG
    cst[:, CC_MP0:CC_MP0 + 256] = mp0
    cst[:, CC_MP0 + 256:CC_MP0 + 512] = mp0
    jq = (np.arange(128) % DS)[:, None]
    key = np.arange(136)[None, :]
    vis_s = np.where(key < 128, key > jq, (key - 128) <= jq)
    cst[:, CC_MS:CC_MS + 136] = np.where(vis_s, 0.0, NEG).astype(np.float32)
    inv = np.exp(-math.log(500000.0) * np.arange(0, 16, 2, dtype=np.float32) / np.float32(16)).astype(np.float32)
    pos = np.concatenate([np.arange(SEQ), np.tile(PAST + np.arange(DS), NS)]).astype(np.float32)
    ang = (pos[:, None] * inv[None, :]).astype(np.float32)
    cos = np.cos(ang).astype(np.float32)
    sin = np.sin(ang).astype(np.float32)
    rope = np.zeros((128, 2, SEQ + NS * DS), np.float32)
    rope[:, 0, :] = 1.0
    for p in range(128):
        d = p % 64
        if d < 8:
            rope[p, 0] = cos[:, d]
            rope[p, 1] = -sin[:, d]
        elif d < 16:
            rope[p, 0] = cos[:, d - 8]
            rope[p, 1] = sin[:, d - 8]
    return cst, rope


_CACHE = {}


def kernel(x_prompt, x_sample, state_conv, cache_k_win, cache_v_win,
           ffn1_pre_g, ffn1_w_up, ffn1_w_down, ffn1_post_g,
           mix_pre_g, w_in, conv_dw_w, conv_dw_b, conv_ln_g, conv_ln_b, w_conv_out,
           attn_sinks, w_attn_out, w_out, mix_post_g,
           ffn2_pre_g, ffn2_w_up, ffn2_w_down, ffn2_post_g):
    f = lambda a: np.ascontiguousarray(np.asarray(a, dtype=np.float32))
    n_cores = 8
    if 'nc' not in _CACHE:
        _CACHE['nc'] = build_program()
        _CACHE['cst'] = _constants()
    nc = _CACHE['nc']
    cst, rope = _CACHE['cst']

    win_ = f(w_in[0]).copy()
    qcols = np.concatenate([np.arange(2048 + h * 64, 2048 + (h + 1) * 64) for h in HP])
    win_[:, 2048:3072] = win_[:, qcols]
    wao_ = f(w_attn_out[0])
    arows = np.concatenate([np.arange(h * 64, (h + 1) * 64) for h in HP])
    wao_ = np.ascontiguousarray(wao_[arows, :])
    pp = np.zeros((128, NPP), np.float32)
    for col, v in ((PC_F1PRE, ffn1_pre_g), (PC_F1POST, ffn1_post_g), (PC_MPRE, mix_pre_g), (PC_MPOST, mix_post_g),
                   (PC_F2PRE, ffn2_pre_g), (PC_F2POST, ffn2_post_g), (PC_LNG, conv_ln_g), (PC_LNB, conv_ln_b),
                   (PC_CB, conv_dw_b)):
        pp[:, col:col + 8] = _fm(f(v)[0])
    cw = f(conv_dw_w[0])
    pp[:, PC_CW:PC_CW + 8 * CWID] = cw.reshape(CWID, KD, 128).transpose(2, 1, 0).reshape(128, KD * CWID)
    sinks = f(attn_sinks[0])
    sp = sinks[np.array(HP)]
    pp[:, PC_SINKROW] = np.repeat(sp, DS)
    pp[:, PC_SINKBC:PC_SINKBC + 16] = sp[None, :]
    for cg in range(2):
        for pair in range(2):
            for hf in range(2):
                for uu in range(2):
                    pos = 2 * (cg * 4 + 2 * pair + uu) + hf
                    pp[:, PC_SINKW + 8 * cg + 4 * pair + 2 * hf + uu] = sp[pos]
    pp[:, PC_EPS] = EPS
    pp[:, PC_EPS4] = 4 * EPS

    wsrc = dict(f1u=f(ffn1_w_up[0]), f1d=f(ffn1_w_down[0]), win=win_, wco=f(w_conv_out[0]), wao=wao_, wo=f(w_out[0]),
                f2u=f(ffn2_w_up[0]), f2d=f(ffn2_w_down[0]))
    units = weight_plan()
    wall = np.zeros((len(units), 128, 4096), np.float32)
    for u, parts in enumerate(units):
        for (a, kk, cc, nm, r0, c0) in parts:
            blk = wsrc[nm][r0:r0 + kk * 128, c0:c0 + cc].reshape(kk, 128, cc).transpose(1, 0, 2).reshape(128, kk * cc)
            wall[u, :, a:a + kk * cc] = blk
    shared = dict(wall=wall, pp=pp, cst=cst, rope=rope)
    xpa, xsa = f(x_prompt), f(x_sample)
    sca, cka, cva = f(state_conv[0]), f(cache_k_win[0]), f(cache_v_win[0])
    in_maps = []
    for i in range(n_cores):
        m = dict(shared)
        m['xp'] = xpa[i]
        m['xs'] = xsa[NS * i:NS * (i + 1)].reshape(NS * DS, D)
        m['stc'] = sca[NS * i:NS * (i + 1)].reshape(NS * HALO, D)
        m['ck'] = cka[NS * i:NS * (i + 1)].reshape(NS, 128, 256)
        m['cv'] = cva[NS * i:NS * (i + 1)].reshape(NS, 128, 256)
        in_maps.append(m)
    if _CACHE.get('dry'):
        return nc, in_maps
    res = run_bass_kernel_spmd(nc, in_maps, core_ids=list(range(n_cores)))
    R = res.results
    y_prompt = np.stack([R[i]['yp'] for i in range(n_cores)]).astype(np.float32)
    y_sample = np.concatenate([R[i]['ys'].reshape(NS, DS, D) for i in range(n_cores)]).astype(np.float32)
    conv_p = np.stack([R[i]['csp'] for i in range(n_cores)])[None].astype(np.float32)
    k_p = np.stack([R[i]['kwp'].reshape(128, 4, 64) for i in range(n_cores)])[None].astype(np.float32)
    v_p = np.stack([R[i]['vwp'].reshape(128, 4, 64) for i in range(n_cores)])[None].astype(np.float32)
    conv_s = np.concatenate([R[i]['css'] for i in range(n_cores)])[None].astype(np.float32)
    k_s = np.concatenate([R[i]['kws'].reshape(NS, 128, 4, 64) for i in range(n_cores)])[None].astype(np.float32)
    v_s = np.concatenate([R[i]['vws'].reshape(NS, 128, 4, 64) for i in range(n_cores)])[None].astype(np.float32)
    return (y_prompt, y_sample, conv_p, k_p, v_p, conv_s, k_s, v_s)
```

```python
import math
import numpy as np
from contextlib import ExitStack
import concourse.bass as bass
import concourse.mybir as mybir
from concourse.bass_utils import run_bass_kernel_spmd

F32 = mybir.dt.float32
BF16 = mybir.dt.bfloat16
ALU = mybir.AluOpType
AF = mybir.ActivationFunctionType
AX = mybir.AxisListType

COMPUTE = ('pe', 'act', 'dve', 'pool')


class Op:
    __slots__ = ('eng', 'fn', 'deps', 'dma_key', 'dma_cnt', 'needed', 'ms', 'is_dma')

    def __init__(self, eng, fn, dma_key=None):
        self.eng = eng
        self.fn = fn
        self.deps = []
        self.dma_key = dma_key
        self.is_dma = dma_key is not None
        self.dma_cnt = 0
        self.needed = False
        self.ms = 0


class Prog:
    def __init__(self):
        self.ops = {e: [] for e in ('pe', 'act', 'dve', 'pool', 'sp')}
        self.last_writer = {}
        self.readers = {}
        self.dma_count = {}

    def add(self, eng, fn, reads=(), writes=(), dma_key=None):
        op = Op(eng, fn, dma_key)
        deps = {}
        for b in reads:
            w = self.last_writer.get(b)
            if w is not None:
                deps[id(w)] = (w, True)
        for b in writes:
            w = self.last_writer.get(b)
            if w is not None and id(w) not in deps:
                deps[id(w)] = (w, False)
            for r in self.readers.get(b, ()):
                if id(r) not in deps:
                    deps[id(r)] = (r, False)
        for d, raw in deps.values():
            if (not d.is_dma) and (not op.is_dma) and d.eng == op.eng:
                if op.eng == 'pe':
                    continue
            op.deps.append(d)
            d.needed = True
        if op.is_dma:
            c = self.dma_count.get(dma_key, 0) + 1
            self.dma_count[dma_key] = c
            op.dma_cnt = c
        for b in writes:
            self.last_writer[b] = op
            self.readers[b] = []
        for b in reads:
            if b in writes:
                continue
            self.readers.setdefault(b, []).append(op)
        self.ops[eng].append(op)
        return op

    def alias(self, old_keys, new_keys):
        pend = []
        seen = set()
        for k in old_keys:
            w = self.last_writer.get(k)
            if w is not None and id(w) not in seen:
                seen.add(id(w))
                pend.append(w)
            for r in self.readers.get(k, ()):
                if id(r) not in seen:
                    seen.add(id(r))
                    pend.append(r)
        for k in new_keys:
            extra = []
            w = self.last_writer.get(k)
            if w is not None and id(w) not in seen:
                extra.append(w)
            for r in self.readers.get(k, ()):
                if id(r) not in seen:
                    extra.append(r)
            self.last_writer[k] = None
            self.readers[k] = list(pend) + extra

    def emit(self, nc):
        with ExitStack() as es:
            esem = {e: es.enter_context(nc.semaphore('ms_' + e)) for e in COMPUTE}
            dsem = {}
            for i, k in enumerate(self.dma_count):
                dsem[k] = es.enter_context(nc.semaphore('dq%d' % i))
            for e in COMPUTE:
                m = 0
                for op in self.ops[e]:
                    if (not op.is_dma) and op.needed:
                        m += 1
                        op.ms = m
            block = es.enter_context(nc.Block())

            def run(eng_name, eng):
                seen = {}
                for op in self.ops[eng_name]:
                    need = {}
                    for d in op.deps:
                        if d.is_dma:
                            s, v = dsem[d.dma_key], 16 * d.dma_cnt
                        else:
                            s, v = esem[d.eng], d.ms
                        key = id(s)
                        if key not in need or need[key][1] < v:
                            need[key] = (s, v)
                    for key, (s, v) in need.items():
                        if seen.get(key, 0) >= v:
                            continue
                        seen[key] = v
                        eng.wait_ge(s, v)
                    ins = op.fn(eng)
                    if op.is_dma:
                        ins.then_inc(dsem[op.dma_key], 16)
                    elif op.needed:
                        ins.then_inc(esem[eng_name], 1)
                if eng_name == 'sp':
                    for k, c in self.dma_count.items():
                        eng.wait_ge(dsem[k], 16 * c)

            @block.tensor
            def _(e):
                run('pe', e)

            @block.scalar
            def _(e):
                run('act', e)

            @block.vector
            def _(e):
                run('dve', e)

            @block.gpsimd
            def _(e):
                run('pool', e)

            @block.sync
            def _(e):
                run('sp', e)


D = 1024
DFF = 2816
KD = 8
KF = 22
SEQ = 2048
NS = 16
DS = 8
PAST = 8192
CWID = 31
HALO = 30
EPS = 1e-6
SCALE = 0.125
NEG = -1e30
W = 640
HP = [0, 4, 1, 5, 2, 6, 3, 7, 8, 12, 9, 13, 10, 14, 11, 15]

PC_F1PRE, PC_F1POST, PC_MPRE, PC_MPOST, PC_F2PRE, PC_F2POST, PC_LNG, PC_LNB, PC_CB = [8 * i for i in range(9)]
PC_CW = 72
PC_SINKROW = PC_CW + 8 * CWID
PC_SINKBC = PC_SINKROW + 1
PC_EPS = PC_SINKBC + 16
PC_EPS4 = PC_EPS + 1
PC_SINKW = 340
NPP = 356
CC_ID, CC_PERM, CC_MP, CC_MP0, CC_MS = 0, 128, 256, 768, 1280
NCC = 1280 + 136

FFN_STAGE = 3
USE_WSCR = True
DIAG_ENG = 'pool'
MIX_STAGE = 7
ATT_STAGE = 6
TILES = [
    [('p', 0, 512), ('s', 0, 128)],
    [('p', 512, 512)],
    [('p', 1024, 512)],
    [('p', 1536, 512)],
]


def weight_plan():
    units = []

    def ffn_units(up, dn):
        for g in range(11):
            units.append([(0, 8, 256, up, 0, g * 256), (2048, 8, 256, up, 0, DFF + g * 256)])
        for g in range(8):
            units.append([(0, 11, 128, dn, 0, g * 128), (1408, 11, 128, dn, 1408, g * 128)])
    ffn_units('f1u', 'f1d')
    units.append([(0, 8, 512, 'win', 0, 3072)])
    units.append([(0, 8, 512, 'win', 0, 2048)])
    units.append([(0, 8, 512, 'win', 0, 2560)])
    for g in range(4):
        units.append([(0, 8, 256, 'win', 0, g * 256), (2048, 8, 256, 'win', 0, 1024 + g * 256)])
    for g in range(4):
        units.append([(0, 8, 256, 'wco', 0, g * 256), (2048, 8, 256, 'wao', 0, g * 256)])
        units.append([(0, 8, 256, 'win', 0, 3584 + g * 256), (2048, 8, 256, 'win', 0, 4608 + g * 256)])
    for g in range(2):
        units.append([(0, 8, 512, 'wo', 0, g * 512)])
    ffn_units('f2u', 'f2d')
    return units


def build_program(tiles=TILES, skip=()):
    nc = bass.Bass("TRN2", target_bir_lowering=False)
    P = Prog()

    def din(name, shape):
        return nc.dram_tensor(name, shape, F32, kind="ExternalInput").ap()

    def dout(name, shape):
        return nc.dram_tensor(name, shape, F32, kind="ExternalOutput").ap()

    xp = din("xp", [SEQ, D])
    xs = din("xs", [NS * DS, D])
    stc = din("stc", [NS * HALO, D])
    ck = din("ck", [NS, 128, 256])
    cv = din("cv", [NS, 128, 256])
    UNITS = weight_plan()
    wall = din("wall", [len(UNITS), 128, 4096])
    ppd = din("pp", [128, NPP])
    cstd = din("cst", [128, NCC])
    roped = din("rope", [128, 2, SEQ + NS * DS])

    yp = dout("yp", [SEQ, D])
    ys = dout("ys", [NS * DS, D])
    csp = dout("csp", [HALO, D])
    kwp = dout("kwp", [128, 256])
    vwp = dout("vwp", [128, 256])
    css = dout("css", [NS, HALO, D])
    kws = dout("kws", [NS, 128, 256])
    vws = dout("vws", [NS, 128, 256])

    with ExitStack() as es:
        def sb(name, shape, dt):
            return es.enter_context(nc.sbuf_tensor(name, shape, dt))

        xT = sb("xT", [128, KD, W], F32)
        hT = sb("hT", [128, KD, W], BF16)
        yT = sb("yT", [128, KD, W], F32)
        sq = sb("sq", [128, KD * W], BF16)
        c2 = sb("c2", [128, KD, W], BF16)
        R1 = sb("R1", [128, 7680], F32)
        kT = sb("kT", [128, 2, 128 + W], BF16)
        vpad = sb("vpad", [128, 6, 4, 128], BF16)
        ropeC = sb("ropeC", [128, W], F32)
        ropeS = sb("ropeS", [128, W], F32)
        wsl = [sb("wsl%d" % i, [128, 4096], BF16) for i in range(3)]
        pp = sb("pp_sb", [128, NPP], F32)
        cst = sb("cst_sb", [128, NCC], F32)
        identb = sb("identb", [128, 128], BF16)
        ones = sb("ones", [128, 128], BF16)
        stage = [sb("stage%d" % i, [128, D], F32) for i in range(2)]
        rsb = [sb("rs%d" % i, [128, 512], F32) for i in range(2)]
        tmp = [sb("tmp%d" % i, [128, 512], F32) for i in range(4)]
        uhalo = sb("uhalo", [128, KD, HALO], F32)
        usb = sb("usb", [128, (HALO + DS) * NS], BF16)
        diagt = [sb("diag%d" % i, [128, CWID * 128], BF16) for i in range(2)]
        diag = [d[:].rearrange("p (t m) -> p t m", t=CWID) for d in diagt]
        stst = [sb("stst0", [128, 4, 128], F32)] * 2
        smb = [sb("sm%d" % i, [128, 512], F32) for i in range(4)]
        pbb = [sb("pb%d" % i, [128, 512], BF16) for i in range(4)]
        pTb = [sb("pT%d" % i, [128, 512], BF16) for i in range(4)]
        stw = [sb("stw%d" % i, [128, 64], F32) for i in range(2)]
        pT8 = [sb("pT8%d" % i, [128, 128], BF16) for i in range(2)]
        stt_ = [sb("st%d" % i, [128, 16], F32) for i in range(4)]
        ckin = [diagt[0][:, 2048 + i * 512:2048 + (i + 1) * 512].bitcast(F32) for i in range(2)]
        kcat = [sb("kcat%d" % i, [128, 2, 136], BF16) for i in range(2)]
        vcat = [diagt[0][:, 3072 + i * 256:3072 + (i + 1) * 256] for i in range(2)]
        vnew = [sb("vnew%d" % i, [128, 256], BF16) for i in range(2)]
        ckin += [diagt[1][:, i * 512:(i + 1) * 512].bitcast(F32) for i in range(2)]
        vcat += [diagt[1][:, 1024 + i * 256:1024 + (i + 1) * 256] for i in range(2)]
        kcat = [kcat[0][:], kcat[1][:]] + [diagt[1][:, 1536 + i * 272:1536 + (i + 1) * 272].rearrange("p (k c) -> p k c", k=2) for i in range(2)]
        vnew = [vnew[0][:], vnew[1][:]] + [diagt[1][:, 2080 + i * 256:2080 + (i + 1) * 256] for i in range(2)]
        pT8 = [pT8[0][:], pT8[1][:]] + [diagt[1][:, 2592 + i * 128:2592 + (i + 1) * 128] for i in range(2)]
        SAMP_D1_KEYS = [(nm, i) for nm in ('ckin', 'vcat', 'kcat', 'vnew', 'pT8') for i in (2, 3)]
        ostage = diagt[0][:, 0:2048].rearrange("p (s c) -> p s c", s=NS)
        kf = sb("kf", [128, 2, 256], F32)
        vf = [stage[1][:, i * 256:(i + 1) * 256] for i in range(2)]
        vsb = sb("vsb", [128, 256], BF16)
        psum = [es.enter_context(nc.psum_tensor("ps%d" % i, [128, 512], F32)) for i in range(8)]

        sq3 = sq[:].rearrange("p (k w) -> p k w", k=KD)
        qpad = sq[:, 0:4096].rearrange("p (a s c) -> p a s c", a=2, s=NS)
        R1b = R1[:].bitcast(BF16)
        hid = R1b[:, 0:KF * W].rearrange("p (k w) -> p k w", k=KF)
        qT = R1b[:, 0:KD * W].rearrange("p (k w) -> p k w", k=KD)
        oT = R1b[:, KD * W:2 * KD * W].rearrange("p (k w) -> p k w", k=KD)
        mT = R1b[:, 2 * KD * W:3 * KD * W].rearrange("p (k w) -> p k w", k=KD)
        up = [R1[:, 5120:5662]] * 2
        us = [R1[:, 5664:6272]] * 2
        upb = R1b[:, 12544:13088]
        ident = cst[:, CC_ID:CC_ID + 128]
        permT = cst[:, CC_PERM:CC_PERM + 128]
        maskP = cst[:, CC_MP:CC_MP + 512]
        maskP0 = cst[:, CC_MP0:CC_MP0 + 512]
        maskS = cst[:, CC_MS:CC_MS + 136]
        eps_ap = pp[:, PC_EPS:PC_EPS + 1]
        eps4_ap = pp[:, PC_EPS4:PC_EPS4 + 1]

        R1_KEYS_HID = [('hid', nt) for nt in range(2)]
        R1_KEYS_U = [('up', 0), ('us', 0), 'upb']
        R1_KEYS_QO = [('qT', nt) for nt in range(2)] + [('oT', nt) for nt in range(2)]
        R1_KEYS_M = [('mT', nt) for nt in range(2)]
        R1_KEYS_QOM = R1_KEYS_QO + R1_KEYS_M

        state = {'bank': 0, 'tmp': 0, 'rs': 0}

        def bank():
            b = state['bank']
            state['bank'] = (b + 1) % 8
            return b

        def newtmp():
            t = state['tmp']
            state['tmp'] = (t + 1) % 4
            return t

        def newrs():
            t = state['rs']
            state['rs'] = (t + 1) % 2
            return t

        def PS(b):
            return ('ps', b)

        upt = len(UNITS)
        wunits = list(range(upt)) * len(tiles)
        UEXT = [max(a + kk * cc for (a, kk, cc, nm, r0, c0) in u_) for u_ in UNITS]
        wstate = {'issued': 0, 'used': 0, 'done': -1}
        PF = 2

        wscr = nc.dram_tensor("wscr", [upt, 128, 4096], BF16).ap() if (USE_WSCR and len(tiles) > 1) else None

        def w_issue(i):
            slot = i % 3
            u = i % upt
            ext = UEXT[u]
            if wscr is not None and i >= upt:
                P.add('pool', lambda e, slot=slot, u=u, ext=ext: e.dma_start(out=wsl[slot][:, 0:ext], in_=wscr[u, :, 0:ext]),
                      reads=[('wscr', u)], writes=[('w', slot)], dma_key=('w', slot))
                return
            for off in range(0, ext, 2048):
                nn = min(2048, ext - off)
                P.add('pool', lambda e, slot=slot, u=u, off=off, nn=nn: e.dma_start(out=wsl[slot][:, off:off + nn], in_=wall[u, :, off:off + nn]),
                      writes=[('w', slot)], dma_key=('w', slot))
            if wscr is not None:
                P.add('sp', lambda e, slot=slot, u=u, ext=ext: e.dma_start(out=wscr[u, :, 0:ext], in_=wsl[slot][:, 0:ext]),
                      reads=[('w', slot)], writes=[('wscr', u)], dma_key=('wst', slot))

        def w_next(keep_prev=False):
            i = wstate['used']
            wstate['used'] = i + 1
            if not keep_prev:
                wstate['done'] = i - 1
            while (wstate['issued'] < len(wunits) and wstate['issued'] <= i + PF
                   and (wstate['issued'] - 3 <= wstate['done'] or wstate['issued'] <= i)):
                assert wstate['issued'] - 3 <= wstate['done'], "weight ring too small"
                w_issue(wstate['issued'])
                wstate['issued'] += 1
            slot = i % 3
            return wsl[slot], ('w', slot)

        P.add('sp', lambda e: e.dma_start(out=pp[:], in_=ppd), writes=['pp'], dma_key='pp')
        P.add('sp', lambda e: e.dma_start(out=cst[:], in_=cstd), writes=['cst'], dma_key='cst')
        P.add('dve', lambda e: e.memset(ones[:], 1.0), writes=['ones'])
        P.add('dve', lambda e: e.tensor_copy(identb[:], ident), reads=['cst'], writes=['identb'])
        P.add('dve', lambda e: e.memset(vpad[:], 0.0), writes=[('vpad', s) for s in range(6)])
        P.add('dve', lambda e: e.memset(kT[:, :, 0:128], 0.0), writes=[('kT', 'halo')])
        P.add('dve', lambda e: e.memset(uhalo[:], 0.0), writes=['uhalo'])
        stc3 = stc.rearrange("(s r) c -> s r c", r=HALO)

        def passthrough_dmas():
            P.add('sp', lambda e: e.dma_start(out=kws[:, 0:120, :], in_=ck[:, 8:128, :]), dma_key='d2d')
            P.add('sp', lambda e: e.dma_start(out=vws[:, 0:120, :], in_=cv[:, 8:128, :]), dma_key='d2d')
            P.add('sp', lambda e: e.dma_start(out=css[:, 0:22, :], in_=stc3[:, 8:30, :]), dma_key='d2d')

        def stats_rs(src_keys, nt, c0, n, scale, eps_t):
            b = bank()
            for k in range(KD):
                P.add('pe', lambda e, b=b, k=k: e.matmul(psum[b][:, 0:n], lhsT=ones[:], rhs=sq3[:, k, c0:c0 + n],
                                                         start=(k == 0), stop=(k == KD - 1)),
                      reads=['ones', ('sq', nt, k)], writes=[PS(b)])
            r = newrs()
            P.add('act', lambda e, b=b, r=r: e.activation(rsb[r][:, 0:n], psum[b][:, 0:n], AF.Ln, bias=eps_t, scale=scale),
                  reads=[PS(b), 'pp'], writes=[('rs', r)])
            P.add('act', lambda e, r=r: e.activation(rsb[r][:, 0:n], rsb[r][:, 0:n], AF.Exp, scale=-0.5),
                  reads=[('rs', r)], writes=[('rs', r)])
            return r

        def norm_to_hT(pcol, nt, c0, n):
            for k in range(KD):
                P.add('act', lambda e, k=k: e.activation(sq3[:, k, c0:c0 + n], xT[:, k, c0:c0 + n], AF.Square),
                      reads=[('xT', nt, k)], writes=[('sq', nt, k)])
            r = stats_rs(None, nt, c0, n, 1.0 / D, eps_ap)
            for k in range(KD):
                P.add('dve', lambda e, k=k, r=r: e.scalar_tensor_tensor(hT[:, k, c0:c0 + n], xT[:, k, c0:c0 + n],
                                                                       pp[:, pcol + k:pcol + k + 1], rsb[r][:, 0:n],
                                                                       ALU.mult, ALU.mult),
                      reads=[('xT', nt, k), 'pp', ('rs', r)], writes=[('hT', nt, k)])

        def post_norm_residual(pcol, nt, c0, n, half):
            r = stats_rs(None, nt, c0, n, (4.0 if half else 1.0) / D, eps4_ap if half else eps_ap)
            for c in range(KD):
                t = newtmp()
                P.add('dve', lambda e, c=c, r=r, t=t: e.scalar_tensor_tensor(tmp[t][:, 0:n], yT[:, c, c0:c0 + n],
                                                                            pp[:, pcol + c:pcol + c + 1], rsb[r][:, 0:n],
                                                                            ALU.mult, ALU.mult),
                      reads=[('yT', nt), 'pp', ('rs', r)], writes=[('tmp', t)])
                P.add('dve', lambda e, c=c, t=t: e.tensor_tensor(xT[:, c, c0:c0 + n], xT[:, c, c0:c0 + n], tmp[t][:, 0:n], ALU.add),
                      reads=[('xT', nt, c), ('tmp', t)], writes=[('xT', nt, c)])

        def ffn(ntl, pre, post):
            for nt, (kind, t0, n, c0) in enumerate(ntl):
                norm_to_hT(pre, nt, c0, n)
            if FFN_STAGE < 1:
                return
            P.alias(R1_KEYS_QOM + R1_KEYS_U + R1S_KEYS, R1_KEYS_HID)
            for g in range(11):
                wv, wk = w_next()
                wg = wv[:, 0:2048].rearrange("p (k c) -> p k c", k=8)
                wu = wv[:, 2048:4096].rearrange("p (k c) -> p k c", k=8)
                for ci in range(2):
                    i = g * 2 + ci
                    for nt, (kind, t0, n, c0) in enumerate(ntl):
                        bg = bank()
                        for k in range(KD):
                            P.add('pe', lambda e, b=bg, k=k, wg=wg, ci=ci, c0=c0, n=n: e.matmul(
                                psum[b][:, 0:n], lhsT=wg[:, k, ci * 128:(ci + 1) * 128], rhs=hT[:, k, c0:c0 + n],
                                start=(k == 0), stop=(k == KD - 1)), reads=[wk, ('hT', nt, k)], writes=[PS(bg)])
                        bu = bank()
                        for k in range(KD):
                            P.add('pe', lambda e, b=bu, k=k, wu=wu, ci=ci, c0=c0, n=n: e.matmul(
                                psum[b][:, 0:n], lhsT=wu[:, k, ci * 128:(ci + 1) * 128], rhs=hT[:, k, c0:c0 + n],
                                start=(k == 0), stop=(k == KD - 1)), reads=[wk, ('hT', nt, k)], writes=[PS(bu)])
                        t = newtmp()
                        P.add('act', lambda e, b=bg, t=t, n=n: e.activation(tmp[t][:, 0:n], psum[b][:, 0:n], AF.Silu),
                              reads=[PS(bg)], writes=[('tmp', t)])
                        P.add('dve', lambda e, b=bu, t=t, i=i, c0=c0, n=n: e.tensor_tensor(
                            hid[:, i, c0:c0 + n], tmp[t][:, 0:n], psum[b][:, 0:n], ALU.mult),
                            reads=[PS(bu), ('tmp', t)], writes=[('hid', nt)])
            for c in range(KD):
                wv, wk = w_next()
                wd = wv[:, 0:2816].rearrange("p (k c) -> p k c", k=KF)
                for nt, (kind, t0, n, c0) in enumerate(ntl):
                    b = bank()
                    for i in range(KF):
                        P.add('pe', lambda e, b=b, i=i, wd=wd, c0=c0, n=n: e.matmul(
                            psum[b][:, 0:n], lhsT=wd[:, i, :], rhs=hid[:, i, c0:c0 + n],
                            start=(i == 0), stop=(i == KF - 1)), reads=[wk, ('hid', nt)], writes=[PS(b)])
                    P.add('dve', lambda e, b=b, c=c, c0=c0, n=n: e.tensor_copy(yT[:, c, c0:c0 + n], psum[b][:, 0:n]),
                          reads=[PS(b)], writes=[('yT', nt)])
                    P.add('act', lambda e, c=c, c0=c0, n=n: e.activation(sq3[:, c, c0:c0 + n], yT[:, c, c0:c0 + n], AF.Square),
                          reads=[('yT', nt)], writes=[('sq', nt, c)])
            if FFN_STAGE < 3:
                return
            for nt, (kind, t0, n, c0) in enumerate(ntl):
                post_norm_residual(post, nt, c0, n, True)

        def rope(b, n, c0):
            tq = newtmp()
            P.add('act', lambda e: e.copy(tmp[tq][:, 0:n], psum[b][:, 0:n]), reads=[PS(b)], writes=[('tmp', tq)])
            b2 = bank()
            P.add('pe', lambda e: e.matmul(psum[b2][:, 0:n], lhsT=permT, rhs=tmp[tq][:, 0:n], start=True, stop=True),
                  reads=['cst', ('tmp', tq)], writes=[PS(b2)])
            tB = newtmp()
            P.add('dve', lambda e: e.tensor_tensor(tmp[tB][:, 0:n], psum[b2][:, 0:n], ropeS[:, c0:c0 + n], ALU.mult),
                  reads=[PS(b2), 'rope'], writes=[('tmp', tB)])
            P.add('dve', lambda e: e.tensor_tensor(tmp[tq][:, 0:n], tmp[tq][:, 0:n], ropeC[:, c0:c0 + n], ALU.mult),
                  reads=[('tmp', tq), 'rope'], writes=[('tmp', tq)])
            return tq, tB

        n_tiles = len(tiles)
        r1s = [R1[:, i * 1024:(i + 1) * 1024] for i in range(7)]
        R1S_KEYS = [('r1s', i) for i in range(7)]
        c2f = c2[:].rearrange("p k w -> p (k w)").bitcast(F32)
        C2K = [('c2', nt_, k_) for nt_ in range(2) for k_ in range(KD)]
        LBUF = [(stage[0][:], ('stage', 0), []), (stage[1][:], ('stage', 1), []),
                (c2f[:, 0:1024], ('xstg', 2), C2K), (c2f[:, 1024:2048], ('xstg', 3), C2K),
                (stage[0][:], ('stage', 0), [])]

        def tile_ntl(tl):
            out, c0 = [], 0
            for (kind, t0, n) in tl:
                out.append((kind, t0, n, c0))
                c0 += n
            return out

        def tile_blocks(ntl_):
            blocks = []
            for nt, (kind, t0, n, c0) in enumerate(ntl_):
                for bl in range(n // 128):
                    blocks.append((nt, kind, t0 + bl * 128, c0 + bl * 128))
            return blocks

        def issue_rope(ntl_):
            for nt, (kind, t0, n, c0) in enumerate(ntl_):
                src0 = t0 if kind == 'p' else SEQ
                P.add('sp', lambda e, c0=c0, n=n, src0=src0: e.dma_start(out=ropeC[:, c0:c0 + n], in_=roped[:, 0, src0:src0 + n]),
                      writes=['rope'], dma_key='rope')
                P.add('sp', lambda e, c0=c0, n=n, src0=src0: e.dma_start(out=ropeS[:, c0:c0 + n], in_=roped[:, 1, src0:src0 + n]),
                      writes=['rope'], dma_key='rope')

        def issue_x_load(ntl_, bi):
            nt, kind, r0, col = tile_blocks(ntl_)[bi]
            src = xp if kind == 'p' else xs
            buf, key, extra = LBUF[bi]
            P.add('sp', lambda e: e.dma_start(out=buf, in_=src[r0:r0 + 128, :]), writes=[key] + extra, dma_key=key)

        for ti, tl in enumerate(tiles):
            ntl = []
            c0 = 0
            for (kind, t0, n) in tl:
                ntl.append((kind, t0, n, c0))
                c0 += n
            last_tile = (ti == n_tiles - 1)
            has_s = any(k == 's' for (k, _, _, _) in ntl)
            tok0 = ntl[0][1]
            blocks = tile_blocks(ntl)
            if ti == 0:
                issue_rope(ntl)
                for bi in range(min(2, len(blocks))):
                    issue_x_load(ntl, bi)
            P.alias(R1_KEYS_HID, R1S_KEYS[5:7])
            if ti == 0:
                for bi in range(2, min(4, len(blocks))):
                    issue_x_load(ntl, bi)
            for bi, (nt, kind, r0, col) in enumerate(blocks):
                buf, key, extra = LBUF[bi]
                for hb in range(2):
                    b = bank()
                    for kk in range(4):
                        k = hb * 4 + kk
                        P.add('pe', lambda e, b=b, kk=kk, k=k, buf=buf: e.transpose(
                            psum[b][:, kk * 128:(kk + 1) * 128], buf[:, k * 128:(k + 1) * 128], ident),
                            reads=[key, 'cst'] + extra, writes=[PS(b)])
                    dst = xT[:, hb * 4:hb * 4 + 4, col:col + 128]
                    srcv = psum[b][:, 0:512].rearrange("p (k c) -> p k c", k=4)
                    if hb == 0:
                        P.add('act', lambda e, dst=dst, srcv=srcv: e.copy(dst, srcv), reads=[PS(b)], writes=[('xT', nt, hb * 4 + i_) for i_ in range(4)])
                    else:
                        P.add('dve', lambda e, dst=dst, srcv=srcv: e.tensor_copy(dst, srcv), reads=[PS(b)], writes=[('xT', nt, hb * 4 + i_) for i_ in range(4)])
                if bi + 4 < len(blocks):
                    issue_x_load(ntl, bi + 4)

            if 'ffn1' not in skip:
                ffn(ntl, PC_F1PRE, PC_F1POST)

            if 'mixer' not in skip:
                for nt, (kind, t0, n, c0) in enumerate(ntl):
                    norm_to_hT(PC_MPRE, nt, c0, n)
                P.alias(R1_KEYS_HID, R1_KEYS_U + R1_KEYS_QO)
                if MIX_STAGE >= 3:
                    if has_s:
                        P.alias([('sq', nt, k) for nt in range(2) for k in range(KD)], ['qpad'])
                        P.add('pool', lambda e: e.memset(sq[:, 0:4096], 0.0), writes=['qpad'])
                    wv, wk = w_next()
                    wkv = wv[:, 0:4096].rearrange("p (k c) -> p k c", k=8)
                    for kc in range(2):
                        for nt, (kind, t0, n, c0) in enumerate(ntl):
                            b = bank()
                            for k in range(KD):
                                P.add('pe', lambda e, b=b, k=k, kc=kc, c0=c0, n=n, wkv=wkv: e.matmul(
                                    psum[b][:, 0:n], lhsT=wkv[:, k, kc * 128:(kc + 1) * 128], rhs=hT[:, k, c0:c0 + n],
                                    start=(k == 0), stop=(k == KD - 1)), reads=[wk, ('hT', nt, k)], writes=[PS(b)])
                            tA, tB = rope(b, n, c0)
                            kcol = 128 + c0
                            P.add('dve', lambda e, tA=tA, tB=tB, kc=kc, kcol=kcol, n=n: e.tensor_tensor(
                                kT[:, kc, kcol:kcol + n], tmp[tA][:, 0:n], tmp[tB][:, 0:n], ALU.add),
                                reads=[('tmp', tA), ('tmp', tB)], writes=[('kT', nt)])
                            if (last_tile and kind == 'p') or kind == 's':
                                if kind == 'p':
                                    P.add('pool', lambda e, tA=tA, tB=tB, kc=kc, n=n: e.tensor_tensor(
                                        kf[:, kc, 0:128], tmp[tA][:, n - 128:n], tmp[tB][:, n - 128:n], ALU.add),
                                        reads=[('tmp', tA), ('tmp', tB)], writes=['kf'])
                                else:
                                    P.add('pool', lambda e, tA=tA, tB=tB, kc=kc, n=n: e.tensor_tensor(
                                        kf[:, kc, 128:256], tmp[tA][:, 0:128], tmp[tB][:, 0:128], ALU.add),
                                        reads=[('tmp', tA), ('tmp', tB)], writes=['kf'])
                    if last_tile or has_s:
                        for part in ([0] if last_tile else []) + ([1] if has_s else []):
                            b = bank()
                            for kc in range(2):
                                P.add('pe', lambda e, b=b, kc=kc, part=part: e.transpose(
                                    psum[b][:, kc * 128:(kc + 1) * 128], kf[:, kc, part * 128:(part + 1) * 128], ident),
                                    reads=['kf', 'cst'], writes=[PS(b)])
                            P.add('act', lambda e, b=b, part=part: e.copy(stage[0][:, part * 256:(part + 1) * 256], psum[b][:, 0:256]),
                                  reads=[PS(b)], writes=[('stage', 0)])
                            if part == 0:
                                P.add('sp', lambda e: e.dma_start(out=kwp, in_=stage[0][:, 0:256]), reads=[('stage', 0)], dma_key=('stage', 0))
                            else:
                                for s in range(NS):
                                    P.add('sp', lambda e, s=s: e.dma_start(out=kws[s, 120:128, :], in_=stage[0][s * DS:(s + 1) * DS, 256:512]),
                                          reads=[('stage', 0)], dma_key=('stage', 0))
                    for nt, (kind, t0, n, c0) in enumerate(ntl):
                        for bl in range(n // 128):
                            b = bank()
                            for k in range(KD):
                                P.add('pe', lambda e, b=b, k=k, c0=c0, bl=bl, wkv=wkv: e.matmul(
                                    psum[b][:, 0:256], lhsT=hT[:, k, c0 + bl * 128:c0 + (bl + 1) * 128], rhs=wkv[:, k, 256:512],
                                    start=(k == 0), stop=(k == KD - 1)), reads=[wk, ('hT', nt, k)], writes=[PS(b)])
                            if kind == 'p':
                                slot = 1 + bl
                                for hf in range(2):
                                    dst = vpad[:, slot, :, :].rearrange("p (kc hf) c -> p kc hf c", hf=2)[:, :, hf, hf * 64:(hf + 1) * 64]
                                    srcv = psum[b][:, 0:256].rearrange("p (kc hf c) -> p kc hf c", kc=2, hf=2)[:, :, hf, :]
                                    eng = 'act' if bl % 2 == 0 else 'dve'
                                    if eng == 'act':
                                        P.add('act', lambda e, dst=dst, srcv=srcv: e.copy(dst, srcv), reads=[PS(b)], writes=[('vpad', slot)])
                                    else:
                                        P.add('dve', lambda e, dst=dst, srcv=srcv: e.tensor_copy(dst, srcv), reads=[PS(b)], writes=[('vpad', slot)])
                                if last_tile and bl == n // 128 - 1:
                                    if bl % 2 == 0:
                                        P.add('act', lambda e, b=b: e.copy(vf[0], psum[b][:, 0:256]), reads=[PS(b)], writes=[('stage', 1)])
                                    else:
                                        P.add('dve', lambda e, b=b: e.tensor_copy(vf[0], psum[b][:, 0:256]), reads=[PS(b)], writes=[('stage', 1)])
                                    P.add('sp', lambda e: e.dma_start(out=vwp, in_=vf[0]), reads=[('stage', 1)], dma_key=('stage', 1))
                            else:
                                P.add('act', lambda e, b=b: e.copy(vf[1], psum[b][:, 0:256]), reads=[PS(b)], writes=[('stage', 1)])
                                P.add('dve', lambda e: e.tensor_copy(vsb[:], vf[1]), reads=[('stage', 1)], writes=['vsb'])
                                for s in range(NS):
                                    P.add('sp', lambda e, s=s: e.dma_start(out=vws[s, 120:128, :], in_=stage[1][s * DS:(s + 1) * DS, 256:512]),
                                          reads=[('stage', 1)], dma_key=('stage', 1))
                    for c in range(KD):
                        if c % 4 == 0:
                            wv, wk = w_next()
                            wq = wv[:, 0:4096].rearrange("p (k c) -> p k c", k=8)
                        cq = c % 4
                        kc = c // 4
                        for nt, (kind, t0, n, c0) in enumerate(ntl):
                            b = bank()
                            for k in range(KD):
                                P.add('pe', lambda e, b=b, k=k, wq=wq, cq=cq, c0=c0, n=n: e.matmul(
                                    psum[b][:, 0:n], lhsT=wq[:, k, cq * 128:(cq + 1) * 128], rhs=hT[:, k, c0:c0 + n],
                                    start=(k == 0), stop=(k == KD - 1)), reads=[wk, ('hT', nt, k)], writes=[PS(b)])
                            tA, tB = rope(b, n, c0)
                            if kind == 'p':
                                P.add('dve', lambda e, tA=tA, tB=tB, c=c, c0=c0, n=n: e.tensor_tensor(
                                    qT[:, c, c0:c0 + n], tmp[tA][:, 0:n], tmp[tB][:, 0:n], ALU.add),
                                    reads=[('tmp', tA), ('tmp', tB)], writes=[('qT', nt)])
                            else:
                                for hf in range(2):
                                    pos = 2 * c + hf
                                    dst = qpad[hf * 64:(hf + 1) * 64, kc, :, pos * DS:(pos + 1) * DS]
                                    P.add('dve', lambda e, tA=tA, tB=tB, hf=hf, dst=dst, n=n: e.tensor_tensor(
                                        dst, tmp[tA][hf * 64:(hf + 1) * 64, 0:n].rearrange("p (s c) -> p s c", s=NS),
                                        tmp[tB][hf * 64:(hf + 1) * 64, 0:n].rearrange("p (s c) -> p s c", s=NS), ALU.add),
                                        reads=[('tmp', tA), ('tmp', tB)], writes=['qpad'])


                pkind, pt0, pn, pc0 = ntl[0]
                np_ = pn
                gl = {}

                def glu_part(j):
                    if j % 2 == 0:
                        wv, wk = w_next()
                        gl['wk'] = wk
                        gl['wval'] = wv[:, 0:2048].rearrange("p (k c) -> p k c", k=8)
                        gl['wgate'] = wv[:, 2048:4096].rearrange("p (k c) -> p k c", k=8)
                    wk, wval, wgate = gl['wk'], gl['wval'], gl['wgate']
                    cj = j % 2
                    jb = 0
                    us3 = us[jb].rearrange("p (s c) -> p s c", s=NS)
                    P.add('dve', lambda e: e.tensor_copy(up[jb][:, 0:HALO], uhalo[:, j, :]),
                          reads=['uhalo'], writes=[('up', jb)])
                    for nt, (kind, t0, n, c0) in enumerate(ntl):
                        bv = bank()
                        for k in range(KD):
                            P.add('pe', lambda e, b=bv, k=k, c0=c0, n=n: e.matmul(
                                psum[b][:, 0:n], lhsT=wval[:, k, cj * 128:(cj + 1) * 128], rhs=hT[:, k, c0:c0 + n],
                                start=(k == 0), stop=(k == KD - 1)), reads=[wk, ('hT', nt, k)], writes=[PS(bv)])
                        bg = bank()
                        for k in range(KD):
                            P.add('pe', lambda e, b=bg, k=k, c0=c0, n=n: e.matmul(
                                psum[b][:, 0:n], lhsT=wgate[:, k, cj * 128:(cj + 1) * 128], rhs=hT[:, k, c0:c0 + n],
                                start=(k == 0), stop=(k == KD - 1)), reads=[wk, ('hT', nt, k)], writes=[PS(bg)])
                        t = newtmp()
                        P.add('act', lambda e, b=bg, t=t, n=n: e.activation(tmp[t][:, 0:n], psum[b][:, 0:n], AF.Sigmoid),
                              reads=[PS(bg)], writes=[('tmp', t)])
                        if kind == 'p':
                            P.add('dve', lambda e, b=bv, t=t, n=n: e.tensor_tensor(
                                up[jb][:, HALO:HALO + n], psum[b][:, 0:n], tmp[t][:, 0:n], ALU.mult),
                                reads=[PS(bv), ('tmp', t)], writes=[('up', jb)])
                            P.add('dve', lambda e, n=n: e.tensor_copy(upb[:, 0:HALO + n], up[jb][:, 0:HALO + n]),
                                  reads=[('up', jb)], writes=['upb'])
                        else:
                            P.add('dve', lambda e, b=bv, t=t, n=n: e.tensor_tensor(
                                us3[:, :, HALO:HALO + DS], psum[b][:, 0:n].rearrange("p (s c) -> p s c", s=NS),
                                tmp[t][:, 0:n].rearrange("p (s c) -> p s c", s=NS), ALU.mult),
                                reads=[PS(bv), ('tmp', t)], writes=[('us', jb)])
                    P.add('pool', lambda e: e.tensor_copy(uhalo[:, j, :], up[jb][:, np_:np_ + HALO]),
                          reads=[('up', jb)], writes=['uhalo'])
                    if has_s:
                        stv = stc[:, j * 128:(j + 1) * 128].rearrange("(q r) c -> r q c", r=120)
                        P.add('sp', lambda e: e.dma_start(out=stst[jb][0:120, :, :], in_=stv),
                              writes=[('stst', 0)], dma_key=('stst', 0))
                        b = bank()
                        for q in range(4):
                            P.add('pe', lambda e, b=b, q=q: e.transpose(
                                psum[b][:, q * 120:(q + 1) * 120], stst[jb][0:120, q, :], ident[0:120, 0:120]),
                                reads=[('stst', 0), 'cst'], writes=[PS(b)])
                        P.add('act', lambda e, b=b: e.copy(us3[:, :, 0:HALO], psum[b][:, 0:480].rearrange("p (s c) -> p s c", s=NS)),
                              reads=[PS(b)], writes=[('us', jb)])

                def diag_build(j):
                    cwc = PC_CW + j * CWID
                    dg = diag[j % 2]
                    P.add(DIAG_ENG, lambda e: e.tensor_tensor(
                        dg, identb[:].unsqueeze(1).to_broadcast([128, CWID, 128]),
                        pp[:, cwc:cwc + CWID].unsqueeze(2).to_broadcast([128, CWID, 128]), ALU.mult),
                        reads=['identb', 'pp'], writes=[('diag', j % 2)])

                def conv_part(j):
                    jb = 0
                    us3 = us[jb].rearrange("p (s c) -> p s c", s=NS)
                    cwc = PC_CW + j * CWID
                    for nt, (kind, t0, n, c0) in enumerate(ntl):
                        if kind == 'p':
                            bcv = bank()
                            for t_ in range(CWID):
                                P.add('pe', lambda e, b=bcv, t_=t_, n=n: e.matmul(
                                    psum[b][:, 0:n], lhsT=diag[j % 2][:, t_, :], rhs=upb[:, t_:t_ + n],
                                    start=(t_ == 0), stop=(t_ == CWID - 1)), reads=[('diag', j % 2), 'upb'], writes=[PS(bcv)])
                            P.add('act', lambda e, b=bcv, c0=c0, n=n: e.activation(
                                yT[:, j, c0:c0 + n], psum[b][:, 0:n], AF.Identity, bias=pp[:, PC_CB + j:PC_CB + j + 1], scale=1.0),
                                reads=[PS(bcv), 'pp'], writes=[('yT', nt)])
                            if not has_s:
                                P.add('act', lambda e, c0=c0, n=n: e.copy(c2[:, j, c0:c0 + n], yT[:, j, c0:c0 + n]),
                                      reads=[('yT', nt)], writes=[('c2', nt, j)])
                                P.add('act', lambda e, c0=c0, n=n: e.activation(sq3[:, j, c0:c0 + n], yT[:, j, c0:c0 + n], AF.Square),
                                      reads=[('yT', nt)], writes=[('sq', nt, j)])
                            continue
                        P.add('dve', lambda e: e.tensor_copy(usb[:].rearrange("p (t s) -> p t s", s=NS), us3.rearrange("p s t -> p t s")),
                              reads=[('us', jb)], writes=['usb'])
                        bcs = bank()
                        for t_ in range(CWID):
                            P.add('pe', lambda e, b=bcs, t_=t_: e.matmul(
                                psum[b][:, 0:DS * NS], lhsT=diag[j % 2][:, t_, :], rhs=usb[:, t_ * NS:(t_ + DS) * NS],
                                start=(t_ == 0), stop=(t_ == CWID - 1)), reads=[('diag', j % 2), 'usb'], writes=[PS(bcs)])
                        P.add('act', lambda e, b=bcs, c0=c0, n=n: e.activation(
                            yT[:, j, c0:c0 + n].rearrange("p (s t) -> p s t", s=NS),
                            psum[b][:, 0:DS * NS].rearrange("p (t s) -> p s t", s=NS), AF.Identity,
                            bias=pp[:, PC_CB + j:PC_CB + j + 1], scale=1.0),
                            reads=[PS(bcs), 'pp'], writes=[('yT', nt)])
                    if last_tile:
                        b = bank()
                        P.add('pe', lambda e, b=b: e.transpose(psum[b][0:HALO, 0:128], up[jb][:, np_:np_ + HALO], ident),
                              reads=[('up', jb), 'cst'], writes=[PS(b)])
                        P.add('act', lambda e, b=b: e.copy(stage[0][0:HALO, j * 128:(j + 1) * 128], psum[b][0:HALO, 0:128]),
                              reads=[PS(b)], writes=[('stage', 0)])
                    if True:
                        if has_s:
                            b2 = bank()
                            tu = newtmp()
                            P.add('pool', lambda e, tu=tu: e.tensor_copy(
                                tmp[tu][:, 0:128].rearrange("p (s c) -> p s c", s=NS), us3[:, :, HALO:HALO + DS]),
                                reads=[('us', jb)], writes=[('tmp', tu)])
                            P.add('pe', lambda e, b=b2, tu=tu: e.transpose(psum[b][:, 0:128], tmp[tu][:, 0:128], ident),
                                  reads=[('tmp', tu), 'cst'], writes=[PS(b2)])
                            P.add('act', lambda e, b=b2: e.copy(stage[1][:, j * 128:(j + 1) * 128], psum[b][:, 0:128]),
                                  reads=[PS(b2)], writes=[('stage', 1)])

                def att_stage1(wi, bl, cg):
                    wa = wi % 2
                    st = stw[wa]
                    gb = (pt0 // 128) + bl
                    mk = maskP0 if gb == 0 else maskP
                    for pair in range(2):
                        bSs = [bank(), bank()]
                        for uu in range(2):
                            c = cg * 4 + 2 * pair + uu
                            for hf in range(2):
                                P.add('pe', lambda e, b=bSs[hf], hf=hf, c=c, uu=uu: e.matmul(
                                    psum[b][:, uu * 256:(uu + 1) * 256],
                                    lhsT=qT[hf * 64:(hf + 1) * 64, c, pc0 + bl * 128:pc0 + (bl + 1) * 128],
                                    rhs=kT[hf * 64:(hf + 1) * 64, cg, bl * 128:bl * 128 + 256], start=True, stop=True),
                                    reads=[('qT', 0), ('kT', 0), ('kT', 'halo')], writes=[PS(bSs[hf])])
                        for hf in range(2):
                            si = 2 * pair + hf
                            sm = smb[si]
                            P.add('dve', lambda e, b=bSs[hf], sm=sm: e.tensor_tensor(sm[:], psum[b][:, 0:512], mk, ALU.add),
                                  reads=[PS(bSs[hf]), 'cst'], writes=[('sm', si)])
                            k8 = 4 * pair + 2 * hf
                            P.add('dve', lambda e, sm=sm, k8=k8: e.tensor_reduce(
                                st[:, k8:k8 + 2], sm[:].rearrange("p (h k) -> p h k", h=2), AX.X, ALU.max),
                                reads=[('sm', si)], writes=[('stw', wa)])
                    sk = pp[:, PC_SINKW + 8 * cg:PC_SINKW + 8 * cg + 8]
                    P.add('dve', lambda e: e.scalar_tensor_tensor(st[:, 8:16], st[:, 0:8], SCALE, sk, ALU.mult, ALU.max),
                          reads=[('stw', wa), 'pp'], writes=[('stw', wa)])
                    P.add('dve', lambda e: e.tensor_scalar(st[:, 16:24], st[:, 8:16], -1.0, None, ALU.mult),
                          reads=[('stw', wa)], writes=[('stw', wa)])
                    P.add('dve', lambda e: e.tensor_tensor(st[:, 24:32], sk, st[:, 16:24], ALU.add),
                          reads=[('stw', wa), 'pp'], writes=[('stw', wa)])

                def att_stage2(wi, bl, cg):
                    wa = wi % 2
                    st = stw[wa]
                    for pair in range(2):
                        for hf in range(2):
                            si = 2 * pair + hf
                            sm = smb[si]
                            for uu in range(2):
                                k8 = 4 * pair + 2 * hf + uu
                                P.add('act', lambda e, sm=sm, uu=uu, k8=k8: e.activation(
                                    sm[:, uu * 256:(uu + 1) * 256], sm[:, uu * 256:(uu + 1) * 256], AF.Exp,
                                    bias=st[:, 16 + k8:17 + k8], scale=SCALE, accum_out=st[:, 32 + k8:33 + k8]),
                                    reads=[('sm', si), ('stw', wa)], writes=[('sm', si), ('stw', wa)])
                    P.add('act', lambda e: e.activation(st[:, 40:48], st[:, 24:32], AF.Exp),
                          reads=[('stw', wa)], writes=[('stw', wa)])

                def att_stage3(wi, bl, cg):
                    wa = wi % 2
                    st = stw[wa]
                    P.add('dve', lambda e: e.tensor_tensor(st[:, 48:56], st[:, 32:40], st[:, 40:48], ALU.add),
                          reads=[('stw', wa)], writes=[('stw', wa)])
                    P.add('dve', lambda e: e.reciprocal(st[:, 56:64], st[:, 48:56]), reads=[('stw', wa)], writes=[('stw', wa)])
                    for pair in range(2):
                        for uu in range(2):
                            u = 2 * pair + uu
                            pb = pbb[u]
                            for hf in range(2):
                                si = 2 * pair + hf
                                sm = smb[si]
                                k8 = 4 * pair + 2 * hf + uu
                                P.add('dve', lambda e, sm=sm, pb=pb, hf=hf, uu=uu, k8=k8: e.tensor_scalar(
                                    pb[:, hf * 256:(hf + 1) * 256], sm[:, uu * 256:(uu + 1) * 256],
                                    st[:, 56 + k8:57 + k8], None, ALU.mult),
                                    reads=[('sm', si), ('stw', wa)], writes=[('pb', u)])

                def att_stage45(wi, bl, cg):
                    for u in range(4):
                        c = cg * 4 + u
                        pb, pT = pbb[u], pTb[u]
                        bT = bank()
                        bTb = psum[bT][:].bitcast(BF16)
                        for m in range(4):
                            P.add('pe', lambda e, bTb=bTb, pb=pb, m=m: e.transpose(
                                bTb[:, m * 128:(m + 1) * 128], pb[:, m * 128:(m + 1) * 128], identb[:]),
                                reads=[('pb', u), 'identb'], writes=[PS(bT)])
                        P.add('dve', lambda e, bTb=bTb, pT=pT: e.tensor_copy(pT[:], bTb[:, 0:512]), reads=[PS(bT)], writes=[('pT', u)])
                    for u in range(4):
                        c = cg * 4 + u
                        pT = pTb[u]
                        bO = bank()
                        idx = 0
                        for hf in range(2):
                            for kb in range(2):
                                P.add('pe', lambda e, b=bO, hf=hf, kb=kb, pT=pT, idx=idx: e.matmul(
                                    psum[b][:, 0:128], lhsT=vpad[:, bl + kb, cg * 2 + hf, :],
                                    rhs=pT[:, (hf * 2 + kb) * 128:(hf * 2 + kb + 1) * 128], start=(idx == 0), stop=(idx == 3)),
                                    reads=[('pT', u), ('vpad', bl + kb)], writes=[PS(bO)])
                                idx += 1
                        P.add('dve', lambda e, b=bO, c=c: e.tensor_copy(oT[:, c, pc0 + bl * 128:pc0 + (bl + 1) * 128], psum[b][:, 0:128]),
                              reads=[PS(bO)], writes=[('oT', 0)])

                waves = [(bl, cg) for bl in range(pn // 128) for cg in range(2)]
                assert len(waves) == KD
                P.alias(['ostage', ('ckin', 0), ('ckin', 1), ('vcat', 0), ('vcat', 1)], [('diag', 0)])
                P.alias(SAMP_D1_KEYS, [('diag', 1)])
                diag_build(0)
                for j in range(KD):
                    bl, cg = waves[j]
                    att_stage1(j, bl, cg)
                    glu_part(j)
                    if j + 1 < KD:
                        diag_build(j + 1)
                    if j > 0:
                        att_stage45(j - 1, *waves[j - 1])
                    att_stage2(j, bl, cg)
                    conv_part(j)
                    att_stage3(j, bl, cg)
                att_stage45(KD - 1, *waves[KD - 1])
                if last_tile:
                    P.add('sp', lambda e: e.dma_start(out=csp, in_=stage[0][0:HALO, :]), reads=[('stage', 0)], dma_key=('stage', 0))
                if True:
                    if has_s:
                        for s in range(NS):
                            P.add('sp', lambda e, s=s: e.dma_start(out=css[s, 22:30, :], in_=stage[1][s * DS:(s + 1) * DS, :]),
                                  reads=[('stage', 1)], dma_key=('stage', 1))
                if not last_tile:
                    nb_ = pn // 128
                    P.add('pool', lambda e: e.tensor_copy(kT[:, :, 0:128], kT[:, :, pn:pn + 128]),
                          reads=[('kT', 0)], writes=[('kT', 'halo')])
                    P.add('pool', lambda e: e.tensor_copy(vpad[:, 0, :, :], vpad[:, nb_, :, :]),
                          reads=[('vpad', nb_)], writes=[('vpad', 0)])

                if MIX_STAGE >= 5:
                    if has_s:
                        skind, st0, sn, sc0 = ntl[1]
                        P.alias([('diag', 0)], ['ostage', ('ckin', 0), ('ckin', 1), ('vcat', 0), ('vcat', 1)])
                        P.alias([('diag', 1)], SAMP_D1_KEYS)
                        def samp_A(s):
                            a = s % 4
                            sm, pe_, pb, pT, st = smb[a], smb[a], pbb[a], pTb[a], stt_[a]
                            P.add('sp', lambda e, a=a, s=s: e.dma_start(out=ckin[a], in_=ck[s, :, :]), writes=[('ckin', a)], dma_key=('ckin', a))
                            P.add('pool', lambda e, a=a, s=s: e.dma_start(out=vcat[a], in_=cv[s, :, :]), writes=[('vcat', a)], dma_key=('vcat', a))
                            P.add('sp', lambda e, a=a, s=s: e.dma_start(out=vnew[a][0:DS, :], in_=vsb[s * DS:(s + 1) * DS, :]),
                                  reads=['vsb'], writes=[('vnew', a)], dma_key=('vnew', a))
                            b = bank()
                            for kc in range(2):
                                P.add('pe', lambda e, b=b, kc=kc, a=a: e.transpose(psum[b][:, kc * 128:(kc + 1) * 128], ckin[a][:, kc * 128:(kc + 1) * 128], ident),
                                      reads=[('ckin', a), 'cst'], writes=[PS(b)])
                            P.add('act', lambda e, b=b, a=a: e.copy(kcat[a][:, :, 0:128], psum[b][:, 0:256].rearrange("p (k c) -> p k c", k=2)),
                                  reads=[PS(b)], writes=[('kcat', a)])
                            P.add('dve', lambda e, a=a, s=s, sc0=sc0: e.tensor_copy(kcat[a][:, :, 128:136], kT[:, :, 128 + sc0 + s * DS:128 + sc0 + (s + 1) * DS]),
                                  reads=[('kT', 1)], writes=[('kcat', a)])
                            bS = bank()
                            for kc in range(2):
                                P.add('pe', lambda e, b=bS, kc=kc, a=a, s=s: e.matmul(psum[b][:, 0:136], lhsT=qpad[:, kc, s, :], rhs=kcat[a][:, kc, :],
                                                                                   start=(kc == 0), stop=(kc == 1)),
                                      reads=['qpad', ('kcat', a)], writes=[PS(bS)])
                            P.add('dve', lambda e, b=bS, sm=sm: e.tensor_tensor(sm[:, 0:136], psum[b][:, 0:136], maskS, ALU.add),
                                  reads=[PS(bS), 'cst'], writes=[('sm', a)])
                            P.add('dve', lambda e, sm=sm, st=st: e.tensor_reduce(st[:, 0:1], sm[:, 0:136], AX.X, ALU.max),
                                  reads=[('sm', a)], writes=[('st', a)])
                            sk = pp[:, PC_SINKROW:PC_SINKROW + 1]
                            P.add('dve', lambda e, st=st, sk=sk: e.scalar_tensor_tensor(st[:, 2:3], st[:, 0:1], SCALE, sk, ALU.mult, ALU.max),
                                  reads=[('st', a), 'pp'], writes=[('st', a)])
                            P.add('dve', lambda e, st=st: e.tensor_scalar(st[:, 4:5], st[:, 2:3], -1.0, None, ALU.mult),
                                  reads=[('st', a)], writes=[('st', a)])
                            P.add('dve', lambda e, st=st, sk=sk: e.tensor_tensor(st[:, 8:9], sk, st[:, 4:5], ALU.add),
                                  reads=[('st', a), 'pp'], writes=[('st', a)])
                        def samp_B(s):
                            a = s % 4
                            sm, pe_, pb, pT, st = smb[a], smb[a], pbb[a], pTb[a], stt_[a]
                            P.add('act', lambda e, sm=sm, pe_=pe_, st=st: e.activation(pe_[:, 0:136], sm[:, 0:136], AF.Exp, bias=st[:, 4:5], scale=SCALE,
                                                                                      accum_out=st[:, 6:7]),
                                  reads=[('sm', a), ('st', a)], writes=[('sm', a), ('st', a)])
                            P.add('act', lambda e, st=st: e.activation(st[:, 10:11], st[:, 8:9], AF.Exp), reads=[('st', a)], writes=[('st', a)])
                            P.add('dve', lambda e, st=st: e.tensor_tensor(st[:, 12:13], st[:, 6:7], st[:, 10:11], ALU.add),
                                  reads=[('st', a)], writes=[('st', a)])
                            P.add('dve', lambda e, st=st: e.reciprocal(st[:, 14:15], st[:, 12:13]), reads=[('st', a)], writes=[('st', a)])
                            P.add('dve', lambda e, pe_=pe_, pb=pb, st=st: e.tensor_scalar(pb[:, 0:136], pe_[:, 0:136], st[:, 14:15], None, ALU.mult),
                                  reads=[('sm', a), ('st', a)], writes=[('pb', a)])
                        def samp_C(s):
                            a = s % 4
                            sm, pe_, pb, pT, st = smb[a], smb[a], pbb[a], pTb[a], stt_[a]
                            bT = bank()
                            bTb = psum[bT][:].bitcast(BF16)
                            P.add('pe', lambda e, bTb=bTb, pb=pb: e.transpose(bTb[:, 0:128], pb[:, 0:128], identb[:]),
                                  reads=[('pb', a), 'identb'], writes=[PS(bT)])
                            P.add('pe', lambda e, bTb=bTb, pb=pb: e.transpose(bTb[0:DS, 128:256], pb[:, 128:136], identb[:]),
                                  reads=[('pb', a), 'identb'], writes=[PS(bT)])
                            P.add('act', lambda e, bTb=bTb, pT=pT: e.copy(pT[:, 0:128], bTb[:, 0:128]), reads=[PS(bT)], writes=[('pT', a)])
                            P.add('act', lambda e, bTb=bTb, a=a: e.copy(pT8[a][0:DS, :], bTb[0:DS, 128:256]), reads=[PS(bT)], writes=[('pT8', a)])
                            bO = bank()
                            for kc in range(2):
                                P.add('pe', lambda e, b=bO, kc=kc, a=a, pT=pT: e.matmul(psum[b][:, kc * 64:(kc + 1) * 64], lhsT=vcat[a][:, kc * 128:(kc + 1) * 128],
                                                                                      rhs=pT[:, kc * 64:(kc + 1) * 64], start=True, stop=False),
                                      reads=[('vcat', a), ('pT', a)], writes=[PS(bO)])
                                P.add('pe', lambda e, b=bO, kc=kc, a=a: e.matmul(psum[b][:, kc * 64:(kc + 1) * 64], lhsT=vnew[a][0:DS, kc * 128:(kc + 1) * 128],
                                                                               rhs=pT8[a][0:DS, kc * 64:(kc + 1) * 64], start=False, stop=True),
                                      reads=[('vnew', a), ('pT8', a)], writes=[PS(bO)])
                            P.add('act', lambda e, b=bO, s=s: e.copy(ostage[:, s, :], psum[b][:, 0:128]), reads=[PS(bO)], writes=['ostage'])

                        for t_ in range(NS + 2):
                            if t_ < NS:
                                samp_A(t_)
                            if 0 <= t_ - 1 < NS:
                                samp_B(t_ - 1)
                            if 0 <= t_ - 2 < NS:
                                samp_C(t_ - 2)
                        for pos in range(16):
                            c, hf = pos // 2, pos % 2
                            P.add('dve', lambda e, c=c, hf=hf, pos=pos, sc0=sc0, sn=sn: e.tensor_copy(
                                oT[hf * 64:(hf + 1) * 64, c, sc0:sc0 + sn].rearrange("p (s c) -> p s c", s=NS),
                                ostage[hf * 64:(hf + 1) * 64, :, pos * DS:(pos + 1) * DS]),
                                reads=['ostage'], writes=[('oT', 1)])
                        P.alias(['qpad'], [('sq', nt, k) for nt in range(2) for k in range(KD)])

                if MIX_STAGE >= 2:
                    for nt, (kind, t0, n, c0) in enumerate(ntl):
                        for k in range(KD if has_s else 0):
                            P.add('act', lambda e, k=k, c0=c0, n=n: e.copy(c2[:, k, c0:c0 + n], yT[:, k, c0:c0 + n]),
                                  reads=[('yT', nt)], writes=[('c2', nt, k)])
                            P.add('act', lambda e, k=k, c0=c0, n=n: e.activation(sq3[:, k, c0:c0 + n], yT[:, k, c0:c0 + n], AF.Square),
                                  reads=[('yT', nt)], writes=[('sq', nt, k)])
                        b1 = bank()
                        for k in range(KD):
                            P.add('pe', lambda e, b=b1, k=k, c0=c0, n=n: e.matmul(psum[b][:, 0:n], lhsT=ones[:], rhs=c2[:, k, c0:c0 + n],
                                                                               start=(k == 0), stop=(k == KD - 1)),
                                  reads=['ones', ('c2', nt, k)], writes=[PS(b1)])
                        b2 = bank()
                        for k in range(KD):
                            P.add('pe', lambda e, b=b2, k=k, c0=c0, n=n: e.matmul(psum[b][:, 0:n], lhsT=ones[:], rhs=sq3[:, k, c0:c0 + n],
                                                                               start=(k == 0), stop=(k == KD - 1)),
                                  reads=['ones', ('sq', nt, k)], writes=[PS(b2)])
                        tm = newtmp()
                        P.add('dve', lambda e, b=b1, tm=tm, n=n: e.tensor_scalar(tmp[tm][:, 0:n], psum[b][:, 0:n], 1.0 / D, None, ALU.mult),
                              reads=[PS(b1)], writes=[('tmp', tm)])
                        tv = newtmp()
                        P.add('dve', lambda e, tm=tm, tv=tv, n=n: e.tensor_tensor(tmp[tv][:, 0:n], tmp[tm][:, 0:n], tmp[tm][:, 0:n], ALU.mult),
                              reads=[('tmp', tm)], writes=[('tmp', tv)])
                        P.add('dve', lambda e, b=b2, tv=tv, n=n: e.scalar_tensor_tensor(tmp[tv][:, 0:n], psum[b][:, 0:n], 1.0 / D, tmp[tv][:, 0:n],
                                                                                       ALU.mult, ALU.subtract),
                              reads=[PS(b2), ('tmp', tv)], writes=[('tmp', tv)])
                        r = newrs()
                        P.add('act', lambda e, tv=tv, r=r, n=n: e.activation(rsb[r][:, 0:n], tmp[tv][:, 0:n], AF.Ln, bias=eps_ap, scale=1.0),
                              reads=[('tmp', tv), 'pp'], writes=[('rs', r)])
                        P.add('act', lambda e, r=r, n=n: e.activation(rsb[r][:, 0:n], rsb[r][:, 0:n], AF.Exp, scale=-0.5),
                              reads=[('rs', r)], writes=[('rs', r)])
                        P.add('dve', lambda e, tm=tm, r=r, n=n: e.scalar_tensor_tensor(tmp[tm][:, 0:n], tmp[tm][:, 0:n], -1.0, rsb[r][:, 0:n],
                                                                                      ALU.mult, ALU.mult),
                              reads=[('tmp', tm), ('rs', r)], writes=[('tmp', tm)])
                        for k in range(KD):
                            t1 = newtmp()
                            while t1 in (tm, tv):
                                t1 = newtmp()
                            P.add('dve', lambda e, k=k, t1=t1, r=r, c0=c0, n=n: e.scalar_tensor_tensor(
                                tmp[t1][:, 0:n], yT[:, k, c0:c0 + n], pp[:, PC_LNG + k:PC_LNG + k + 1], rsb[r][:, 0:n], ALU.mult, ALU.mult),
                                reads=[('yT', nt), 'pp', ('rs', r)], writes=[('tmp', t1)])
                            P.add('dve', lambda e, k=k, t1=t1, tm=tm, n=n: e.scalar_tensor_tensor(
                                tmp[t1][:, 0:n], tmp[tm][:, 0:n], pp[:, PC_LNG + k:PC_LNG + k + 1], tmp[t1][:, 0:n], ALU.mult, ALU.add),
                                reads=[('tmp', tm), 'pp', ('tmp', t1)], writes=[('tmp', t1)])
                            P.add('act', lambda e, k=k, t1=t1, c0=c0, n=n: e.activation(
                                c2[:, k, c0:c0 + n], tmp[t1][:, 0:n], AF.Silu, bias=pp[:, PC_LNB + k:PC_LNB + k + 1], scale=1.0),
                                reads=[('tmp', t1), 'pp'], writes=[('c2', nt, k)])

                P.alias(R1_KEYS_U, R1_KEYS_M)
                if MIX_STAGE >= 6:
                    for g in range(4):
                        wvX, wkX = w_next()
                        wvY, wkY = w_next(keep_prev=True)
                        wc_ = wvX[:, 0:2048].rearrange("p (k c) -> p k c", k=8)
                        wa_ = wvX[:, 2048:4096].rearrange("p (k c) -> p k c", k=8)
                        wgc_ = wvY[:, 0:2048].rearrange("p (k c) -> p k c", k=8)
                        wga_ = wvY[:, 2048:4096].rearrange("p (k c) -> p k c", k=8)
                        for cc in range(2):
                            c = g * 2 + cc
                            for nt, (kind, t0, n, c0) in enumerate(ntl):
                                bs_ = []
                                for (wmat, wkey, src, skey) in ((wc_, wkX, c2, ('c2', nt)), (wa_, wkX, oT, ('oT', nt)),
                                                                (wgc_, wkY, hT, ('hT', nt)), (wga_, wkY, hT, ('hT', nt))):
                                    b = bank()
                                    bs_.append(b)
                                    for k in range(KD):
                                        P.add('pe', lambda e, b=b, k=k, wmat=wmat, src=src, cc=cc, c0=c0, n=n: e.matmul(
                                            psum[b][:, 0:n], lhsT=wmat[:, k, cc * 128:(cc + 1) * 128], rhs=src[:, k, c0:c0 + n],
                                            start=(k == 0), stop=(k == KD - 1)), reads=[wkey, (skey + (k,)) if skey[0] in ('hT', 'c2') else skey], writes=[PS(b)])
                                bA, bB, bC, bD = bs_
                                t1 = newtmp()
                                t2 = newtmp()
                                P.add('act', lambda e, b=bC, t1=t1, n=n: e.activation(tmp[t1][:, 0:n], psum[b][:, 0:n], AF.Sigmoid),
                                      reads=[PS(bC)], writes=[('tmp', t1)])
                                P.add('act', lambda e, b=bD, t2=t2, n=n: e.activation(tmp[t2][:, 0:n], psum[b][:, 0:n], AF.Sigmoid),
                                      reads=[PS(bD)], writes=[('tmp', t2)])
                                P.add('dve', lambda e, b=bA, t1=t1, n=n: e.tensor_tensor(tmp[t1][:, 0:n], psum[b][:, 0:n], tmp[t1][:, 0:n], ALU.mult),
                                      reads=[PS(bA), ('tmp', t1)], writes=[('tmp', t1)])
                                P.add('dve', lambda e, b=bB, t2=t2, n=n: e.tensor_tensor(tmp[t2][:, 0:n], psum[b][:, 0:n], tmp[t2][:, 0:n], ALU.mult),
                                      reads=[PS(bB), ('tmp', t2)], writes=[('tmp', t2)])
                                P.add('dve', lambda e, t1=t1, t2=t2, c=c, c0=c0, n=n: e.tensor_tensor(mT[:, c, c0:c0 + n], tmp[t1][:, 0:n], tmp[t2][:, 0:n], ALU.add),
                                      reads=[('tmp', t1), ('tmp', t2)], writes=[('mT', nt)])
                if MIX_STAGE >= 7:
                    for c in range(KD):
                        if c % 4 == 0:
                            wv, wk = w_next()
                            wo_ = wv[:, 0:4096].rearrange("p (k c) -> p k c", k=8)
                        cq = c % 4
                        for nt, (kind, t0, n, c0) in enumerate(ntl):
                            b = bank()
                            for k in range(KD):
                                P.add('pe', lambda e, b=b, k=k, wo_=wo_, cq=cq, c0=c0, n=n: e.matmul(
                                    psum[b][:, 0:n], lhsT=wo_[:, k, cq * 128:(cq + 1) * 128], rhs=mT[:, k, c0:c0 + n],
                                    start=(k == 0), stop=(k == KD - 1)), reads=[wk, ('mT', nt)], writes=[PS(b)])
                            P.add('dve', lambda e, b=b, c=c, c0=c0, n=n: e.tensor_copy(yT[:, c, c0:c0 + n], psum[b][:, 0:n]),
                                  reads=[PS(b)], writes=[('yT', nt)])
                            P.add('act', lambda e, c=c, c0=c0, n=n: e.activation(sq3[:, c, c0:c0 + n], yT[:, c, c0:c0 + n], AF.Square),
                                  reads=[('yT', nt)], writes=[('sq', nt, c)])
                    for nt, (kind, t0, n, c0) in enumerate(ntl):
                        post_norm_residual(PC_MPOST, nt, c0, n, False)

            if ti + 1 < n_tiles:
                nxt = tile_ntl(tiles[ti + 1])
                issue_rope(nxt)
                for bi in range(min(4, len(tile_blocks(nxt)))):
                    issue_x_load(nxt, bi)
            if ti == n_tiles - 1:
                passthrough_dmas()
            if 'ffn2' not in skip:
                ffn(ntl, PC_F2PRE, PC_F2POST)

            P.alias(R1_KEYS_HID, R1S_KEYS[0:5])
            for bi, (nt, kind, r0, col) in enumerate(blocks):
                dstd = yp if kind == 'p' else ys
                sbuf_, skey = r1s[bi], ('r1s', bi)
                for hb in range(2):
                    b = bank()
                    for kk in range(4):
                        k = hb * 4 + kk
                        P.add('pe', lambda e, b=b, kk=kk, k=k, col=col: e.transpose(
                            psum[b][:, kk * 128:(kk + 1) * 128], xT[:, k, col:col + 128], ident),
                            reads=[('xT', nt, k), 'cst'], writes=[PS(b)])
                    if hb == 0:
                        P.add('act', lambda e, b=b, sbuf_=sbuf_: e.copy(sbuf_[:, 0:512], psum[b][:, 0:512]),
                              reads=[PS(b)], writes=[skey])
                    else:
                        P.add('dve', lambda e, b=b, sbuf_=sbuf_: e.tensor_copy(sbuf_[:, 512:1024], psum[b][:, 0:512]),
                              reads=[PS(b)], writes=[skey])
                P.add('sp', lambda e, sbuf_=sbuf_, dstd=dstd, r0=r0: e.dma_start(out=dstd[r0:r0 + 128, :], in_=sbuf_),
                      reads=[skey], dma_key=skey)

        P.emit(nc)
    return nc


def _fm(v):
    return np.ascontiguousarray(np.asarray(v, np.float32).reshape(KD, 128).T)


def _constants():
    cst = np.zeros((128, NCC), np.float32)
    cst[:, CC_ID:CC_ID + 128] = np.eye(128, dtype=np.float32)
    for m in range(128):
        d = m % 64
        k = m + 8 if d < 8 else (m - 8 if d < 16 else m)
        cst[k, CC_PERM + m] = 1.0
    i = np.arange(128)[:, None]
    j = np.arange(256)[None, :]
    vis = (j > i) & (j <= i + 128)
    mp = np.where(vis, 0.0, NEG).astype(np.float32)
    cst[:, CC_MP:CC_MP + 256] = mp
    cst[:, CC_MP + 256:CC_MP + 512] = mp
    mp0 = mp.copy()
    mp0[:, 0:128] = NEG
    cst[:, CC_MP0:CC_MP0 + 256] = mp0
    cst[:, CC_MP0 + 256:CC_MP0 + 512] = mp0
    jq = (np.arange(128) % DS)[:, None]
    key = np.arange(136)[None, :]
    vis_s = np.where(key < 128, key > jq, (key - 128) <= jq)
    cst[:, CC_MS:CC_MS + 136] = np.where(vis_s, 0.0, NEG).astype(np.float32)
    inv = np.exp(-math.log(500000.0) * np.arange(0, 16, 2, dtype=np.float32) / np.float32(16)).astype(np.float32)
    pos = np.concatenate([np.arange(SEQ), np.tile(PAST + np.arange(DS), NS)]).astype(np.float32)
    ang = (pos[:, None] * inv[None, :]).astype(np.float32)
    cos = np.cos(ang).astype(np.float32)
    sin = np.sin(ang).astype(np.float32)
    rope = np.zeros((128, 2, SEQ + NS * DS), np.float32)
    rope[:, 0, :] = 1.0
    for p in range(128):
        d = p % 64
        if d < 8:
            rope[p, 0] = cos[:, d]
            rope[p, 1] = -sin[:, d]
        elif d < 16:
            rope[p, 0] = cos[:, d - 8]
            rope[p, 1] = sin[:, d - 8]
    return cst, rope


_CACHE = {}


def kernel(x_prompt, x_sample, state_conv, cache_k_win, cache_v_win,
           ffn1_pre_g, ffn1_w_up, ffn1_w_down, ffn1_post_g,
           mix_pre_g, w_in, conv_dw_w, conv_dw_b, conv_ln_g, conv_ln_b, w_conv_out,
           attn_sinks, w_attn_out, w_out, mix_post_g,
           ffn2_pre_g, ffn2_w_up, ffn2_w_down, ffn2_post_g):
    f = lambda a: np.ascontiguousarray(np.asarray(a, dtype=np.float32))
    n_cores = 8
    if 'nc' not in _CACHE:
        _CACHE['nc'] = build_program()
        _CACHE['cst'] = _constants()
    nc = _CACHE['nc']
    cst, rope = _CACHE['cst']

    win_ = f(w_in[0]).copy()
    qcols = np.concatenate([np.arange(2048 + h * 64, 2048 + (h + 1) * 64) for h in HP])
    win_[:, 2048:3072] = win_[:, qcols]
    wao_ = f(w_attn_out[0])
    arows = np.concatenate([np.arange(h * 64, (h + 1) * 64) for h in HP])
    wao_ = np.ascontiguousarray(wao_[arows, :])
    pp = np.zeros((128, NPP), np.float32)
    for col, v in ((PC_F1PRE, ffn1_pre_g), (PC_F1POST, ffn1_post_g), (PC_MPRE, mix_pre_g), (PC_MPOST, mix_post_g),
                   (PC_F2PRE, ffn2_pre_g), (PC_F2POST, ffn2_post_g), (PC_LNG, conv_ln_g), (PC_LNB, conv_ln_b),
                   (PC_CB, conv_dw_b)):
        pp[:, col:col + 8] = _fm(f(v)[0])
    cw = f(conv_dw_w[0])
    pp[:, PC_CW:PC_CW + 8 * CWID] = cw.reshape(CWID, KD, 128).transpose(2, 1, 0).reshape(128, KD * CWID)
    sinks = f(attn_sinks[0])
    sp = sinks[np.array(HP)]
    pp[:, PC_SINKROW] = np.repeat(sp, DS)
    pp[:, PC_SINKBC:PC_SINKBC + 16] = sp[None, :]
    for cg in range(2):
        for pair in range(2):
            for hf in range(2):
                for uu in range(2):
                    pos = 2 * (cg * 4 + 2 * pair + uu) + hf
                    pp[:, PC_SINKW + 8 * cg + 4 * pair + 2 * hf + uu] = sp[pos]
    pp[:, PC_EPS] = EPS
    pp[:, PC_EPS4] = 4 * EPS

    wsrc = dict(f1u=f(ffn1_w_up[0]), f1d=f(ffn1_w_down[0]), win=win_, wco=f(w_conv_out[0]), wao=wao_, wo=f(w_out[0]),
                f2u=f(ffn2_w_up[0]), f2d=f(ffn2_w_down[0]))
    units = weight_plan()
    wall = np.zeros((len(units), 128, 4096), np.float32)
    for u, parts in enumerate(units):
        for (a, kk, cc, nm, r0, c0) in parts:
            blk = wsrc[nm][r0:r0 + kk * 128, c0:c0 + cc].reshape(kk, 128, cc).transpose(1, 0, 2).reshape(128, kk * cc)
            wall[u, :, a:a + kk * cc] = blk
    shared = dict(wall=wall, pp=pp, cst=cst, rope=rope)
    xpa, xsa = f(x_prompt), f(x_sample)
    sca, cka, cva = f(state_conv[0]), f(cache_k_win[0]), f(cache_v_win[0])
    in_maps = []
    for i in range(n_cores):
        m = dict(shared)
        m['xp'] = xpa[i]
        m['xs'] = xsa[NS * i:NS * (i + 1)].reshape(NS * DS, D)
        m['stc'] = sca[NS * i:NS * (i + 1)].reshape(NS * HALO, D)
        m['ck'] = cka[NS * i:NS * (i + 1)].reshape(NS, 128, 256)
        m['cv'] = cva[NS * i:NS * (i + 1)].reshape(NS, 128, 256)
        in_maps.append(m)
    if _CACHE.get('dry'):
        return nc, in_maps
    res = run_bass_kernel_spmd(nc, in_maps, core_ids=list(range(n_cores)))
    R = res.results
    y_prompt = np.stack([R[i]['yp'] for i in range(n_cores)]).astype(np.float32)
    y_sample = np.concatenate([R[i]['ys'].reshape(NS, DS, D) for i in range(n_cores)]).astype(np.float32)
    conv_p = np.stack([R[i]['csp'] for i in range(n_cores)])[None].astype(np.float32)
    k_p = np.stack([R[i]['kwp'].reshape(128, 4, 64) for i in range(n_cores)])[None].astype(np.float32)
    v_p = np.stack([R[i]['vwp'].reshape(128, 4, 64) for i in range(n_cores)])[None].astype(np.float32)
    conv_s = np.concatenate([R[i]['css'] for i in range(n_cores)])[None].astype(np.float32)
    k_s = np.concatenate([R[i]['kws'].reshape(NS, 128, 4, 64) for i in range(n_cores)])[None].astype(np.float32)
    v_s = np.concatenate([R[i]['vws'].reshape(NS, 128, 4, 64) for i in range(n_cores)])[None].astype(np.float32)
    return (y_prompt, y_sample, conv_p, k_p, v_p, conv_s, k_s, v_s)
```
